# Optimizing a Trainium2 kernel written in Bass

```python
import math
import jax, jax.numpy as jnp
from jax import lax
import numpy as np


D_MODEL = 1024
BATCH = 8
SEQ = 2048
DEPTH = 2
DEC_BATCH = 128
DEC_SEQ = 1
PAST_LEN = 16384
PAGE_SIZE = 128

D_PLE = 256
D_A = D_MODEL // 2
D_B = D_MODEL // 2
D_C = D_MODEL // 2
CONV_A_WIDTH = 31
CHUNK = 128
N_GROUPS_B = 4
GW_B = D_B // N_GROUPS_B
SSM_GROUP = 16
N_GROUPS_C = D_C // SSM_GROUP
SSM_STATE = 64
N_BRANCH = 3
D_FF = 2816
FFN_CONV_WIDTH = 3
IN_COLS = 2 * D_A + 2 * D_B + D_C + N_BRANCH * D_MODEL
EPS = 1e-6

kernel_name = 'hybrid_conv_gmlp_s5_decoder'


def rmsnorm(x, g):
    xf = x.astype(jnp.float32)
    y = xf * lax.rsqrt(jnp.mean(xf * xf, axis=-1, keepdims=True) + EPS)
    return (y * g.astype(jnp.float32)).astype(x.dtype)


def layernorm(x, g, b):
    xf = x.astype(jnp.float32)
    mu = jnp.mean(xf, axis=-1, keepdims=True)
    xc = xf - mu
    var = jnp.mean(xc * xc, axis=-1, keepdims=True)
    y = xc * lax.rsqrt(var + EPS) * g.astype(jnp.float32) + b.astype(jnp.float32)
    return y.astype(x.dtype)


def causal_dwconv(x, buf, w, b):
    xp = jnp.concatenate([buf.astype(x.dtype), x], axis=1)
    y = lax.conv_general_dilated(xp, w[:, None, :].astype(x.dtype), (1,), 'VALID',
                                 dimension_numbers=('NWC', 'WIO', 'NWC'),
                                 feature_group_count=x.shape[-1])
    k1 = w.shape[0] - 1
    return y + b.astype(x.dtype), xp[:, xp.shape[1] - k1:]


def chunk_spatial_mix(v, w_s, b_s):
    bt, t, c = v.shape
    nc = -(-t // CHUNK)
    vp = jnp.pad(v, ((0, 0), (0, nc * CHUNK - t), (0, 0)))
    vp = vp.reshape(bt, nc, CHUNK, N_GROUPS_B, GW_B)
    mask = jnp.tril(jnp.ones((CHUNK, CHUNK), dtype=bool))
    w = jnp.where(mask, w_s, 0).astype(v.dtype)
    out = jnp.einsum('gij,bcjgd->bcigd', w, vp) + b_s.T[:, :, None].astype(v.dtype)
    return out.reshape(bt, nc * CHUNK, c)[:, :t]


def s5_scan(u, s0_re, s0_im, a_re, a_im, log_dt, b_re, b_im, c_re, c_im, d):
    f32 = jnp.float32
    bt, t, _ = u.shape
    uf = u.astype(f32).reshape(bt, t, N_GROUPS_C, SSM_GROUP)
    A = lax.complex(a_re.astype(f32), a_im.astype(f32))
    dt = jnp.exp(log_dt.astype(f32))[:, None]
    a_bar = jnp.exp(A * dt)
    B = lax.complex(b_re.astype(f32), b_im.astype(f32))
    b_bar = ((a_bar - 1.0) / A)[:, :, None] * B
    C = lax.complex(c_re.astype(f32), c_im.astype(f32))
    bu = jnp.einsum('gnh,btgh->btgn', b_bar, uf.astype(jnp.complex64))
    s0 = lax.complex(s0_re.astype(f32), s0_im.astype(f32))
    bu = bu.at[:, 0].add(a_bar * s0)
    a_seq = jnp.broadcast_to(a_bar, bu.shape)

    def combine(l, r):
        return (r[0] * l[0], r[0] * l[1] + r[1])

    _, s = lax.associative_scan(combine, (a_seq, bu), axis=1)
    y = jnp.real(jnp.einsum('ghn,btgn->btgh', C, s)) + d.astype(f32).reshape(N_GROUPS_C, SSM_GROUP) * uf
    s_last = s[:, -1]
    return y.reshape(bt, t, D_C).astype(u.dtype), jnp.real(s_last), jnp.imag(s_last)


def mixer_layer(h, buf_a, s_re, s_im, prm, i):
    z = h @ prm['w_in'][i]
    o1 = 2 * D_A
    o2 = o1 + 2 * D_B
    o3 = o2 + D_C
    za, zb, zc, zg = z[..., :o1], z[..., o1:o2], z[..., o2:o3], z[..., o3:]
    a = za[..., :D_A] * jax.nn.sigmoid(za[..., D_A:])
    a_conv, new_buf_a = causal_dwconv(a, buf_a, prm['conv_a_w'][i], prm['conv_a_b'][i])
    a_out = jax.nn.silu(layernorm(a_conv, prm['ln_a_g'][i], prm['ln_a_b'][i])) @ prm['w_a_out'][i]
    u, v = zb[..., :D_B], zb[..., D_B:]
    v = layernorm(v, prm['ln_b_g'][i], prm['ln_b_b'][i])
    vm = chunk_spatial_mix(v, prm['w_s'][i], prm['b_s'][i])
    b_out = (u * vm) @ prm['w_b_out'][i]
    t = v.shape[1]
    v_state = v[:, ((t - 1) // CHUNK) * CHUNK:]
    yc, new_re, new_im = s5_scan(zc, s_re, s_im, prm['ssm_a_re'][i], prm['ssm_a_im'][i],
                                 prm['ssm_log_dt'][i], prm['ssm_b_re'][i], prm['ssm_b_im'][i],
                                 prm['ssm_c_re'][i], prm['ssm_c_im'][i], prm['ssm_d'][i])
    gc = jax.nn.gelu(yc) @ prm['w_c_glu'][i]
    c_out = gc[..., :D_MODEL] * jax.nn.sigmoid(gc[..., D_MODEL:])
    g = jax.nn.sigmoid(zg)
    m = (g[..., :D_MODEL] * a_out + g[..., D_MODEL:2 * D_MODEL] * b_out
         + g[..., 2 * D_MODEL:] * c_out)
    return m @ prm['w_out'][i], new_buf_a, v_state, new_re, new_im


def conv_ffn(h, buf, w_up, cw, cb, w_down):
    up, nb = causal_dwconv(h @ w_up, buf, cw, cb)
    gate, val = up[..., :D_FF], up[..., D_FF:]
    return (jax.nn.gelu(gate) * val) @ w_down, nb


def trunk(x, p, buf_a, s_re, s_im, buf_f, prm):
    la, lv, lre, lim, lf = [], [], [], [], []
    for i in range(DEPTH):
        h = rmsnorm(x, prm['g_mix'][i])
        m, na, nv, nre, nim = mixer_layer(h, buf_a[i], s_re[i], s_im[i], prm, i)
        x = x + m
        f, nf = conv_ffn(rmsnorm(x, prm['g_ffn'][i]), buf_f[i], prm['w_up'][i],
                         prm['conv_f_w'][i], prm['conv_f_b'][i], prm['w_down'][i])
        x = x + f
        gate = jax.nn.sigmoid(rmsnorm(x, prm['g_ple'][i]) @ prm['w_pg'][i])
        x = x + gate * (p[i].astype(x.dtype) @ prm['w_pe'][i])
        la.append(na); lv.append(nv); lre.append(nre); lim.append(nim); lf.append(nf)
    y = rmsnorm(x, prm['g_final'])
    return (y, jnp.stack(la), jnp.stack(lv), jnp.stack(lre), jnp.stack(lim), jnp.stack(lf))


def setup_inputs(seed: int = 0) -> dict:
    key = jax.random.key(seed)
    ks = iter(jax.random.split(key, 64))
    f32 = jnp.float32

    def nrm(shape, scale=1.0):
        return jax.random.normal(next(ks), shape, f32) * scale

    def gain(shape):
        return 1.0 + nrm(shape, 0.01)

    L = DEPTH
    a_im_base = jnp.pi * jnp.arange(SSM_STATE, dtype=f32)
    return {
        'x_prompt': nrm((BATCH, SEQ, D_MODEL)),
        'x_sample': nrm((DEC_BATCH, DEC_SEQ, D_MODEL)),
        'state_conv_a': nrm((L, DEC_BATCH, CONV_A_WIDTH - 1, D_A), 0.5),
        'state_ssm_re': nrm((L, DEC_BATCH, N_GROUPS_C, SSM_STATE)),
        'state_ssm_im': nrm((L, DEC_BATCH, N_GROUPS_C, SSM_STATE)),
        'state_conv_ffn': nrm((L, DEC_BATCH, FFN_CONV_WIDTH - 1, 2 * D_FF), 0.5),
        'p_prompt': nrm((L, BATCH, SEQ, D_PLE)),
        'p_sample': nrm((L, DEC_BATCH, DEC_SEQ, D_PLE)),
        'g_mix': gain((L, D_MODEL)),
        'w_in': nrm((L, D_MODEL, IN_COLS), D_MODEL ** -0.5),
        'conv_a_w': nrm((L, CONV_A_WIDTH, D_A), CONV_A_WIDTH ** -0.5),
        'conv_a_b': nrm((L, D_A), 0.01),
        'ln_a_g': gain((L, D_A)),
        'ln_a_b': nrm((L, D_A), 0.01),
        'w_a_out': nrm((L, D_A, D_MODEL), D_A ** -0.5),
        'ln_b_g': gain((L, D_B)),
        'ln_b_b': nrm((L, D_B), 0.01),
        'w_s': nrm((L, N_GROUPS_B, CHUNK, CHUNK), CHUNK ** -0.5),
        'b_s': gain((L, N_GROUPS_B, CHUNK)),
        'w_b_out': nrm((L, D_B, D_MODEL), D_B ** -0.5),
        'ssm_a_re': -0.5 + nrm((L, N_GROUPS_C, SSM_STATE), 0.01),
        'ssm_a_im': a_im_base + nrm((L, N_GROUPS_C, SSM_STATE), 0.01),
        'ssm_log_dt': jax.random.uniform(next(ks), (L, N_GROUPS_C), f32,
                                         math.log(1e-3), math.log(1e-1)),
        'ssm_b_re': nrm((L, N_GROUPS_C, SSM_STATE, SSM_GROUP), (2.0 * SSM_GROUP) ** -0.5),
        'ssm_b_im': nrm((L, N_GROUPS_C, SSM_STATE, SSM_GROUP), (2.0 * SSM_GROUP) ** -0.5),
        'ssm_c_re': nrm((L, N_GROUPS_C, SSM_GROUP, SSM_STATE), (2.0 * SSM_STATE) ** -0.5),
        'ssm_c_im': nrm((L, N_GROUPS_C, SSM_GROUP, SSM_STATE), (2.0 * SSM_STATE) ** -0.5),
        'ssm_d': nrm((L, D_C)),
        'w_c_glu': nrm((L, D_C, 2 * D_MODEL), D_C ** -0.5),
        'w_out': nrm((L, D_MODEL, D_MODEL), D_MODEL ** -0.5),
        'g_ffn': gain((L, D_MODEL)),
        'w_up': nrm((L, D_MODEL, 2 * D_FF), D_MODEL ** -0.5),
        'conv_f_w': nrm((L, FFN_CONV_WIDTH, 2 * D_FF), FFN_CONV_WIDTH ** -0.5),
        'conv_f_b': nrm((L, 2 * D_FF), 0.01),
        'w_down': nrm((L, D_FF, D_MODEL), D_FF ** -0.5),
        'g_ple': gain((L, D_MODEL)),
        'w_pg': nrm((L, D_MODEL, D_MODEL), D_MODEL ** -0.5),
        'w_pe': nrm((L, D_PLE, D_MODEL), D_PLE ** -0.5),
        'g_final': gain((D_MODEL,)),
    }


def reference(x_prompt, x_sample, state_conv_a, state_ssm_re, state_ssm_im, state_conv_ffn,
              p_prompt, p_sample, g_mix, w_in, conv_a_w, conv_a_b, ln_a_g, ln_a_b, w_a_out,
              ln_b_g, ln_b_b, w_s, b_s, w_b_out, ssm_a_re, ssm_a_im, ssm_log_dt, ssm_b_re,
              ssm_b_im, ssm_c_re, ssm_c_im, ssm_d, w_c_glu, w_out, g_ffn, w_up, conv_f_w,
              conv_f_b, w_down, g_ple, w_pg, w_pe, g_final):
    prm = dict(g_mix=g_mix, w_in=w_in, conv_a_w=conv_a_w, conv_a_b=conv_a_b, ln_a_g=ln_a_g,
               ln_a_b=ln_a_b, w_a_out=w_a_out, ln_b_g=ln_b_g, ln_b_b=ln_b_b, w_s=w_s, b_s=b_s,
               w_b_out=w_b_out, ssm_a_re=ssm_a_re, ssm_a_im=ssm_a_im, ssm_log_dt=ssm_log_dt,
               ssm_b_re=ssm_b_re, ssm_b_im=ssm_b_im, ssm_c_re=ssm_c_re, ssm_c_im=ssm_c_im,
               ssm_d=ssm_d, w_c_glu=w_c_glu, w_out=w_out, g_ffn=g_ffn, w_up=w_up,
               conv_f_w=conv_f_w, conv_f_b=conv_f_b, w_down=w_down, g_ple=g_ple, w_pg=w_pg,
               w_pe=w_pe, g_final=g_final)
    bp = x_prompt.shape[0]
    dt_p = x_prompt.dtype
    zero_a = jnp.zeros((DEPTH, bp, CONV_A_WIDTH - 1, D_A), dt_p)
    zero_re = jnp.zeros((DEPTH, bp, N_GROUPS_C, SSM_STATE), jnp.float32)
    zero_f = jnp.zeros((DEPTH, bp, FFN_CONV_WIDTH - 1, 2 * D_FF), dt_p)
    y_prompt, conv_a_p, chunk_v_p, ssm_re_p, ssm_im_p, conv_ffn_p = trunk(
        x_prompt, p_prompt, zero_a, zero_re, zero_re, zero_f, prm)
    y_sample, conv_a_s, chunk_v_s, ssm_re_s, ssm_im_s, conv_ffn_s = trunk(
        x_sample, p_sample, state_conv_a, state_ssm_re, state_ssm_im, state_conv_ffn, prm)
    return (y_prompt, y_sample, conv_a_p, conv_a_s, chunk_v_p, chunk_v_s,
            ssm_re_p, ssm_im_p, ssm_re_s, ssm_im_s, conv_ffn_p, conv_ffn_s)
```

```python
import numpy as np
from contextlib import ExitStack
import concourse.bass as bass
import concourse.mybir as mybir
from concourse.bass_utils import run_bass_kernel_spmd

F32 = mybir.dt.float32
F32R = mybir.dt.float32r
BF = mybir.dt.bfloat16
ALU = mybir.AluOpType
AF = mybir.ActivationFunctionType

D = 1024
SEQ = 2048
NS = 16
L = 2
TT = 512
DA = 512
DFF = 2816
NJ = 22
EPS = 1e-6
O1 = 1024
O2 = 2048
O3 = 2560
SLOT = 2048
NSLOT = 3
NSTEP = 9
S5_POOL = False
S5_USE_DVE = False
DBG = None


class Res:
    __slots__ = ("lastw", "readers")

    def __init__(self):
        self.lastw = None
        self.readers = []


class Tile:
    def __init__(self, ap, psum=False):
        self.ap = ap
        self.res = [Res()]
        self.dsem = None
        self.psum = psum

    def __getitem__(self, k):
        return self.ap[k]


class Eng:
    def __init__(self, name, eng, sem, inorder=False):
        self.name = name
        self.eng = eng
        self.sem = sem
        self.count = 0
        self.waited = {}
        self.inorder = inorder


class Sched:
    def __init__(self, nc, es):
        self.nc = nc
        self.es = es
        self.nsem = 0
        self.dma_toks = {}
        self.all_toks = {}
        self.dpool = []
        self.dpi = 0
        self.pe = Eng("pe", nc.tensor, self.newsem("s_pe"), inorder=True)
        self.act = Eng("act", nc.scalar, self.newsem("s_act"))
        self.dve = Eng("dve", nc.vector, self.newsem("s_dve"))
        self.pool = Eng("pool", nc.gpsimd, self.newsem("s_pool"))
        self.sp = Eng("sp", nc.sync, None)
        self.compute = [self.pe, self.act, self.dve, self.pool]

    def newsem(self, name):
        self.nsem += 1
        return self.es.enter_context(self.nc.semaphore(name))

    def _need(self, E, tok, needs):
        if tok is None:
            return
        sem, val, src = tok
        if src is E and E.inorder:
            return
        key = id(sem)
        if E.waited.get(key, 0) >= val:
            return
        if needs.get(key, (None, 0))[1] < val:
            needs[key] = (sem, val)

    def _deps(self, E, reads, writes):
        needs = {}
        for t in reads:
            for r in t.res:
                self._need(E, r.lastw, needs)
                if t.psum:
                    for rd in r.readers:
                        if rd[2] is not E:
                            self._need(E, rd, needs)
        for t in writes:
            for r in t.res:
                self._need(E, r.lastw, needs)
                for rd in r.readers:
                    self._need(E, rd, needs)
        for key, (sem, val) in needs.items():
            E.eng.wait_ge(sem, val)
            E.waited[key] = val

    def _mark(self, tok, reads, writes):
        for t in writes:
            for r in t.res:
                r.lastw = tok
                r.readers = []
        for t in reads:
            for r in t.res:
                r.readers.append(tok)

    def op(self, E, fn, reads=(), writes=()):
        self._deps(E, reads, writes)
        ins = fn()
        E.count += 1
        ins.then_inc(E.sem, 1)
        self._mark((E.sem, E.count, E), reads, writes)

    def dma(self, out_ap, in_ap, reads=(), writes=(), st=None, E=None, track=True, group=False):
        E = E or self.sp
        if group and st is not None and st.dsem is not None:
            saved = []
            for t in writes:
                for r in t.res:
                    if r.lastw is not None and r.lastw[0] is st.dsem[0]:
                        saved.append((r, r.lastw))
                        r.lastw = None
            self._deps(E, reads, writes)
            for r, lw in saved:
                r.lastw = lw
        else:
            self._deps(E, reads, writes)
        if st is None:
            if not self.dpool:
                self.dpool = [Tile(None) for _ in range(8)]
                for t in self.dpool:
                    t.dsem = [self.newsem("dp%d" % self.nsem), 0]
            st = self.dpool[self.dpi % len(self.dpool)]
            self.dpi += 1
            if st.dsem[1] > 0 and E.waited.get(id(st.dsem[0]), 0) < st.dsem[1]:
                E.eng.wait_ge(st.dsem[0], st.dsem[1])
                E.waited[id(st.dsem[0])] = st.dsem[1]
        if st.dsem is None:
            st.dsem = [self.newsem("d%d" % self.nsem), 0]
        ins = E.eng.dma_start(out=out_ap, in_=in_ap)
        st.dsem[1] += 16
        ins.then_inc(st.dsem[0], 16)
        tok = (st.dsem[0], st.dsem[1], None)
        self.all_toks[id(st.dsem[0])] = tok
        if track:
            self.dma_toks[id(st.dsem[0])] = tok
        self._mark(tok, reads, writes)
        return tok

    def drain_pool(self, E):
        for t in self.dpool:
            sem, val = t.dsem
            if val > 0 and E.waited.get(id(sem), 0) < val:
                E.eng.wait_ge(sem, val)
                E.waited[id(sem)] = val

    def phase_barrier(self, tiles):
        toks = [(E.sem, E.count, None) for E in self.compute if E.count > 0]
        toks += list(self.dma_toks.values())
        for t in tiles:
            for r in t.res:
                r.lastw = None
                r.readers = list(toks)

    def finish(self):
        for E in self.compute:
            if E.count > 0 and self.sp.waited.get(id(E.sem), 0) < E.count:
                self.sp.eng.wait_ge(E.sem, E.count)
        for tok in self.all_toks.values():
            sem, val, _ = tok
            if self.sp.waited.get(id(sem), 0) < val:
                self.sp.eng.wait_ge(sem, val)
                self.sp.waited[id(sem)] = val


class Pool_:
    def __init__(self, tiles):
        self.free = list(tiles)

    def get(self):
        assert self.free, "pool exhausted"
        return self.free.pop(0)

    def put(self, *ts):
        for t in ts:
            self.free.append(t)


class Ring:
    def __init__(self, tiles):
        self.tiles = tiles
        self.i = 0

    def get(self):
        t = self.tiles[self.i % len(self.tiles)]
        self.i += 1
        return t


def layer_chunks(l):
    ch = []
    for c in range(4):
        ch.append([("w_in", l, 0, 8, 128 * c, 128), ("w_in", l, 0, 8, 512 + 128 * c, 128)])
    for c2 in range(2):
        ch.append([("w_in", l, 0, 8, O1 + 256 * c2, 256)])
    for kh in range(2):
        ch.append([("w_in", l, 512 * kh, 4, O1 + 512, 512)])
    for c2 in range(2):
        ch.append([("w_in", l, 0, 8, O2 + 256 * c2, 256)])
    for f in range(8):
        ch.append([("w_in", l, 0, 8, O3 + 128 * f, 128), ("w_in", l, 0, 8, O3 + 1024 + 128 * f, 128)])
        ch.append([("w_a_out", l, 0, 4, 128 * f, 128), ("w_b_out", l, 0, 4, 128 * f, 128),
                   ("w_in", l, 0, 8, O3 + 2048 + 128 * f, 128)])
    for f in range(8):
        ch.append([("w_c_glu", l, 0, 4, 128 * f, 128), ("w_c_glu", l, 0, 4, 1024 + 128 * f, 128)])
    for c4 in range(4):
        ch.append([("w_out", l, 0, 8, 256 * c4, 256)])
    for j in range(NJ):
        ch.append([("w_up", l, 0, 8, 128 * j, 128), ("w_up", l, 0, 8, DFF + 128 * j, 128)])
    for fh in range(2):
        for j0 in range(0, NJ, 4):
            nj = min(4, NJ - j0)
            ch.append([("w_down", l, 128 * j0, nj, 512 * fh, 512)])
    ch.append([("w_pe", l, 0, 2, 0, 1024)])
    for c4 in range(4):
        ch.append([("w_pg", l, 0, 8, 256 * c4, 256)])
    return ch


def build_nc():
    nc = bass.Bass("TRN2", target_bir_lowering=False)
    nc.dge_precook = False
    es = ExitStack()
    with es:
        _build(nc, es)
    return nc


def _build(nc, es):
    def din(name, shape, dt=F32):
        return nc.dram_tensor(name, list(shape), dt, kind="ExternalInput").ap()

    def dout(name, shape):
        return nc.dram_tensor(name, list(shape), F32, kind="ExternalOutput").ap()

    xp = din("xp", [SEQ, D]); xs = din("xs", [NS, D])
    pp = din("pp", [L, SEQ, 256]); psm = din("psm", [L, NS, 256])
    st_ca = din("st_ca", [L, NS, 30, DA]); st_re = din("st_re", [L, NS, 2048]); st_im = din("st_im", [L, NS, 2048])
    st_cf = din("st_cf", [L, NS, 2, 2 * DFF])
    NCH = len(layer_chunks(0))
    wpack = din("wpack", [L, NCH, 128, SLOT])
    g_mix = din("g_mix", [L, D]); g_ffn = din("g_ffn", [L, D]); g_ple = din("g_ple", [L, D]); g_final = din("g_final", [D])
    conv_a_w = din("conv_a_w", [L, 31, DA]); conv_a_b = din("conv_a_b", [L, DA])
    ln_a_g = din("ln_a_g", [L, DA]); ln_a_b = din("ln_a_b", [L, DA])
    ln_b_g = din("ln_b_g", [L, DA]); ln_b_b = din("ln_b_b", [L, DA])
    w_s = din("w_s", [L, 4, 128, 128]); b_s = din("b_s", [L, 4, 128]); b_s_r = din("b_s_r", [L, 4, 128])
    ssm_a_re = din("ssm_a_re", [L, 32, 64]); ssm_a_im = din("ssm_a_im", [L, 32, 64]); ssm_log_dt = din("ssm_log_dt", [L, 32])
    ssm_b_re = din("ssm_b_re", [L, 32, 64, 16]); ssm_b_im = din("ssm_b_im", [L, 32, 64, 16])
    ssm_c_re = din("ssm_c_re", [L, 32, 16, 64]); ssm_c_im = din("ssm_c_im", [L, 32, 16, 64])
    ssm_d = din("ssm_d", [L, DA])
    conv_f_w = din("conv_f_w", [L, 3, 2 * DFF]); conv_f_b = din("conv_f_b", [L, 1, 2 * DFF])
    cst = din("cst", [128, 256])
    cst_r = din("cst_r", [128, 384])

    y_p = dout("y_p", [SEQ, D]); y_s = dout("y_s", [NS, D])
    o_ca_p = dout("o_ca_p", [L, 30, DA]); o_ca_s = dout("o_ca_s", [L, NS, 30, DA])
    o_cv_p = dout("o_cv_p", [L, 128, DA]); o_cv_s = dout("o_cv_s", [L, NS, DA])
    o_re_p = dout("o_re_p", [L, 16, 128]); o_im_p = dout("o_im_p", [L, 16, 128])
    o_re_s = dout("o_re_s", [L, NS, 16, 128]); o_im_s = dout("o_im_s", [L, NS, 16, 128])
    o_cf_p = dout("o_cf_p", [L, 2, 2 * DFF]); o_cf_s = dout("o_cf_s", [L, NS, 2, 2 * DFF])

    S = Sched(nc, es)
    pe, act, dve, pool = S.pe, S.act, S.dve, S.pool
    V = nc.vector; A = nc.scalar; G = nc.gpsimd; PE = nc.tensor

    def sbt(name, shape, dt=F32):
        return es.enter_context(nc.sbuf_tensor(name, list(shape), dt)).ap()

    T = Tile

    cstt = T(sbt("cstt", [128, 256]))
    ident = cstt.ap[:, 0:128]
    tril = cstt.ap[:, 128:256]
    cstr = T(sbt("cstr", [128, 384], BF))
    ones_d = cstr.ap[:, 0:128]
    ones_c = cstr.ap[:, 128:256]
    ones_row = cstr.ap[0:1, 256:384]
    xbig = sbt("xT", [128, 8, TT]); x = [T(xbig[:, f, :]) for f in range(8)]
    hbig = sbt("hT", [128, 8, TT], BF); h = [T(hbig[:, f, :]) for f in range(8)]
    wstg_ap = sbt("wstg", [128, NSLOT, SLOT])
    wstg = [T(wstg_ap[:, i, :]) for i in range(NSLOT)]
    wring_ap = sbt("wring", [128, NSLOT, SLOT], BF)
    wslots = [T(wring_ap[:, i, :]) for i in range(NSLOT)]
    psb = [Tile(es.enter_context(nc.psum_tensor("ps%d" % i, [128, 512], F32)).ap(), psum=True) for i in range(8)]
    P = Pool_(psb)
    NSCR = 8
    scr_ap = sbt("scr", [128, NSCR, TT])
    scr = Ring([T(scr_ap[:, i, :]) for i in range(NSCR)])
    scr_r_ap = sbt("scrr", [128, 3, TT], BF)
    scr_r = Ring([T(scr_r_ap[:, i, :]) for i in range(3)])
    sm_ap = sbt("small", [128, 16, 8])
    small = Ring([T(sm_ap[:, i, :]) for i in range(16)])

    par = []
    for l in range(L):
        p = {}
        p["gains"] = T(sbt("gains%d" % l, [128, 3, 8]))
        p["caw"] = T(sbt("caw%d" % l, [128, 4, 31]))
        p["avec"] = T(sbt("avec%d" % l, [128, 4, 4]))
        p["lnb"] = T(sbt("lnb%d" % l, [128, 2, DA]))
        p["WsT"] = T(sbt("WsT%d" % l, [128, 4, 128], BF))
        p["bs1"] = T(sbt("bs1%d" % l, [1, 4, 128], BF))
        p["bs0"] = T(sbt("bs0%d" % l, [128, 4]))
        p["w00"] = T(sbt("w00%d" % l, [128, 4]))
        p["Wsm"] = T(sbt("Wsm%d" % l, [16, 4, 16], BF))
        p["BbRe"] = T(sbt("BbRe%d" % l, [128, 4, 128], BF))
        p["BbIm"] = T(sbt("BbIm%d" % l, [128, 4, 128], BF))
        p["CTre"] = T(sbt("CTre%d" % l, [128, 4, 128]))
        p["CTim"] = T(sbt("CTim%d" % l, [128, 4, 128]))
        p["HSC"] = T(sbt("HSC%d" % l, [128, 1, 3, 16]))
        p["UPH"] = T(sbt("UPH%d" % l, [128, 3, 16]))
        p["RHO"] = T(sbt("RHO%d" % l, [128, 16]))
        p["cf"] = T(sbt("cf%d" % l, [128, 44, 4]))
        p["carA"] = T(sbt("carA%d" % l, [128, 4, 30], BF))
        p["carF"] = T(sbt("carF%d" % l, [128, 44, 2]))
        p["carS"] = T(sbt("carS%d" % l, [128, 2, 16]))
        par.append(p)
    gfin = T(sbt("gfin", [128, 8]))
    dg_ap = sbt("dg", [128, 2, 31, 128], BF)
    dg = [T(dg_ap[:, i, :, :]) for i in range(2)]
    alast = T(sbt("alast", [128, 4, 32]))
    sCt_ap = sbt("sCt", [128, 8, TT], BF)
    sCt = [T(sCt_ap[:, f, :]) for f in range(8)]
    tabring_ap = sbt("tabring", [128, 2, 1024])
    tabring = [T(tabring_ap[:, i, :]) for i in range(2)]
    tab_d = nc.dram_tensor("tab_d", [L, 16, 128, 1024], F32, kind="Internal").ap()
    Cpad_re_ap = sbt("Cpad_re", [128, 2, 4, 128], BF); Cpad_re = [T(Cpad_re_ap[:, i, :, :]) for i in range(2)]
    Cpad_im_ap = sbt("Cpad_im", [128, 2, 4, 128], BF); Cpad_im = [T(Cpad_im_ap[:, i, :, :]) for i in range(2)]

    ARR = 16512
    ARF = 6272
    arenaR = sbt("arenaR", [128, ARR], BF)
    arenaF = sbt("arenaF", [128, ARF])
    offR = [0]; offF = [0]

    def cR(n):
        a = arenaR[:, offR[0]:offR[0] + n]; offR[0] += n
        assert offR[0] <= ARR, offR[0]
        return a

    def cF(n):
        a = arenaF[:, offF[0]:offF[0] + n]; offF[0] += n
        assert offF[0] <= ARF, offF[0]
        return a

    m = [T(cR(TT)) for f in range(8)]
    acs = [T(cR(TT)) for c in range(4)]
    ub = [T(cR(TT)) for c in range(4)]
    vtok = [T(cR(DA)) for r in range(4)]
    zc = [T(cR(TT)) for c in range(4)]
    hsF2 = [[T(cR(TT)) for _ in range(2)] for _ in range(2)]
    hsF = hsF2[0]
    aext = [T(cR(544)) for c in range(4)]
    hsAB = [T(cF(TT)) for _ in range(4)]
    hsCD = [T(cF(TT)) for _ in range(4)]
    hsA = hsAB[0:2]; hsB = hsAB[2:4]
    xstage = [T(cF(D)) for _ in range(2)]
    mixer_tiles = m + acs + ub + vtok + zc + hsF2[0] + hsF2[1] + aext + hsAB + hsCD + xstage
    offR[0] = 0; offF[0] = 0
    actt = [T(cR(TT)) for j in range(NJ)]
    pT = [T(cR(TT)) for _ in range(2)]
    eg = [T(cF(520)) for _ in range(2)]
    ev = [T(cF(520)) for _ in range(2)]
    pesb_off = offF[0]
    pesb = [T(cF(TT)) for f in range(8)]
    ffn_tiles = actt + pT + eg + ev + pesb
    offF[0] = 0
    yT = [T(cF(TT)) for f in range(8)]
    ystage = [T(cF(D)) for _ in range(2)]
    fin_tiles = yT + ystage
    offF[0] = 0
    prep_ws = T(cF(512)); prep_x = [T(cF(512)) for _ in range(2)]
    prep_cst = [T(cF(512)) for _ in range(2)]
    pv = {}
    for nm in ["are", "aim", "ldt", "dt", "zr", "th", "p", "er", "c", "s", "t1", "t2", "abr", "abi", "pp", "den", "cfr", "cfi", "ncfi"]:
        pv[nm] = T(cF(16))
    pB = [T(cF(256)) for _ in range(2)]
    pBb = [T(cF(256)) for _ in range(2)]
    prep_stg = T(cF(1024))
    uph = T(cF(NSTEP * 3 * 16))
    prep_tiles = [prep_ws] + prep_x + prep_cst + list(pv.values()) + pB + pBb + [prep_stg, uph]

    def mm(ps_ap, pairs, reads, writes, tp=None):
        def fn():
            n = len(pairs)
            ins = None
            for i, (lt, rh) in enumerate(pairs):
                kw = {}
                if tp is not None:
                    kw["tile_position"] = tp
                ins = PE.matmul(ps_ap, lhsT=lt, rhs=rh, start=(i == 0), stop=(i == n - 1), **kw)
            return ins
        S.op(pe, fn, reads=reads, writes=writes)

    cp_flip = [0]

    def evac(out_ap, in_ap, reads, writes, eng=None):
        if eng is None:
            cp_flip[0] ^= 1
            eng = act if cp_flip[0] else dve
        if eng is act:
            S.op(act, lambda: A.copy(out=out_ap, in_=in_ap), reads=reads, writes=writes)
        elif eng is dve:
            S.op(dve, lambda: V.tensor_copy(out=out_ap, in_=in_ap), reads=reads, writes=writes)
        else:
            S.op(pool, lambda: G.tensor_copy(out=out_ap, in_=in_ap), reads=reads, writes=writes)

    def tt_op(E, out_ap, a, b, op, reads, writes):
        e = V if E is dve else G
        S.op(E, lambda: e.tensor_tensor(out=out_ap, in0=a, in1=b, op=op), reads=reads, writes=writes)

    def ts_op(E, out_ap, a, s1, s2, op0, op1, reads, writes):
        e = V if E is dve else G
        if op1 is None:
            S.op(E, lambda: e.tensor_scalar(out=out_ap, in0=a, scalar1=s1, scalar2=None, op0=op0), reads=reads, writes=writes)
        else:
            S.op(E, lambda: e.tensor_scalar(out=out_ap, in0=a, scalar1=s1, scalar2=s2, op0=op0, op1=op1), reads=reads, writes=writes)

    def stt(out_ap, a, s, b, op0, op1, reads, writes):
        S.op(dve, lambda: V.scalar_tensor_tensor(out=out_ap, in0=a, scalar=s, in1=b, op0=op0, op1=op1), reads=reads, writes=writes)

    def actf(out_ap, in_ap, func, reads, writes, bias=None, scale=None, accum=None):
        kw = {}
        if bias is not None:
            kw["bias"] = bias
        if scale is not None:
            kw["scale"] = scale
        if accum is not None:
            kw["accum_out"] = accum
        S.op(act, lambda: A.activation(out=out_ap, in_=in_ap, func=func, **kw), reads=reads, writes=writes)

    def transpose(ps_ap, in_ap, npart, reads, writes):
        S.op(pe, lambda: PE.transpose(ps_ap, in_ap, ident[:npart, :npart]), reads=list(reads) + [cstt], writes=writes)

    def memset(t, ap, val=0.0):
        S.op(pool, lambda: G.memset(ap, val), reads=[], writes=[t])

    tiles_order = [("p", i) for i in range(4)] + [("s", 0)]
    NL = L
    if DBG is not None:
        tiles_order = DBG["tiles"]
        NL = DBG["layers"]
    gchunks = []
    gcidx = []
    for _t in tiles_order:
        for l in range(NL):
            lc = layer_chunks(l)
            gchunks += lc
            gcidx += [(l, i) for i in range(len(lc))]

    def dbg(name, ap, reads):
        if DBG is None:
            return
        shp = list(ap.shape)
        d = nc.dram_tensor("dbg_" + name, shp, ap.dtype, kind="ExternalOutput").ap()
        S.dma(d, ap, reads=reads, st=None)
    wstate = {"next_load": 0, "next_use": 0, "next_cast": 0}

    def w_load(idx):
        chunk = gchunks[idx]
        stg = wstg[idx % NSLOT]
        o = sum(nkt * nc_ for (wn, l, r0, nkt, c0, nc_) in chunk)
        assert o <= SLOT
        l, ci = gcidx[idx]
        S.dma(stg.ap[:, 0:o], wpack[l, ci, :, 0:o], writes=[stg], st=stg, track=False)
        return o

    wsize = {}

    def w_cast(idx, eng):
        n = wsize[idx]
        stg = wstg[idx % NSLOT]; slot = wslots[idx % NSLOT]
        evac(slot.ap[:, 0:n], stg.ap[:, 0:n], [stg], [slot], eng=eng)

    def w_next(cast_eng=None):
        cast_eng = cast_eng or act
        idx = wstate["next_use"]
        n = len(gchunks)
        if idx == 0:
            for k in range(min(NSLOT, n)):
                wsize[k] = w_load(k)
            wstate["next_load"] = min(NSLOT, n)
            w_cast(0, cast_eng)
            wstate["next_cast"] = 1
        if wstate["next_cast"] <= idx + 1 and wstate["next_cast"] < n:
            k = wstate["next_cast"]
            w_cast(k, cast_eng)
            wstate["next_cast"] = k + 1
        while wstate["next_load"] < n and wstate["next_load"] - NSLOT < wstate["next_cast"]:
            k = wstate["next_load"]
            wsize[k] = w_load(k)
            wstate["next_load"] += 1
        wstate["next_use"] += 1
        slot = wslots[idx % NSLOT]
        views = []
        o = 0
        for (wn, l, r0, nkt, c0, nc_) in gchunks[idx]:
            views.append(slot.ap[:, o:o + nkt * nc_].rearrange("p (k c) -> p k c", k=nkt))
            o += nkt * nc_
        return slot, views

    S.dma(cstt.ap, cst, writes=[cstt], st=None, track=False)
    cst_stg = scr.get()
    S.dma(cst_stg.ap[:, 0:384], cst_r, writes=[cst_stg], st=None, track=False)
    evac(cstr.ap, cst_stg.ap[:, 0:384], [cst_stg], [cstr], eng=dve)
    nonc = nc.allow_non_contiguous_dma(reason="small param loads")
    nonc.__enter__()

    def pdma(dst_tile, dst_ap, src_ap):
        S.dma(dst_ap, src_ap, writes=[dst_tile], st=None, track=False)

    def featvec(dst_tile, dst_ap, src_vec):
        pdma(dst_tile, dst_ap, src_vec.rearrange("(f p) -> p f", p=128))

    def v16(nm):
        return pv[nm].ap

    for l in range(L):
        p = par[l]
        featvec(p["gains"], p["gains"].ap[:, 0, :], g_mix[l])
        featvec(p["gains"], p["gains"].ap[:, 1, :], g_ffn[l])
        featvec(p["gains"], p["gains"].ap[:, 2, :], g_ple[l])
        for i, v_ in enumerate([conv_a_b, ln_a_g, ln_a_b, ssm_d]):
            featvec(p["avec"], p["avec"].ap[:, i, :], v_[l])
        pdma(p["lnb"], p["lnb"].ap[:, 0, :], ln_b_g[l].partition_broadcast(128))
        pdma(p["lnb"], p["lnb"].ap[:, 1, :], ln_b_b[l].partition_broadcast(128))
        bstg = small.get()
        bs_stg = scr.get()
        pdma(bs_stg, bs_stg.ap[0:1, 0:512], b_s_r[l:l + 1, :, :].rearrange("o g i -> o (g i)"))
        evac(p["bs1"].ap[0:1, :, :].rearrange("o g i -> o (g i)"), bs_stg.ap[0:1, 0:512], [bs_stg], [p["bs1"]], eng=dve)
        pdma(p["bs0"], p["bs0"].ap, b_s[l, :, 0].partition_broadcast(128))
        pdma(p["w00"], p["w00"].ap, w_s[l, :, 0, 0].partition_broadcast(128))
        pdma(prep_stg, prep_stg.ap[:31, 0:512], conv_a_w[l])
        pst = P.get()
        for c in range(4):
            transpose(pst.ap[:, 32 * c:32 * c + 31], prep_stg.ap[:31, 128 * c:128 * (c + 1)], 31, [prep_stg], [pst])
        evac(p["caw"].ap, pst.ap[:, 0:128].rearrange("p (c k) -> p c k", k=32)[:, :, 0:31], [pst], [p["caw"]])
        P.put(pst)
        for j4 in range(11):
            stg = scr.get()
            pdma(stg, stg.ap[0:3, :], conv_f_w[l][:, 512 * j4:512 * (j4 + 1)])
            pdma(stg, stg.ap[3:4, :], conv_f_b[l][:, 512 * j4:512 * (j4 + 1)])
            pst = P.get()
            for jj in range(4):
                transpose(pst.ap[:, 4 * jj:4 * jj + 4], stg.ap[0:4, 128 * jj:128 * (jj + 1)], 4, [stg], [pst])
            evac(p["cf"].ap[:, 4 * j4:4 * j4 + 4, :], pst.ap[:, 0:16].rearrange("p (j k) -> p j k", k=4), [pst], [p["cf"]])
            P.put(pst)
        pdma(prep_ws, prep_ws.ap.rearrange("p (g j) -> p g j", g=4), w_s[l].rearrange("g i j -> i g j"))
        for g in range(4):
            tt_op(dve, prep_ws.ap[:, g * 128:(g + 1) * 128], prep_ws.ap[:, g * 128:(g + 1) * 128], tril, ALU.mult,
                  reads=[prep_ws, cstt], writes=[prep_ws])
        pst = P.get()
        for g in range(4):
            transpose(pst.ap[:, g * 128:(g + 1) * 128], prep_ws.ap[:, g * 128:(g + 1) * 128], 128, [prep_ws], [pst])
        evac(p["WsT"].ap.rearrange("p g i -> p (g i)"), pst.ap, [pst], [p["WsT"]])
        P.put(pst)
        for g in range(4):
            ts_op(dve, p["Wsm"].ap[:, g, :], ident[:16, :16], p["w00"].ap[:16, g:g + 1], None, ALU.mult, None,
                  reads=[cstt, p["w00"]], writes=[p["Wsm"]])
        for gl in range(2):
            sl = slice(64 * gl, 64 * gl + 64)
            pdma(pv["are"], v16("are")[sl, :], ssm_a_re[l].rearrange("(q g) n -> g n q", g=2)[gl])
            pdma(pv["aim"], v16("aim")[sl, :], ssm_a_im[l].rearrange("(q g) n -> g n q", g=2)[gl])
            pdma(pv["ldt"], v16("ldt")[sl, :], ssm_log_dt[l].rearrange("(q g) -> g q", g=2)[gl].partition_broadcast(64))
            for ri, src in enumerate([ssm_b_re, ssm_b_im]):
                pdma(pB[ri], pB[ri].ap[sl, :].rearrange("p (q h) -> p q h", q=16),
                     src[l].rearrange("(q g) n h -> g n q h", g=2)[gl])
        actf(v16("dt"), v16("ldt"), AF.Exp, [pv["ldt"]], [pv["dt"]])
        tt_op(dve, v16("zr"), v16("are"), v16("dt"), ALU.mult, [pv["are"], pv["dt"]], [pv["zr"]])
        tt_op(dve, v16("th"), v16("aim"), v16("dt"), ALU.mult, [pv["aim"], pv["dt"]], [pv["th"]])
        ts_op(dve, v16("p"), v16("zr"), 1.0 / 6.0, 1.0, ALU.mult, ALU.add, [pv["zr"]], [pv["p"]])
        for k in [5.0, 4.0, 3.0, 2.0]:
            tt_op(dve, v16("p"), v16("p"), v16("zr"), ALU.mult, [pv["p"], pv["zr"]], [pv["p"]])
            ts_op(dve, v16("p"), v16("p"), 1.0 / k, 1.0, ALU.mult, ALU.add, [pv["p"]], [pv["p"]])
        tt_op(dve, v16("p"), v16("p"), v16("zr"), ALU.mult, [pv["p"], pv["zr"]], [pv["p"]])
        ts_op(dve, v16("er"), v16("p"), 1.0, None, ALU.add, None, [pv["p"]], [pv["er"]])
        actf(v16("s"), v16("th"), AF.Sin, [pv["th"]], [pv["s"]], scale=1.0 / 16.0)
        ts_op(dve, v16("t1"), v16("th"), 1.0 / 16.0, float(np.pi / 2), ALU.mult, ALU.add, [pv["th"]], [pv["t1"]])
        actf(v16("c"), v16("t1"), AF.Sin, [pv["t1"]], [pv["c"]])
        for _ in range(4):
            tt_op(dve, v16("t1"), v16("c"), v16("c"), ALU.mult, [pv["c"]], [pv["t1"]])
            tt_op(dve, v16("t2"), v16("s"), v16("s"), ALU.mult, [pv["s"]], [pv["t2"]])
            tt_op(dve, v16("s"), v16("s"), v16("c"), ALU.mult, [pv["s"], pv["c"]], [pv["s"]])
            ts_op(dve, v16("s"), v16("s"), 2.0, None, ALU.mult, None, [pv["s"]], [pv["s"]])
            tt_op(dve, v16("c"), v16("t1"), v16("t2"), ALU.subtract, [pv["t1"], pv["t2"]], [pv["c"]])
        tt_op(dve, v16("abr"), v16("er"), v16("c"), ALU.mult, [pv["er"], pv["c"]], [pv["abr"]])
        tt_op(dve, v16("abi"), v16("er"), v16("s"), ALU.mult, [pv["er"], pv["s"]], [pv["abi"]])
        H = p["HSC"]
        evac(H.ap[:, 0, 0, :], v16("abr"), [pv["abr"]], [H], eng=dve)
        evac(H.ap[:, 0, 1, :], v16("abi"), [pv["abi"]], [H], eng=dve)
        ts_op(dve, H.ap[:, 0, 2, :], v16("abi"), -1.0, None, ALU.mult, None, [pv["abi"]], [H])
        evac(p["RHO"].ap, v16("er"), [pv["er"]], [p["RHO"]], eng=dve)
        Uv = uph.ap.rearrange("p (k c q) -> p k c q", k=NSTEP, c=3)
        evac(Uv[:, 0, 0, :], v16("c"), [pv["c"]], [uph], eng=dve)
        evac(Uv[:, 0, 1, :], v16("s"), [pv["s"]], [uph], eng=dve)
        for k in range(1, NSTEP):
            tt_op(dve, v16("t1"), Uv[:, k - 1, 0, :], Uv[:, k - 1, 0, :], ALU.mult, [uph], [pv["t1"]])
            tt_op(dve, v16("t2"), Uv[:, k - 1, 1, :], Uv[:, k - 1, 1, :], ALU.mult, [uph], [pv["t2"]])
            tt_op(dve, Uv[:, k, 0, :], v16("t1"), v16("t2"), ALU.subtract, [pv["t1"], pv["t2"]], [uph])
            tt_op(dve, v16("t1"), Uv[:, k - 1, 0, :], Uv[:, k - 1, 1, :], ALU.mult, [uph], [pv["t1"]])
            ts_op(dve, Uv[:, k, 1, :], v16("t1"), 2.0, None, ALU.mult, None, [pv["t1"]], [uph])
        for k in range(NSTEP):
            ts_op(dve, Uv[:, k, 2, :], Uv[:, k, 1, :], -1.0, None, ALU.mult, None, [uph], [uph])
        evac(p["UPH"].ap, Uv[:, 0, :, :], [uph], [p["UPH"]], eng=dve)
        Cr = prep_x[0].ap.rearrange("p (q r) -> p q r", q=16); Sr = prep_x[1].ap.rearrange("p (q r) -> p q r", q=16)
        Bc = pBb[0].ap.rearrange("p (q r) -> p q r", q=16); Bs = pBb[1].ap.rearrange("p (q r) -> p q r", q=16)
        T1 = prep_cst[0].ap.rearrange("p (q r) -> p q r", q=16); T2 = prep_cst[1].ap.rearrange("p (q r) -> p q r", q=16)
        tset = [prep_x[0], prep_x[1], pBb[0], pBb[1], prep_cst[0], prep_cst[1], uph]

        def dbl(Ctab, Stab, k0, nsteps):
            S.op(dve, lambda: V.memset(Ctab[:, :, 0:1], 1.0), reads=[], writes=tset)
            S.op(dve, lambda: V.memset(Stab[:, :, 0:1], 0.0), reads=[], writes=tset)
            for i in range(nsteps):
                d = 1 << i
                ckb = Uv[:, k0 + i, 0, :].unsqueeze(2).to_broadcast([128, 16, d])
                skb = Uv[:, k0 + i, 1, :].unsqueeze(2).to_broadcast([128, 16, d])
                tt_op(dve, T1[:, :, 0:d], Ctab[:, :, 0:d], ckb, ALU.mult, tset, tset)
                tt_op(dve, T2[:, :, 0:d], Stab[:, :, 0:d], skb, ALU.mult, tset, tset)
                tt_op(dve, Ctab[:, :, d:2 * d], T1[:, :, 0:d], T2[:, :, 0:d], ALU.subtract, tset, tset)
                tt_op(dve, T1[:, :, 0:d], Stab[:, :, 0:d], ckb, ALU.mult, tset, tset)
                tt_op(dve, T2[:, :, 0:d], Ctab[:, :, 0:d], skb, ALU.mult, tset, tset)
                tt_op(dve, Stab[:, :, d:2 * d], T1[:, :, 0:d], T2[:, :, 0:d], ALU.add, tset, tset)
        dbl(Cr, Sr, 0, 5)
        dbl(Bc[:, :, 0:16], Bs[:, :, 0:16], 5, 4)
        for q in range(16):
            def bm(tab):
                return tab[:, q, 0:16].unsqueeze(2).to_broadcast([128, 16, 32])

            def br(tab):
                return tab[:, q, :].unsqueeze(1).to_broadcast([128, 16, 32])
            c1 = scr.get(); c2 = scr.get(); s1_ = scr.get(); s2_ = scr.get()

            def v3(t):
                return t.ap.rearrange("p (m r) -> p m r", m=16)
            tt_op(dve, v3(c1), bm(Bc), br(Cr), ALU.mult, tset, [c1])
            tt_op(dve, v3(c2), bm(Bs), br(Sr), ALU.mult, tset, [c2])
            tt_op(dve, c1.ap, c1.ap, c2.ap, ALU.subtract, [c1, c2], [c1])
            tt_op(pool, v3(s1_), bm(Bs), br(Cr), ALU.mult, tset, [s1_])
            tt_op(pool, v3(s2_), bm(Bc), br(Sr), ALU.mult, tset, [s2_])
            tt_op(pool, s1_.ap, s1_.ap, s2_.ap, ALU.add, [s1_, s2_], [s1_])
            S.dma(tab_d[l, q, :, 0:512], c1.ap, reads=[c1], st=None, track=False)
            S.dma(tab_d[l, q, :, 512:1024], s1_.ap, reads=[s1_], st=None, track=False)
        ts_op(dve, v16("pp"), v16("abr"), -1.0, None, ALU.add, None, [pv["abr"]], [pv["pp"]])
        tt_op(dve, v16("t1"), v16("are"), v16("are"), ALU.mult, [pv["are"]], [pv["t1"]])
        tt_op(dve, v16("t2"), v16("aim"), v16("aim"), ALU.mult, [pv["aim"]], [pv["t2"]])
        tt_op(dve, v16("den"), v16("t1"), v16("t2"), ALU.add, [pv["t1"], pv["t2"]], [pv["den"]])
        S.op(dve, lambda: V.reciprocal(out=v16("den"), in_=v16("den")), reads=[pv["den"]], writes=[pv["den"]])
        tt_op(dve, v16("t1"), v16("pp"), v16("are"), ALU.mult, [pv["pp"], pv["are"]], [pv["t1"]])
        tt_op(dve, v16("t2"), v16("abi"), v16("aim"), ALU.mult, [pv["abi"], pv["aim"]], [pv["t2"]])
        tt_op(dve, v16("t1"), v16("t1"), v16("t2"), ALU.add, [pv["t1"], pv["t2"]], [pv["t1"]])
        tt_op(dve, v16("cfr"), v16("t1"), v16("den"), ALU.mult, [pv["t1"], pv["den"]], [pv["cfr"]])
        tt_op(dve, v16("t1"), v16("abi"), v16("are"), ALU.mult, [pv["abi"], pv["are"]], [pv["t1"]])
        tt_op(dve, v16("t2"), v16("pp"), v16("aim"), ALU.mult, [pv["pp"], pv["aim"]], [pv["t2"]])
        tt_op(dve, v16("t1"), v16("t1"), v16("t2"), ALU.subtract, [pv["t1"], pv["t2"]], [pv["t1"]])
        tt_op(dve, v16("cfi"), v16("t1"), v16("den"), ALU.mult, [pv["t1"], pv["den"]], [pv["cfi"]])
        ts_op(dve, v16("ncfi"), v16("cfi"), -1.0, None, ALU.mult, None, [pv["cfi"]], [pv["ncfi"]])
        for q in range(16):
            bq = slice(16 * q, 16 * q + 16)
            ts_op(dve, pBb[0].ap[:, bq], pB[0].ap[:, bq], v16("cfr")[:, q:q + 1], None, ALU.mult, None,
                  [pB[0], pv["cfr"]], [pBb[0]])
            stt(pBb[0].ap[:, bq], pB[1].ap[:, bq], v16("ncfi")[:, q:q + 1], pBb[0].ap[:, bq], ALU.mult, ALU.add,
                [pB[1], pv["ncfi"], pBb[0]], [pBb[0]])
            ts_op(dve, pBb[1].ap[:, bq], pB[1].ap[:, bq], v16("cfr")[:, q:q + 1], None, ALU.mult, None,
                  [pB[1], pv["cfr"]], [pBb[1]])
            stt(pBb[1].ap[:, bq], pB[0].ap[:, bq], v16("cfi")[:, q:q + 1], pBb[1].ap[:, bq], ALU.mult, ALU.add,
                [pB[0], pv["cfi"], pBb[1]], [pBb[1]])
        for ri in range(2):
            X = prep_x[ri]
            memset(X, X.ap)
            Xv = X.ap.rearrange("p (q g h) -> p q g h", q=16, g=2)
            Bv = pBb[ri].ap.rearrange("p (q h) -> p q h", q=16)
            evac(Xv[0:64, :, 0, :], Bv[0:64, :, :], [pBb[ri]], [X], eng=dve)
            evac(Xv[64:128, :, 1, :], Bv[64:128, :, :], [pBb[ri]], [X], eng=dve)
            pst = P.get()
            for c in range(4):
                transpose(pst.ap[:, c * 128:(c + 1) * 128], X.ap[:, c * 128:(c + 1) * 128], 128, [X], [pst])
            dstT = p["BbRe"] if ri == 0 else p["BbIm"]
            evac(dstT.ap.rearrange("p c m -> p (c m)"), pst.ap, [pst], [dstT])
            P.put(pst)
        for ri, src in enumerate([ssm_c_re, ssm_c_im]):
            Cs = prep_cst[ri]
            memset(Cs, Cs.ap)
            Cv = Cs.ap.rearrange("p (c m) -> p c m", c=4)
            for c in range(4):
                for gi in range(8):
                    S.dma(Cv[16 * gi:16 * gi + 16, c, 64 * (gi % 2):64 * (gi % 2) + 64], src[l, 8 * c + gi],
                          writes=[Cs], st=Cs, track=False, group=True)
            pst = P.get()
            for c in range(4):
                transpose(pst.ap[:, c * 128:(c + 1) * 128], Cs.ap[:, c * 128:(c + 1) * 128], 128, [Cs], [pst])
            if ri == 0:
                evac(p["CTre"].ap.rearrange("p c m -> p (c m)"), pst.ap, [pst], [p["CTre"]], eng=dve)
            else:
                ts_op(dve, p["CTim"].ap.rearrange("p c m -> p (c m)"), pst.ap, -1.0, None, ALU.mult, None, [pst], [p["CTim"]])
            P.put(pst)
    featvec(gfin, gfin.ap, g_final)
    nonc.__exit__(None, None, None)
    for i in range(2):
        memset(Cpad_re[i], Cpad_re[i].ap)
        memset(Cpad_im[i], Cpad_im[i].ap)

    def load_cpad(l, c):
        i = c % 2
        for j in range(4):
            evac(Cpad_re[i].ap[:, j, 32 * j:32 * j + 32], par[l]["CTre"].ap[:, c, 32 * j:32 * j + 32],
                 [par[l]["CTre"]], [Cpad_re[i]], eng=pool)
            evac(Cpad_im[i].ap[:, j, 32 * j:32 * j + 32], par[l]["CTim"].ap[:, c, 32 * j:32 * j + 32],
                 [par[l]["CTim"]], [Cpad_im[i]], eng=pool)

    def S5_ENG():
        return dve if S5_USE_DVE else pool

    tabstate = {"n": 0, "drained": False}

    def tab_load(l, q):
        if not tabstate["drained"]:
            S.drain_pool(S.sp)
            tabstate["drained"] = True
        slot = tabring[tabstate["n"] % 2]
        tabstate["n"] += 1
        S.dma(slot.ap, tab_d[l, q], writes=[slot], st=slot, track=False)
        return slot

    def rmsnorm(gcol_tile, gcol, dst, NT):
        pss = P.get()
        for f in range(8):
            sq = scr_r.get()
            actf(sq.ap[:, :NT], x[f].ap[:, :NT], AF.Square, [x[f]], [sq])
            S.op(pe, lambda: PE.matmul(pss.ap[:, :NT], lhsT=ones_d, rhs=sq.ap[:, :NT], start=(f == 0), stop=(f == 7)),
                 reads=[cstr, sq], writes=[pss])
        sd = scr.get()
        actf(sd.ap[:, :NT], pss.ap[:, :NT], AF.Sqrt, [pss], [sd], bias=EPS)
        P.put(pss)
        S.op(dve, lambda: V.reciprocal(out=sd.ap[:, :NT], in_=sd.ap[:, :NT]), reads=[sd], writes=[sd])
        for f in range(8):
            stt(dst[f].ap[:, :NT], x[f].ap[:, :NT], gcol(f), sd.ap[:, :NT], ALU.mult, ALU.mult,
                [x[f], gcol_tile, sd], [dst[f]])

    def layer(l, kind, ti, NT):
        p = par[l]
        samp = (kind == "s")
        first = (kind == "p" and ti == 0)
        last = (kind == "p" and ti == 3)
        R = 1 if samp else 4
        MT = NS if samp else 128
        gains = p["gains"]
        cf = p["cf"]
        rmsnorm(gains, lambda f: gains.ap[:, 0, f:f + 1], h, NT)
        tg = "%s%dl%d_" % (kind, ti, l)
        dbg(tg + "h0", h[0].ap[:, :NT], [h[0]])

        def a3(c):
            return aext[c].ap[:, 0:496].rearrange("p (b k) -> p b k", k=31)

        def build_dg(c):
            dgt_ = dg[c % 2]
            S.op(pool, lambda: G.tensor_tensor(out=dgt_.ap, in0=ident.unsqueeze(1).to_broadcast([128, 31, 128]),
                                               in1=p["caw"].ap[:, c, :].unsqueeze(2).to_broadcast([128, 31, 128]), op=ALU.mult),
                 reads=[cstt, p["caw"]], writes=[dgt_])
        build_dg(0)
        build_dg(1)

        if samp:
            rows = st_ca[l].rearrange("b k c -> (b k) c")
            for r4 in range(4):
                S.dma(hsAB[r4].ap[:120, :], rows[120 * r4:120 * r4 + 120, :], writes=[hsAB[r4]], st=hsAB[r4])
            for c in range(4):
                pst = P.get()
                for r4 in range(4):
                    transpose(pst.ap[:, r4 * 120:(r4 + 1) * 120], hsAB[r4].ap[:120, c * 128:(c + 1) * 128], 120, [hsAB[r4]], [pst])
                evac(a3(c)[:, :, 0:30], pst.ap[:, 0:480].rearrange("p (b k) -> p b k", k=30), [pst], [aext[c]])
                P.put(pst)
            S.dma(o_ca_s[l, :, 0:29, :], st_ca[l, :, 1:30, :], st=None)
        else:
            for c in range(4):
                if first:
                    memset(aext[c], aext[c].ap[:, 0:30])
                else:
                    evac(aext[c].ap[:, 0:30], p["carA"].ap[:, c, :], [p["carA"]], [aext[c]], eng=pool)
        for c in range(4):
            slot, (vl, vg) = w_next()
            psl = P.get(); psg = P.get()
            mm(psl.ap[:, :NT], [(vl[:, kt, :], h[kt].ap[:, :NT]) for kt in range(8)], [slot] + h, [psl])
            mm(psg.ap[:, :NT], [(vg[:, kt, :], h[kt].ap[:, :NT]) for kt in range(8)], [slot] + h, [psg])
            sg = scr.get()
            actf(sg.ap[:, :NT], psg.ap[:, :NT], AF.Sigmoid, [psg], [sg])
            dsta = a3(c)[:, :, 30] if samp else aext[c].ap[:, 30:30 + NT]
            tt_op(dve, dsta, psl.ap[:, :NT], sg.ap[:, :NT], ALU.mult, [psl, sg], [aext[c]])
            if samp:
                tt_op(dve, alast.ap[:, c, 0:NS], psl.ap[:, :NS], sg.ap[:, :NS], ALU.mult, [psl, sg], [alast])
            elif last:
                tt_op(dve, alast.ap[:, c, 0:30], psl.ap[:, NT - 30:NT], sg.ap[:, NT - 30:NT], ALU.mult, [psl, sg], [alast])
            P.put(psl, psg)
        if samp or last:
            pst = P.get()
            nr = NS if samp else 30
            for c in range(4):
                transpose(pst.ap[:nr, c * 128:(c + 1) * 128], alast.ap[:, c, 0:nr], 128, [alast], [pst])
            so = scr.get()
            evac(so.ap[:nr, :], pst.ap[:nr, :], [pst], [so])
            P.put(pst)
            if samp:
                S.dma(o_ca_s[l, :, 29, :], so.ap[:NS, :], reads=[so], st=so)
            else:
                S.dma(o_ca_p[l], so.ap[:30, :], reads=[so], st=so)
        if not samp and not last:
            for c in range(4):
                evac(p["carA"].ap[:, c, :], aext[c].ap[:, NT:NT + 30], [aext[c]], [p["carA"]], eng=pool)
        accs = hsAB
        for c in range(4):
            acc = accs[c]
            dgt = dg[c % 2]
            if c >= 2:
                build_dg(c)

            def tap(k):
                return a3(c)[:, :, k] if samp else aext[c].ap[:, k:k + NT]
            psc = P.get()
            mm(psc.ap[:, :NT], [(dgt.ap[:, k, :], tap(k)) for k in range(31)], [dgt, aext[c]], [psc])
            actf(acc.ap[:, :NT], psc.ap[:, :NT], AF.Identity, [psc, p["avec"]], [acc], bias=p["avec"].ap[:, 0, c:c + 1])
            P.put(psc)
        for c2 in range(2):
            slot, (vu,) = w_next()
            for cc in range(2):
                c = 2 * c2 + cc
                psu = P.get()
                mm(psu.ap[:, :NT], [(vu[:, kt, cc * 128:(cc + 1) * 128], h[kt].ap[:, :NT]) for kt in range(8)], [slot] + h, [psu])
                evac(ub[c].ap[:, :NT], psu.ap[:, :NT], [psu], [ub[c]])
                P.put(psu)
        psv = [P.get() for r in range(R)]
        for kh in range(2):
            slot, (vv,) = w_next()
            for r in range(R):
                def fn():
                    ins = None
                    for kk in range(4):
                        kt = 4 * kh + kk
                        ins = PE.matmul(psv[r].ap[:MT, :], lhsT=h[kt].ap[:, r * 128:r * 128 + MT], rhs=vv[:, kk, :],
                                        start=(kh == 0 and kk == 0), stop=(kh == 1 and kk == 3))
                    return ins
                S.op(pe, fn, reads=[slot] + h, writes=[psv[r]])
        lnb = p["lnb"]
        for r in range(R):
            st1 = small.get()
            vs = scr.get()
            actf(vs.ap[:MT, :], psv[r].ap[:MT, :], AF.Identity, [psv[r]], [vs, st1], accum=st1.ap[:MT, 0:1])
            junk = scr.get()
            actf(junk.ap[:MT, :], psv[r].ap[:MT, :], AF.Square, [psv[r]], [junk, st1], accum=st1.ap[:MT, 1:2])
            P.put(psv[r])
            ts_op(dve, st1.ap[:MT, 2:3], st1.ap[:MT, 0:1], 1.0 / DA, None, ALU.mult, None, [st1], [st1])
            tt_op(dve, st1.ap[:MT, 3:4], st1.ap[:MT, 2:3], st1.ap[:MT, 2:3], ALU.mult, [st1], [st1])
            stt(st1.ap[:MT, 4:5], st1.ap[:MT, 1:2], 1.0 / DA, st1.ap[:MT, 3:4], ALU.mult, ALU.subtract, [st1], [st1])
            actf(st1.ap[:MT, 5:6], st1.ap[:MT, 4:5], AF.Sqrt, [st1], [st1], bias=EPS)
            S.op(dve, lambda: V.reciprocal(out=st1.ap[:MT, 6:7], in_=st1.ap[:MT, 5:6]), reads=[st1], writes=[st1])
            stt(st1.ap[:MT, 7:8], st1.ap[:MT, 2:3], -1.0, st1.ap[:MT, 6:7], ALU.mult, ALU.mult, [st1], [st1])
            ts_op(dve, vs.ap[:MT, :], vs.ap[:MT, :], st1.ap[:MT, 6:7], st1.ap[:MT, 7:8], ALU.mult, ALU.add, [vs, st1], [vs])
            tt_op(pool, vs.ap[:MT, :], vs.ap[:MT, :], lnb.ap[:MT, 0, :], ALU.mult, [vs, lnb], [vs])
            if samp or (last and r == 3):
                tt_op(pool, vs.ap[:MT, :], vs.ap[:MT, :], lnb.ap[:MT, 1, :], ALU.add, [vs, lnb], [vs])
                evac(vtok[r].ap[:MT, :], vs.ap[:MT, :], [vs], [vtok[r]], eng=act)
                S.dma(o_cv_s[l] if samp else o_cv_p[l], vs.ap[:MT, :], reads=[vs], st=vs)
            else:
                tt_op(pool, vtok[r].ap[:MT, :], vs.ap[:MT, :], lnb.ap[:MT, 1, :], ALU.add, [vs, lnb], [vtok[r]])
        dbg(tg + "vtok0", vtok[0].ap[:MT, :], [vtok[0]])
        for g in range(4):
            psm_ = P.get()
            for r in range(R):
                if samp:
                    S.op(pe, lambda: PE.matmul(psm_.ap[:, :NS], lhsT=vtok[0].ap[:NS, g * 128:(g + 1) * 128],
                                               rhs=p["Wsm"].ap[:NS, g, :], start=True, stop=True),
                         reads=[vtok[0], p["Wsm"]], writes=[psm_])
                else:
                    def fn():
                        PE.matmul(psm_.ap[:, r * 128:(r + 1) * 128], lhsT=vtok[r].ap[:, g * 128:(g + 1) * 128],
                                  rhs=p["WsT"].ap[:, g, :], start=True, stop=False)
                        return PE.matmul(psm_.ap[:, r * 128:(r + 1) * 128], lhsT=ones_row,
                                         rhs=p["bs1"].ap[0:1, g, :], start=False, stop=True)
                    S.op(pe, fn, reads=[vtok[r], p["WsT"], p["bs1"], cstr], writes=[psm_])
            if samp:
                tmp = scr.get()
                ts_op(dve, tmp.ap[:, :NS], psm_.ap[:, :NS], p["bs0"].ap[:, g:g + 1], None, ALU.add, None, [psm_, p["bs0"]], [tmp])
                tt_op(dve, ub[g].ap[:, :NS], ub[g].ap[:, :NS], tmp.ap[:, :NS], ALU.mult, [ub[g], tmp], [ub[g]])
            else:
                tt_op(dve, ub[g].ap[:, :NT], ub[g].ap[:, :NT], psm_.ap[:, :NT], ALU.mult, [ub[g], psm_], [ub[g]])
            P.put(psm_)
        dbg(tg + "ub0", ub[0].ap[:, :NT], [ub[0]])

        for c2 in range(2):
            slot, (vz,) = w_next()
            for cc in range(2):
                c = 2 * c2 + cc
                psz = P.get()
                mm(psz.ap[:, :NT], [(vz[:, kt, cc * 128:(cc + 1) * 128], h[kt].ap[:, :NT]) for kt in range(8)], [slot] + h, [psz])
                evac(zc[c].ap[:, :NT], psz.ap[:, :NT], [psz], [zc[c]])
                P.put(psz)
        dbg(tg + "zc0", zc[0].ap[:, :NT], [zc[0]])
        ps1 = P.get(); ps2 = P.get()
        for c in range(4):
            a_r = scr_r.get()
            evac(a_r.ap[:, :NT], accs[c].ap[:, :NT], [accs[c]], [a_r], eng=act)
            S.op(pe, lambda: PE.matmul(ps1.ap[:, :NT], lhsT=ones_c, rhs=a_r.ap[:, :NT], start=(c == 0), stop=(c == 3)),
                 reads=[cstr, a_r], writes=[ps1])
            sq = scr_r.get()
            actf(sq.ap[:, :NT], accs[c].ap[:, :NT], AF.Square, [accs[c]], [sq])
            S.op(pe, lambda: PE.matmul(ps2.ap[:, :NT], lhsT=ones_c, rhs=sq.ap[:, :NT], start=(c == 0), stop=(c == 3)),
                 reads=[cstr, sq], writes=[ps2])
        mean = scr.get()
        evac(mean.ap[:, :NT], ps1.ap[:, :NT], [ps1], [mean], eng=act)
        var = scr.get()
        tt_op(dve, var.ap[:, :NT], mean.ap[:, :NT], mean.ap[:, :NT], ALU.mult, [mean], [var])
        tt_op(dve, var.ap[:, :NT], ps2.ap[:, :NT], var.ap[:, :NT], ALU.subtract, [ps2, var], [var])
        P.put(ps1, ps2)
        actf(var.ap[:, :NT], var.ap[:, :NT], AF.Sqrt, [var], [var], bias=EPS)
        S.op(dve, lambda: V.reciprocal(out=var.ap[:, :NT], in_=var.ap[:, :NT]), reads=[var], writes=[var])
        for c in range(4):
            tt_op(pool, accs[c].ap[:, :NT], accs[c].ap[:, :NT], mean.ap[:, :NT], ALU.subtract, [accs[c], mean], [accs[c]])
            tt_op(dve, accs[c].ap[:, :NT], accs[c].ap[:, :NT], var.ap[:, :NT], ALU.mult, [accs[c], var], [accs[c]])
            actf(acs[c].ap[:, :NT], accs[c].ap[:, :NT], AF.Silu, [accs[c], p["avec"]], [acs[c]],
                 scale=p["avec"].ap[:, 1, c:c + 1], bias=p["avec"].ap[:, 2, c:c + 1])
        dbg(tg + "acs0", acs[0].ap[:, :NT], [acs[0]])

        def merge1_pe(f):
            s1_, (vgA, vgB) = w_next(dve)
            pgA = P.get(); pgB = P.get()
            mm(pgA.ap[:, :NT], [(vgA[:, kt, :], h[kt].ap[:, :NT]) for kt in range(8)], [s1_] + h, [pgA])
            mm(pgB.ap[:, :NT], [(vgB[:, kt, :], h[kt].ap[:, :NT]) for kt in range(8)], [s1_] + h, [pgB])
            s2_, (vao, vbo, vgC) = w_next(dve)
            pgC = P.get()
            mm(pgC.ap[:, :NT], [(vgC[:, kt, :], h[kt].ap[:, :NT]) for kt in range(8)], [s2_] + h, [pgC])
            pao = P.get(); pbo = P.get()
            mm(pao.ap[:, :NT], [(vao[:, kt, :], acs[kt].ap[:, :NT]) for kt in range(4)], [s2_] + acs, [pao])
            mm(pbo.ap[:, :NT], [(vbo[:, kt, :], ub[kt].ap[:, :NT]) for kt in range(4)], [s2_] + ub, [pbo])
            sA = scr.get(); sB = scr.get()
            actf(sA.ap[:, :NT], pgA.ap[:, :NT], AF.Sigmoid, [pgA], [sA])
            actf(sB.ap[:, :NT], pgB.ap[:, :NT], AF.Sigmoid, [pgB], [sB])
            actf(sCt[f].ap[:, :NT], pgC.ap[:, :NT], AF.Sigmoid, [pgC], [sCt[f]])
            P.put(pgA, pgB, pgC)
            return (sA, sB, pao, pbo)

        def merge1_rest(f, st_):
            sA, sB, pao, pbo = st_
            tt_op(dve, sA.ap[:, :NT], pao.ap[:, :NT], sA.ap[:, :NT], ALU.mult, [pao, sA], [sA])
            tt_op(dve, sB.ap[:, :NT], pbo.ap[:, :NT], sB.ap[:, :NT], ALU.mult, [pbo, sB], [sB])
            P.put(pao, pbo)
            tt_op(pool, m[f].ap[:, :NT], sA.ap[:, :NT], sB.ap[:, :NT], ALU.add, [sA, sB], [m[f]])

        H = p["HSC"]
        carS = p["carS"]
        s0T = xstage[0]
        if samp:
            for ri, srcst in enumerate([st_re, st_im]):
                for i4 in range(4):
                    S.dma(hsAB[i4].ap[:NS, :], srcst[l][:, 512 * i4:512 * (i4 + 1)], writes=[hsAB[i4]], st=hsAB[i4])
                pst = P.get()
                for q in range(16):
                    transpose(pst.ap[:, 16 * q:16 * q + 16], hsAB[q // 4].ap[:NS, 128 * (q % 4):128 * (q % 4 + 1)], NS,
                              [hsAB[q // 4]], [pst])
                evac(s0T.ap[:, 256 * ri:256 * ri + 256], pst.ap[:, 0:256], [pst], [s0T])
                P.put(pst)
        psy = None
        srow = {}
        tabs = {}
        bu = {}
        if not samp:
            tabs[0] = tab_load(l, 0)
        for q in range(16):
            c = q // 4; j = q % 4
            if j == 0:
                load_cpad(l, c)
            def emit_bu(qq):
                cc_ = qq // 4; jj_ = qq % 4
                rs_ = slice(32 * jj_, 32 * jj_ + 32)
                a_ = P.get(); b_ = P.get()
                S.op(pe, lambda: PE.matmul(a_.ap[:, :NT], lhsT=p["BbRe"].ap[rs_, cc_, :], rhs=zc[cc_].ap[rs_, :NT], start=True,
                                           stop=True, tile_position=(32 * jj_, 0)), reads=[p["BbRe"], zc[cc_]], writes=[a_])
                S.op(pe, lambda: PE.matmul(b_.ap[:, :NT], lhsT=p["BbIm"].ap[rs_, cc_, :], rhs=zc[cc_].ap[rs_, :NT], start=True,
                                           stop=True, tile_position=(32 * jj_, 0)), reads=[p["BbIm"], zc[cc_]], writes=[b_])
                return a_, b_
            def pre(qq):
                a_, b_ = bu.pop(qq)
                tq = tabs[qq]
                Cq = tq.ap[:, 0:NT]; Sq = tq.ap[:, 512:512 + NT]
                v0, v1, v2, v3 = (hsAB if qq % 2 == 0 else hsCD)
                tt_op(dve, v0.ap[:, :NT], a_.ap[:, :NT], Cq, ALU.mult, [a_, tq], [v0])
                tt_op(dve, v1.ap[:, :NT], b_.ap[:, :NT], Sq, ALU.mult, [b_, tq], [v1])
                tt_op(dve, v2.ap[:, :NT], b_.ap[:, :NT], Cq, ALU.mult, [b_, tq], [v2])
                tt_op(dve, v3.ap[:, :NT], a_.ap[:, :NT], Sq, ALU.mult, [a_, tq], [v3])
                P.put(a_, b_)
            if q == 0:
                bu[0] = emit_bu(0)
                if not samp:
                    pre(0)
            if samp:
                psr, psi = bu.pop(q)
            c1 = H.ap[:, 0, 0, q:q + 1]; s1 = H.ap[:, 0, 1, q:q + 1]; ns1 = H.ap[:, 0, 2, q:q + 1]
            m1st = None
            if (not samp) and q % 2 == 0:
                m1st = merge1_pe(q // 2)
            if samp:
                cur = hsAB[0:2]
                evac(cur[0].ap[:, :NT], psr.ap[:, :NT], [psr], [cur[0]], eng=act)
                evac(cur[1].ap[:, :NT], psi.ap[:, :NT], [psi], [cur[1]], eng=act)
                P.put(psr, psi)
                sre = s0T.ap[:, 0:256].rearrange("p (q b) -> p q b", q=16)[:, q, :]
                sim = s0T.ap[:, 256:512].rearrange("p (q b) -> p q b", q=16)[:, q, :]
                stt(cur[0].ap[:, :NT], sre, c1, cur[0].ap[:, :NT], ALU.mult, ALU.add, [s0T, H, cur[0]], [cur[0]])
                stt(cur[0].ap[:, :NT], sim, ns1, cur[0].ap[:, :NT], ALU.mult, ALU.add, [s0T, H, cur[0]], [cur[0]])
                stt(cur[1].ap[:, :NT], sim, c1, cur[1].ap[:, :NT], ALU.mult, ALU.add, [s0T, H, cur[1]], [cur[1]])
                stt(cur[1].ap[:, :NT], sre, s1, cur[1].ap[:, :NT], ALU.mult, ALU.add, [s0T, H, cur[1]], [cur[1]])
                fin = hsF
                for ri in range(2):
                    evac(hsF[ri].ap[:, :NT], cur[ri].ap[:, :NT], [cur[ri]], [hsF[ri]], eng=act)
                    if j == 0:
                        srow[ri] = P.get()
                    transpose(srow[ri].ap[:NS, 128 * j:128 * (j + 1)], cur[ri].ap[:, :NS], 128, [cur[ri]], [srow[ri]])
                    if j == 3:
                        so = scr.get()
                        evac(so.ap[:NS, :], srow[ri].ap[:NS, :], [srow[ri]], [so])
                        P.put(srow[ri])
                        dsto = (o_re_s if ri == 0 else o_im_s)[l, :, q - 3:q + 1, :]
                        S.dma(dsto, so.ap[:NS, :].rearrange("b (q m) -> b q m", q=4), reads=[so], st=so)
            else:
                tb = tabs[q]
                if q + 1 < 16:
                    tabs[q + 1] = tab_load(l, q + 1)
                Ct = tb.ap[:, 0:NT]; Sn = tb.ap[:, 512:512 + NT]
                w0, w1, w2, w3 = (hsAB if q % 2 == 0 else hsCD)
                hf = hsF2[q % 2]
                tt_op(dve, w0.ap[:, :NT], w0.ap[:, :NT], w1.ap[:, :NT], ALU.add, [w0, w1], [w0])
                tt_op(dve, w2.ap[:, :NT], w2.ap[:, :NT], w3.ap[:, :NT], ALU.subtract, [w2, w3], [w2])
                rho = p["RHO"].ap[:, q:q + 1].to_broadcast([128, NT])
                if first:
                    ini_re = 0.0; ini_im = 0.0
                    ird = []
                else:
                    st0 = small.get()
                    U = p["UPH"]
                    sre = carS.ap[:, 0, q:q + 1]; sim = carS.ap[:, 1, q:q + 1]
                    uc = U.ap[:, 0, q:q + 1]; us = U.ap[:, 1, q:q + 1]; uns = U.ap[:, 2, q:q + 1]
                    ts_op(dve, st0.ap[:, 0:1], sre, uc, None, ALU.mult, None, [carS, U], [st0])
                    stt(st0.ap[:, 0:1], sim, uns, st0.ap[:, 0:1], ALU.mult, ALU.add, [carS, U, st0], [st0])
                    ts_op(dve, st0.ap[:, 1:2], sim, uc, None, ALU.mult, None, [carS, U], [st0])
                    stt(st0.ap[:, 1:2], sre, us, st0.ap[:, 1:2], ALU.mult, ALU.add, [carS, U, st0], [st0])
                    ini_re = st0.ap[:, 0:1]; ini_im = st0.ap[:, 1:2]
                    ird = [st0]
                S.op(dve, lambda: V.tensor_tensor_scan(out=w1.ap[:, :NT], data0=rho, data1=w0.ap[:, :NT], initial=ini_re,
                                                       op0=ALU.mult, op1=ALU.add), reads=[w0, p["RHO"]] + ird, writes=[w1])
                S.op(dve, lambda: V.tensor_tensor_scan(out=w3.ap[:, :NT], data0=rho, data1=w2.ap[:, :NT], initial=ini_im,
                                                       op0=ALU.mult, op1=ALU.add), reads=[w2, p["RHO"]] + ird, writes=[w3])
                if q + 1 < 16:
                    bu[q + 1] = emit_bu(q + 1)
                    pre(q + 1)
                pa = scr.get(); pb = scr.get()
                tt_op(pool, pa.ap[:, :NT], w3.ap[:, :NT], Ct, ALU.mult, [w3, tb], [pa])
                tt_op(pool, pb.ap[:, :NT], w1.ap[:, :NT], Sn, ALU.mult, [w1, tb], [pb])
                tt_op(dve, w0.ap[:, :NT], w1.ap[:, :NT], Ct, ALU.mult, [w1, tb], [w0])
                pd = scr.get()
                tt_op(pool, pd.ap[:, :NT], w3.ap[:, :NT], Sn, ALU.mult, [w3, tb], [pd])
                tt_op(dve, w0.ap[:, :NT], w0.ap[:, :NT], pd.ap[:, :NT], ALU.subtract, [w0, pd], [w0])
                tt_op(pool, pa.ap[:, :NT], pa.ap[:, :NT], pb.ap[:, :NT], ALU.add, [pa, pb], [pa])
                evac(hf[0].ap[:, :NT], w0.ap[:, :NT], [w0], [hf[0]], eng=act)
                evac(hf[1].ap[:, :NT], pa.ap[:, :NT], [pa], [hf[1]], eng=act)
                fin = hf
                evac(carS.ap[:, 0, q:q + 1], w0.ap[:, NT - 1:NT], [w0], [carS], eng=pool)
                evac(carS.ap[:, 1, q:q + 1], pa.ap[:, NT - 1:NT], [pa], [carS], eng=pool)
            if samp and q + 1 < 16:
                bu[q + 1] = emit_bu(q + 1)
            if m1st is not None:
                merge1_rest(q // 2, m1st)
            if j == 0:
                psy = P.get()
            cpr = Cpad_re[c % 2]; cpi = Cpad_im[c % 2]
            if q == 0:
                dbg(tg + "sre0", fin[0].ap[:, :NT], [fin[0]])
                dbg(tg + "sim0", fin[1].ap[:, :NT], [fin[1]])

            def fny():
                PE.matmul(psy.ap[:, :NT], lhsT=cpr.ap[:, j, :], rhs=fin[0].ap[:, :NT], start=(j == 0), stop=False)
                return PE.matmul(psy.ap[:, :NT], lhsT=cpi.ap[:, j, :], rhs=fin[1].ap[:, :NT], start=False, stop=(j == 3))
            S.op(pe, fny, reads=[cpr, cpi, fin[0], fin[1]], writes=[psy])
            if j == 3:
                ysb = scr.get()
                stt(ysb.ap[:, :NT], zc[c].ap[:, :NT], p["avec"].ap[:, 3, c:c + 1], psy.ap[:, :NT], ALU.mult, ALU.add,
                    [zc[c], p["avec"], psy], [ysb])
                P.put(psy)
                actf(zc[c].ap[:, :NT], ysb.ap[:, :NT], AF.Gelu_apprx_tanh, [ysb], [zc[c]])
                if c == 0:
                    dbg(tg + "gy0", zc[0].ap[:, :NT], [zc[0]])
        if last:
            for ri in range(2):
                pst = P.get()
                transpose(pst.ap[:16, 0:128], carS.ap[:, ri, :], 128, [carS], [pst])
                so = scr.get()
                evac(so.ap[:16, 0:128], pst.ap[:16, 0:128], [pst], [so])
                P.put(pst)
                S.dma((o_re_p if ri == 0 else o_im_p)[l], so.ap[:16, 0:128], reads=[so], st=so)

        if samp:
            for f in range(8):
                st_ = merge1_pe(f)
                merge1_rest(f, st_)
        for f in range(8):
            s3_, (vcl, vcg) = w_next(dve)
            pcl = P.get(); pcg = P.get()
            mm(pcl.ap[:, :NT], [(vcl[:, kt, :], zc[kt].ap[:, :NT]) for kt in range(4)], [s3_] + zc, [pcl])
            mm(pcg.ap[:, :NT], [(vcg[:, kt, :], zc[kt].ap[:, :NT]) for kt in range(4)], [s3_] + zc, [pcg])
            sG = scr.get()
            actf(sG.ap[:, :NT], pcg.ap[:, :NT], AF.Sigmoid, [pcg], [sG])
            tt_op(dve, sG.ap[:, :NT], pcl.ap[:, :NT], sG.ap[:, :NT], ALU.mult, [pcl, sG], [sG])
            P.put(pcl, pcg)
            tt_op(pool, sG.ap[:, :NT], sG.ap[:, :NT], sCt[f].ap[:, :NT], ALU.mult, [sG, sCt[f]], [sG])
            tt_op(dve, m[f].ap[:, :NT], m[f].ap[:, :NT], sG.ap[:, :NT], ALU.add, [m[f], sG], [m[f]])
        dbg(tg + "m0", m[0].ap[:, :NT], [m[0]])
        for c4 in range(4):
            slot, (vo,) = w_next(dve)
            for cc in range(2):
                f = 2 * c4 + cc
                pso = P.get()
                mm(pso.ap[:, :NT], [(vo[:, kt, cc * 128:(cc + 1) * 128], m[kt].ap[:, :NT]) for kt in range(8)], [slot] + m, [pso])
                tt_op(dve, x[f].ap[:, :NT], x[f].ap[:, :NT], pso.ap[:, :NT], ALU.add, [x[f], pso], [x[f]])
                P.put(pso)

        S.phase_barrier(ffn_tiles)
        dbg(tg + "xmix0", x[0].ap[:, :NT], [x[0]])
        rmsnorm(gains, lambda f: gains.ap[:, 1, f:f + 1], h, NT)
        carF = p["carF"]
        hist = pesb[0:3]
        hist_ap = arenaF[:, pesb_off:pesb_off + 44 * 32].rearrange("p (j b k) -> p j b k", j=44, k=2)
        if samp:
            rows = st_cf[l].rearrange("b k c -> (b k) c")
            S.dma(o_cf_s[l, :, 0, :], st_cf[l, :, 1, :], st=None)
            for j4 in range(11):
                rw = scr.get()
                S.dma(rw.ap[:32, :], rows[:, 512 * j4:512 * (j4 + 1)], writes=[rw], st=rw)
                pst = P.get()
                for jj in range(4):
                    transpose(pst.ap[:, 32 * jj:32 * jj + 32], rw.ap[:32, 128 * jj:128 * (jj + 1)], 32, [rw], [pst])
                evac(arenaF[:, pesb_off + 128 * j4: pesb_off + 128 * (j4 + 1)], pst.ap[:, 0:128], [pst], hist)
                P.put(pst)
        for j in range(NJ):
            slot, (vg_, vv_) = w_next(dve if j % 2 else act)
            outs = []
            for half, (vw, idx, ebuf) in enumerate([(vg_, j, eg[j % 2]), (vv_, NJ + j, ev[j % 2])]):
                psu = P.get()
                mm(psu.ap[:, :NT], [(vw[:, kt, :], h[kt].ap[:, :NT]) for kt in range(8)], [slot] + h, [psu])
                t0 = scr.get()
                actf(t0.ap[:, :NT], psu.ap[:, :NT], AF.Identity, [psu, cf], [t0],
                     scale=cf.ap[:, idx, 2:3], bias=cf.ap[:, idx, 3:4])
                if samp:
                    hv = hist_ap[:, idx, :, :]
                    stt(t0.ap[:, :NT], hv[:, :, 1], cf.ap[:, idx, 1:2], t0.ap[:, :NT], ALU.mult, ALU.add, hist + [cf, t0], [t0])
                    stt(t0.ap[:, :NT], hv[:, :, 0], cf.ap[:, idx, 0:1], t0.ap[:, :NT], ALU.mult, ALU.add, hist + [cf, t0], [t0])
                    rawc = ebuf
                    evac(rawc.ap[:, :NT], psu.ap[:, :NT], [psu], [rawc], eng=act)
                    P.put(psu)
                    upst = P.get()
                    transpose(upst.ap[:NS, 0:128], rawc.ap[:, :NS], 128, [rawc], [upst])
                    so2 = scr.get()
                    evac(so2.ap[:NS, 0:128], upst.ap[:NS, 0:128], [upst], [so2])
                    P.put(upst)
                    S.dma(o_cf_s[l, :, 1, 128 * idx:128 * (idx + 1)], so2.ap[:NS, 0:128], reads=[so2], st=so2)
                else:
                    evac(ebuf.ap[:, 2:2 + NT], psu.ap[:, :NT], [psu], [ebuf], eng=act)
                    P.put(psu)
                    if first:
                        memset(ebuf, ebuf.ap[:, 0:2])
                    else:
                        evac(ebuf.ap[:, 0:2], carF.ap[:, idx, :], [carF], [ebuf], eng=pool)
                    stt(t0.ap[:, :NT], ebuf.ap[:, 1:1 + NT], cf.ap[:, idx, 1:2], t0.ap[:, :NT], ALU.mult, ALU.add, [ebuf, cf, t0], [t0])
                    stt(t0.ap[:, :NT], ebuf.ap[:, 0:NT], cf.ap[:, idx, 0:1], t0.ap[:, :NT], ALU.mult, ALU.add, [ebuf, cf, t0], [t0])
                    evac(carF.ap[:, idx, :], ebuf.ap[:, NT:NT + 2], [ebuf], [carF], eng=pool)
                outs.append(t0)
            gl_ = scr.get()
            actf(gl_.ap[:, :NT], outs[0].ap[:, :NT], AF.Gelu_apprx_tanh, [outs[0]], [gl_])
            tt_op(dve, actt[j].ap[:, :NT], gl_.ap[:, :NT], outs[1].ap[:, :NT], ALU.mult, [gl_, outs[1]], [actt[j]])
        if last:
            pst = P.get()
            transpose(pst.ap[:88, 0:128], carF.ap.rearrange("p j k -> p (j k)"), 128, [carF], [pst])
            so = scr.get()
            evac(so.ap[:88, 0:128], pst.ap[:88, 0:128], [pst], [so])
            P.put(pst)
            S.dma(o_cf_p[l].rearrange("k (j c) -> j k c", c=128), so.ap[:88, 0:128], reads=[so], st=so)
        for fh in range(2):
            pacc = [P.get() for _ in range(4)]
            for j0 in range(0, NJ, 4):
                nj = min(4, NJ - j0)
                slot, (vd,) = w_next(dve)

                def fn():
                    ins = None
                    for jj in range(nj):
                        for fi in range(4):
                            ins = PE.matmul(pacc[fi].ap[:, :NT], lhsT=vd[:, jj, fi * 128:(fi + 1) * 128],
                                            rhs=actt[j0 + jj].ap[:, :NT], start=(j0 + jj == 0), stop=(j0 + jj == NJ - 1))
                    return ins
                S.op(pe, fn, reads=[slot] + actt[j0:j0 + nj], writes=pacc)
            for fi in range(4):
                f = 4 * fh + fi
                tt_op(dve, x[f].ap[:, :NT], x[f].ap[:, :NT], pacc[fi].ap[:, :NT], ALU.add, [x[f], pacc[fi]], [x[f]])
                P.put(pacc[fi])

        dbg(tg + "act0", actt[0].ap[:, :NT], [actt[0]])
        dbg(tg + "xffn0", x[0].ap[:, :NT], [x[0]])
        rmsnorm(gains, lambda f: gains.ap[:, 2, f:f + 1], h, NT)
        if samp:
            pstg = scr.get()
            S.dma(pstg.ap[:NS, 0:256], psm[l], writes=[pstg], st=pstg)
            pst = P.get()
            for kt in range(2):
                transpose(pst.ap[:, kt * 16:kt * 16 + 16], pstg.ap[:NS, kt * 128:(kt + 1) * 128], NS, [pstg], [pst])
            for kt in range(2):
                evac(pT[kt].ap[:, :NS], pst.ap[:, kt * 16:kt * 16 + 16], [pst], [pT[kt]])
            P.put(pst)
        else:
            pstg = [scr.get(), scr.get()]
            for r in range(4):
                tk = TT * ti + 128 * r
                S.dma(pstg[r // 2].ap[:, (r % 2) * 256:(r % 2) * 256 + 256], pp[l, tk:tk + 128, :], writes=[pstg[r // 2]], st=pstg[r // 2])
            for kt in range(2):
                pst = P.get()
                for r in range(4):
                    transpose(pst.ap[:, r * 128:(r + 1) * 128],
                              pstg[r // 2].ap[:, (r % 2) * 256 + kt * 128:(r % 2) * 256 + (kt + 1) * 128], 128, [pstg[r // 2]], [pst])
                evac(pT[kt].ap[:, :NT], pst.ap[:, :NT], [pst], [pT[kt]])
                P.put(pst)
        slot, (vpe,) = w_next(dve)
        for f in range(8):
            pse = P.get()
            mm(pse.ap[:, :NT], [(vpe[:, kt, f * 128:(f + 1) * 128], pT[kt].ap[:, :NT]) for kt in range(2)], [slot] + pT, [pse])
            evac(pesb[f].ap[:, :NT], pse.ap[:, :NT], [pse], [pesb[f]])
            P.put(pse)
        for c4 in range(4):
            slot, (vpg,) = w_next(dve)
            for cc in range(2):
                f = 2 * c4 + cc
                psg = P.get()
                mm(psg.ap[:, :NT], [(vpg[:, kt, cc * 128:(cc + 1) * 128], h[kt].ap[:, :NT]) for kt in range(8)], [slot] + h, [psg])
                sg = scr.get()
                actf(sg.ap[:, :NT], psg.ap[:, :NT], AF.Sigmoid, [psg], [sg])
                P.put(psg)
                tt_op(pool, sg.ap[:, :NT], sg.ap[:, :NT], pesb[f].ap[:, :NT], ALU.mult, [sg, pesb[f]], [sg])
                tt_op(dve, x[f].ap[:, :NT], x[f].ap[:, :NT], sg.ap[:, :NT], ALU.add, [x[f], sg], [x[f]])
        dbg(tg + "xple0", x[0].ap[:, :NT], [x[0]])

    for (kind, ti) in tiles_order:
        samp = (kind == "s")
        NT = NS if samp else TT
        R = 1 if samp else 4
        MT = NS if samp else 128
        S.phase_barrier(mixer_tiles)
        for r in range(R):
            xst = xstage[r % 2]
            if samp:
                S.dma(xst.ap[:NS, :], xs, writes=[xst], st=xst)
            else:
                S.dma(xst.ap[:, :], xp[TT * ti + 128 * r: TT * ti + 128 * (r + 1), :], writes=[xst], st=xst)
            for fg in range(2):
                pst = P.get()
                for k in range(4):
                    f = 4 * fg + k
                    transpose(pst.ap[:, k * 128:k * 128 + MT], xst.ap[:MT, f * 128:(f + 1) * 128], MT, [xst], [pst])
                evac(xbig[:, 4 * fg:4 * fg + 4, r * 128:r * 128 + MT],
                     pst.ap.rearrange("p (k t) -> p k t", k=4)[:, :, 0:MT], [pst], x[4 * fg:4 * fg + 4])
                P.put(pst)
        dbg("%s%d_x0" % (kind, ti), x[0].ap[:, :NT], [x[0]])
        for l in range(NL):
            if l > 0:
                S.phase_barrier(mixer_tiles)
            layer(l, kind, ti, NT)
        S.phase_barrier(fin_tiles)
        rmsnorm(gfin, lambda f: gfin.ap[:, f:f + 1], yT, NT)
        for r in range(R):
            ys = ystage[r % 2]
            for fg in range(2):
                pst = P.get()
                for k in range(4):
                    f = 4 * fg + k
                    transpose(pst.ap[:MT, k * 128:(k + 1) * 128], yT[f].ap[:, r * 128:r * 128 + MT], 128, [yT[f]], [pst])
                evac(ys.ap[:MT, fg * 512:(fg + 1) * 512], pst.ap[:MT, :], [pst], [ys])
                P.put(pst)
            if samp:
                S.dma(y_s, ys.ap[:NS, :], reads=[ys], st=ys)
            else:
                S.dma(y_p[TT * ti + 128 * r: TT * ti + 128 * (r + 1), :], ys.ap[:, :], reads=[ys], st=ys)
    assert wstate["next_use"] == len(gchunks), (wstate, len(gchunks))
    S.finish()
    print("SBUF bytes remaining:", nc.sbuf_bytes_remaining, "sems:", S.nsem,
          "ops:", {E.name: E.count for E in S.compute})


_NC_CACHE = {}


def pack_weights(g):
    nch = len(layer_chunks(0))
    out = np.zeros((L, nch, 128, SLOT), np.float32)
    for l in range(L):
        for ci, chunk in enumerate(layer_chunks(l)):
            o = 0
            for (wn, l_, r0, nkt, c0, nc_) in chunk:
                blk = g[wn][l, r0:r0 + 128 * nkt, c0:c0 + nc_].reshape(nkt, 128, nc_)
                out[l, ci, :, o:o + nkt * nc_] = blk.transpose(1, 0, 2).reshape(128, nkt * nc_)
                o += nkt * nc_
    return out


def kernel(**inp):
    f32 = np.float32
    g = {k: np.ascontiguousarray(np.asarray(v), dtype=f32) for k, v in inp.items()}
    if "nc" not in _NC_CACHE:
        _NC_CACHE["nc"] = build_nc()
    nc = _NC_CACHE["nc"]
    cst = np.zeros((128, 256), f32)
    cst[:, 0:128] = np.eye(128, dtype=f32)
    cst[:, 128:256] = np.tril(np.ones((128, 128), f32))
    cst_r = np.zeros((128, 384), f32)
    cst_r[:, 0:128] = 1.0 / 1024.0
    cst_r[:, 128:256] = 1.0 / 512.0
    cst_r[:, 256:384] = 1.0
    shared = {
        "cst": cst, "cst_r": cst_r, "b_s_r": g["b_s"],
    }
    shared["wpack"] = pack_weights(g)
    for k in ["g_mix", "g_ffn", "g_ple",
              "g_final", "conv_a_w", "conv_a_b", "ln_a_g", "ln_a_b", "ln_b_g", "ln_b_b", "w_s", "b_s", "ssm_a_re", "ssm_a_im",
              "ssm_log_dt", "ssm_b_re", "ssm_b_im", "ssm_c_re", "ssm_c_im", "ssm_d", "conv_f_w", "conv_f_b"]:
        shared[k] = g[k]
    shared["conv_f_b"] = np.ascontiguousarray(g["conv_f_b"].reshape(L, 1, 2 * DFF))
    in_maps = []
    for c in range(8):
        sl = slice(NS * c, NS * (c + 1))
        d = dict(shared)
        d["xp"] = g["x_prompt"][c]
        d["xs"] = np.ascontiguousarray(g["x_sample"][sl, 0, :])
        d["pp"] = np.ascontiguousarray(g["p_prompt"][:, c])
        d["psm"] = np.ascontiguousarray(g["p_sample"][:, sl, 0, :])
        d["st_ca"] = np.ascontiguousarray(g["state_conv_a"][:, sl])
        d["st_re"] = np.ascontiguousarray(g["state_ssm_re"][:, sl].reshape(L, NS, 2048))
        d["st_im"] = np.ascontiguousarray(g["state_ssm_im"][:, sl].reshape(L, NS, 2048))
        d["st_cf"] = np.ascontiguousarray(g["state_conv_ffn"][:, sl])
        in_maps.append(d)
    res = run_bass_kernel_spmd(nc, in_maps, core_ids=list(range(8)))
    rs = res.results

    def cat(name, axis, shape=None):
        a = np.concatenate([np.asarray(r[name], dtype=f32) for r in rs], axis=axis)
        return a

    y_prompt = np.stack([np.asarray(r["y_p"], f32) for r in rs], 0)
    y_sample = cat("y_s", 0).reshape(128, 1, D)
    conv_a_p = np.stack([np.asarray(r["o_ca_p"], f32) for r in rs], 1)
    conv_a_s = cat("o_ca_s", 1)
    chunk_v_p = np.stack([np.asarray(r["o_cv_p"], f32) for r in rs], 1)
    chunk_v_s = cat("o_cv_s", 1).reshape(L, 128, 1, DA)
    re_p = np.stack([np.asarray(r["o_re_p"], f32).reshape(L, 32, 64) for r in rs], 1)
    im_p = np.stack([np.asarray(r["o_im_p"], f32).reshape(L, 32, 64) for r in rs], 1)
    re_s = cat("o_re_s", 1).reshape(L, 128, 32, 64)
    im_s = cat("o_im_s", 1).reshape(L, 128, 32, 64)
    cf_p = np.stack([np.asarray(r["o_cf_p"], f32) for r in rs], 1)
    cf_s = cat("o_cf_s", 1)
    return (y_prompt, y_sample, conv_a_p, conv_a_s, chunk_v_p, chunk_v_s, re_p, im_p, re_s, im_s, cf_p, cf_s)
```

```python
import numpy as np
from contextlib import ExitStack
import concourse.bass as bass
import concourse.mybir as mybir
from concourse.bass_utils import run_bass_kernel_spmd

F32 = mybir.dt.float32
F32R = mybir.dt.float32r
BF = mybir.dt.bfloat16
ALU = mybir.AluOpType
AF = mybir.ActivationFunctionType

D = 1024
SEQ = 2048
NS = 16
L = 2
TT = 512
DA = 512
DFF = 2816
NJ = 22
EPS = 1e-6
O1 = 1024
O2 = 2048
O3 = 2560
SLOT = 2048
NSLOT = 3
NSTG = 4
NSTEP = 9
S5_POOL = False
S5_USE_DVE = False
DBG = None


class Res:
    __slots__ = ("lastw", "readers")

    def __init__(self):
        self.lastw = None
        self.readers = []


class Tile:
    def __init__(self, ap, psum=False):
        self.ap = ap
        self.res = [Res()]
        self.dsem = None
        self.psum = psum

    def __getitem__(self, k):
        return self.ap[k]


class Eng:
    def __init__(self, name, eng, sem, inorder=False):
        self.name = name
        self.eng = eng
        self.sem = sem
        self.count = 0
        self.waited = {}
        self.inorder = inorder


class Sched:
    def __init__(self, nc, es):
        self.nc = nc
        self.es = es
        self.nsem = 0
        self.dma_toks = {}
        self.all_toks = {}
        self.dpool = []
        self.dpi = 0
        self.pe = Eng("pe", nc.tensor, self.newsem("s_pe"), inorder=True)
        self.act = Eng("act", nc.scalar, self.newsem("s_act"))
        self.dve = Eng("dve", nc.vector, self.newsem("s_dve"))
        self.pool = Eng("pool", nc.gpsimd, self.newsem("s_pool"))
        self.sp = Eng("sp", nc.sync, None)
        self.compute = [self.pe, self.act, self.dve, self.pool]

    def newsem(self, name):
        self.nsem += 1
        return self.es.enter_context(self.nc.semaphore(name))

    def _need(self, E, tok, needs):
        if tok is None:
            return
        sem, val, src = tok
        if src is E and E.inorder:
            return
        key = id(sem)
        if E.waited.get(key, 0) >= val:
            return
        if needs.get(key, (None, 0))[1] < val:
            needs[key] = (sem, val)

    def _deps(self, E, reads, writes):
        needs = {}
        for t in reads:
            for r in t.res:
                self._need(E, r.lastw, needs)
                if t.psum:
                    for rd in r.readers:
                        if rd[2] is not E:
                            self._need(E, rd, needs)
        for t in writes:
            for r in t.res:
                self._need(E, r.lastw, needs)
                for rd in r.readers:
                    self._need(E, rd, needs)
        for key, (sem, val) in needs.items():
            E.eng.wait_ge(sem, val)
            E.waited[key] = val

    def _mark(self, tok, reads, writes):
        for t in writes:
            for r in t.res:
                r.lastw = tok
                r.readers = []
        for t in reads:
            for r in t.res:
                r.readers.append(tok)

    def op(self, E, fn, reads=(), writes=()):
        self._deps(E, reads, writes)
        ins = fn()
        E.count += 1
        ins.then_inc(E.sem, 1)
        self._mark((E.sem, E.count, E), reads, writes)

    def dma(self, out_ap, in_ap, reads=(), writes=(), st=None, E=None, track=True, group=False):
        E = E or self.sp
        if group and st is not None and st.dsem is not None:
            saved = []
            for t in writes:
                for r in t.res:
                    if r.lastw is not None and r.lastw[0] is st.dsem[0]:
                        saved.append((r, r.lastw))
                        r.lastw = None
            self._deps(E, reads, writes)
            for r, lw in saved:
                r.lastw = lw
        else:
            self._deps(E, reads, writes)
        if st is None:
            if not self.dpool:
                self.dpool = [Tile(None) for _ in range(8)]
                for t in self.dpool:
                    t.dsem = [self.newsem("dp%d" % self.nsem), 0]
            st = self.dpool[self.dpi % len(self.dpool)]
            self.dpi += 1
            if st.dsem[1] > 0 and E.waited.get(id(st.dsem[0]), 0) < st.dsem[1]:
                E.eng.wait_ge(st.dsem[0], st.dsem[1])
                E.waited[id(st.dsem[0])] = st.dsem[1]
        if st.dsem is None:
            st.dsem = [self.newsem("d%d" % self.nsem), 0]
        ins = E.eng.dma_start(out=out_ap, in_=in_ap)
        st.dsem[1] += 16
        ins.then_inc(st.dsem[0], 16)
        tok = (st.dsem[0], st.dsem[1], None)
        self.all_toks[id(st.dsem[0])] = tok
        if track:
            self.dma_toks[id(st.dsem[0])] = tok
        self._mark(tok, reads, writes)
        return tok

    def drain_pool(self, E):
        for t in self.dpool:
            sem, val = t.dsem
            if val > 0 and E.waited.get(id(sem), 0) < val:
                E.eng.wait_ge(sem, val)
                E.waited[id(sem)] = val

    def phase_barrier(self, tiles):
        toks = [(E.sem, E.count, None) for E in self.compute if E.count > 0]
        toks += list(self.dma_toks.values())
        for t in tiles:
            for r in t.res:
                r.lastw = None
                r.readers = list(toks)

    def finish(self):
        for E in self.compute:
            if E.count > 0 and self.sp.waited.get(id(E.sem), 0) < E.count:
                self.sp.eng.wait_ge(E.sem, E.count)
        for tok in self.all_toks.values():
            sem, val, _ = tok
            if self.sp.waited.get(id(sem), 0) < val:
                self.sp.eng.wait_ge(sem, val)
                self.sp.waited[id(sem)] = val


class Pool_:
    def __init__(self, tiles):
        self.free = list(tiles)

    def get(self):
        assert self.free, "pool exhausted"
        return self.free.pop(0)

    def put(self, *ts):
        for t in ts:
            self.free.append(t)


class Ring:
    def __init__(self, tiles):
        self.tiles = tiles
        self.i = 0

    def get(self):
        t = self.tiles[self.i % len(self.tiles)]
        self.i += 1
        return t


def layer_chunks(l):
    ch = []
    for c in range(4):
        ch.append([("w_in", l, 0, 8, 128 * c, 128), ("w_in", l, 0, 8, 512 + 128 * c, 128)])
    for c2 in range(2):
        ch.append([("w_in", l, 0, 8, O1 + 256 * c2, 256)])
    for kh in range(2):
        ch.append([("w_in", l, 512 * kh, 4, O1 + 512, 512)])
    for c2 in range(2):
        ch.append([("w_in", l, 0, 8, O2 + 256 * c2, 256)])
    for f in range(8):
        ch.append([("w_in", l, 0, 8, O3 + 128 * f, 128), ("w_in", l, 0, 8, O3 + 1024 + 128 * f, 128)])
        ch.append([("w_a_out", l, 0, 4, 128 * f, 128), ("w_b_out", l, 0, 4, 128 * f, 128)])
    for f in range(8):
        ch.append([("w_in", l, 0, 8, O3 + 2048 + 128 * f, 128), ("w_c_glu", l, 0, 4, 128 * f, 128),
                   ("w_c_glu", l, 0, 4, 1024 + 128 * f, 128)])
    for c4 in range(4):
        ch.append([("w_out", l, 0, 8, 256 * c4, 256)])
    for j in range(NJ):
        ch.append([("w_up", l, 0, 8, 128 * j, 128), ("w_up", l, 0, 8, DFF + 128 * j, 128)])
    for fh in range(2):
        for j0 in range(0, NJ, 4):
            nj = min(4, NJ - j0)
            ch.append([("w_down", l, 128 * j0, nj, 512 * fh, 512)])
    ch.append([("w_pe", l, 0, 2, 0, 1024)])
    for c4 in range(4):
        ch.append([("w_pg", l, 0, 8, 256 * c4, 256)])
    return ch


def build_nc():
    nc = bass.Bass("TRN2", target_bir_lowering=False)
    nc.dge_precook = False
    es = ExitStack()
    with es:
        _build(nc, es)
    return nc


def _build(nc, es):
    def din(name, shape, dt=F32):
        return nc.dram_tensor(name, list(shape), dt, kind="ExternalInput").ap()

    def dout(name, shape):
        return nc.dram_tensor(name, list(shape), F32, kind="ExternalOutput").ap()

    xp = din("xp", [SEQ, D]); xs = din("xs", [NS, D])
    pp = din("pp", [L, SEQ, 256]); psm = din("psm", [L, NS, 256])
    st_ca = din("st_ca", [L, NS, 30, DA]); st_re = din("st_re", [L, NS, 2048]); st_im = din("st_im", [L, NS, 2048])
    st_cf = din("st_cf", [L, NS, 2, 2 * DFF])
    NCH = len(layer_chunks(0))
    wpack = din("wpack", [L, NCH, 128, SLOT])
    g_mix = din("g_mix", [L, D]); g_ffn = din("g_ffn", [L, D]); g_ple = din("g_ple", [L, D]); g_final = din("g_final", [D])
    conv_a_w = din("conv_a_w", [L, 31, DA]); conv_a_b = din("conv_a_b", [L, DA])
    ln_a_g = din("ln_a_g", [L, DA]); ln_a_b = din("ln_a_b", [L, DA])
    ln_b_g = din("ln_b_g", [L, DA]); ln_b_b = din("ln_b_b", [L, DA])
    w_s = din("w_s", [L, 4, 128, 128]); b_s = din("b_s", [L, 4, 128]); b_s_r = din("b_s_r", [L, 4, 128])
    ssm_a_re = din("ssm_a_re", [L, 32, 64]); ssm_a_im = din("ssm_a_im", [L, 32, 64]); ssm_log_dt = din("ssm_log_dt", [L, 32])
    ssm_b_re = din("ssm_b_re", [L, 32, 64, 16]); ssm_b_im = din("ssm_b_im", [L, 32, 64, 16])
    ssm_c_re = din("ssm_c_re", [L, 32, 16, 64]); ssm_c_im = din("ssm_c_im", [L, 32, 16, 64])
    ssm_d = din("ssm_d", [L, DA])
    conv_f_w = din("conv_f_w", [L, 3, 2 * DFF]); conv_f_b = din("conv_f_b", [L, 1, 2 * DFF])
    cst = din("cst", [128, 256])
    cst_r = din("cst_r", [128, 384])

    y_p = dout("y_p", [SEQ, D]); y_s = dout("y_s", [NS, D])
    o_ca_p = dout("o_ca_p", [L, 30, DA]); o_ca_s = dout("o_ca_s", [L, NS, 30, DA])
    o_cv_p = dout("o_cv_p", [L, 128, DA]); o_cv_s = dout("o_cv_s", [L, NS, DA])
    o_re_p = dout("o_re_p", [L, 16, 128]); o_im_p = dout("o_im_p", [L, 16, 128])
    o_re_s = dout("o_re_s", [L, NS, 16, 128]); o_im_s = dout("o_im_s", [L, NS, 16, 128])
    o_cf_p = dout("o_cf_p", [L, 2, 2 * DFF]); o_cf_s = dout("o_cf_s", [L, NS, 2, 2 * DFF])

    S = Sched(nc, es)
    pe, act, dve, pool = S.pe, S.act, S.dve, S.pool
    V = nc.vector; A = nc.scalar; G = nc.gpsimd; PE = nc.tensor

    def sbt(name, shape, dt=F32):
        return es.enter_context(nc.sbuf_tensor(name, list(shape), dt)).ap()

    T = Tile

    cstt = T(sbt("cstt", [128, 256]))
    ident = cstt.ap[:, 0:128]
    tril = cstt.ap[:, 128:256]
    cstr = T(sbt("cstr", [128, 384], BF))
    ones_d = cstr.ap[:, 0:128]
    ones_c = cstr.ap[:, 128:256]
    ones_row = cstr.ap[0:1, 256:384]
    xbig = sbt("xT", [128, 8, TT]); x = [T(xbig[:, f, :]) for f in range(8)]
    hbig = sbt("hT", [128, 8, TT], BF); h = [T(hbig[:, f, :]) for f in range(8)]
    wstg_ap = sbt("wstg", [128, NSTG, SLOT])
    wstg = [T(wstg_ap[:, i, :]) for i in range(NSTG)]
    wring_ap = sbt("wring", [128, NSLOT, SLOT], BF)
    wslots = [T(wring_ap[:, i, :]) for i in range(NSLOT)]
    psb = [Tile(es.enter_context(nc.psum_tensor("ps%d" % i, [128, 512], F32)).ap(), psum=True) for i in range(8)]
    P = Pool_(psb)
    NSCR = 8
    scr_ap = sbt("scr", [128, NSCR, TT])
    scr = Ring([T(scr_ap[:, i, :]) for i in range(NSCR)])
    scr_r_ap = sbt("scrr", [128, 3, TT], BF)
    scr_r = Ring([T(scr_r_ap[:, i, :]) for i in range(3)])
    sm_ap = sbt("small", [128, 16, 8])
    small = Ring([T(sm_ap[:, i, :]) for i in range(16)])

    par = []
    for l in range(L):
        p = {}
        p["gains"] = T(sbt("gains%d" % l, [128, 3, 8]))
        p["caw"] = T(sbt("caw%d" % l, [128, 4, 31]))
        p["avec"] = T(sbt("avec%d" % l, [128, 4, 4]))
        p["lnb"] = T(sbt("lnb%d" % l, [128, 2, DA]))
        p["WsT"] = T(sbt("WsT%d" % l, [128, 4, 128], BF))
        p["bs1"] = T(sbt("bs1%d" % l, [1, 4, 128], BF))
        p["bs0"] = T(sbt("bs0%d" % l, [128, 4]))
        p["w00"] = T(sbt("w00%d" % l, [128, 4]))
        p["Wsm"] = T(sbt("Wsm%d" % l, [16, 4, 16], BF))
        p["BbRe"] = T(sbt("BbRe%d" % l, [128, 4, 128], BF))
        p["BbIm"] = T(sbt("BbIm%d" % l, [128, 4, 128], BF))
        p["CTre"] = T(sbt("CTre%d" % l, [128, 4, 128]))
        p["CTim"] = T(sbt("CTim%d" % l, [128, 4, 128]))
        p["HSC"] = T(sbt("HSC%d" % l, [128, 1, 3, 16]))
        p["UPH"] = T(sbt("UPH%d" % l, [128, 3, 16]))
        p["RHO"] = T(sbt("RHO%d" % l, [128, 16]))
        p["cf"] = T(sbt("cf%d" % l, [128, 44, 4]))
        p["carA"] = T(sbt("carA%d" % l, [128, 4, 30], BF))
        p["carF"] = T(sbt("carF%d" % l, [128, 44, 2]))
        p["carS"] = T(sbt("carS%d" % l, [128, 2, 16]))
        par.append(p)
    gfin = T(sbt("gfin", [128, 8]))
    dg_ap = sbt("dg", [128, 2, 31, 128], BF)
    dg = [T(dg_ap[:, i, :, :]) for i in range(2)]
    alast = T(sbt("alast", [128, 4, 32]))
    tabring_ap = sbt("tabring", [128, 2, 1024])
    tabring = [T(tabring_ap[:, i, :]) for i in range(2)]
    tab_d = nc.dram_tensor("tab_d", [L, 16, 128, 1024], F32, kind="Internal").ap()
    Cpad_re_ap = sbt("Cpad_re", [128, 2, 4, 128], BF); Cpad_re = [T(Cpad_re_ap[:, i, :, :]) for i in range(2)]
    Cpad_im_ap = sbt("Cpad_im", [128, 2, 4, 128], BF); Cpad_im = [T(Cpad_im_ap[:, i, :, :]) for i in range(2)]

    ARR = 16512
    ARF = 6272
    arenaR = sbt("arenaR", [128, ARR], BF)
    arenaF = sbt("arenaF", [128, ARF])
    offR = [0]; offF = [0]

    def cR(n):
        a = arenaR[:, offR[0]:offR[0] + n]; offR[0] += n
        assert offR[0] <= ARR, offR[0]
        return a

    def cF(n):
        a = arenaF[:, offF[0]:offF[0] + n]; offF[0] += n
        assert offF[0] <= ARF, offF[0]
        return a

    m = [T(cR(TT)) for f in range(8)]
    acs = [T(cR(TT)) for c in range(4)]
    ub = [T(cR(TT)) for c in range(4)]
    vtok = [T(cR(DA)) for r in range(4)]
    zc = [T(cR(TT)) for c in range(4)]
    hsF2 = [[T(cR(TT)) for _ in range(2)] for _ in range(2)]
    hsF = hsF2[0]
    aext = [T(cR(544)) for c in range(4)]
    hsAB = [T(cF(TT)) for _ in range(4)]
    hsCD = [T(cF(TT)) for _ in range(4)]
    hsA = hsAB[0:2]; hsB = hsAB[2:4]
    xstage = [T(cF(D)) for _ in range(2)]
    mixer_tiles = m + acs + ub + vtok + zc + hsF2[0] + hsF2[1] + aext + hsAB + hsCD + xstage
    offR[0] = 0; offF[0] = 0
    actt = [T(cR(TT)) for j in range(NJ)]
    pT = [T(cR(TT)) for _ in range(2)]
    eg = [T(cF(520)) for _ in range(2)]
    ev = [T(cF(520)) for _ in range(2)]
    pesb_off = offF[0]
    pesb = [T(cF(TT)) for f in range(8)]
    ffn_tiles = actt + pT + eg + ev + pesb
    offF[0] = 0
    yT = [T(cF(TT)) for f in range(8)]
    ystage = [T(cF(D)) for _ in range(2)]
    fin_tiles = yT + ystage
    offF[0] = 0
    prep_ws = T(cF(512)); prep_x = [T(cF(512)) for _ in range(2)]
    prep_cst = [T(cF(512)) for _ in range(2)]
    pv = {}
    for nm in ["are", "aim", "ldt", "dt", "zr", "th", "p", "er", "c", "s", "t1", "t2", "abr", "abi", "pp", "den", "cfr", "cfi", "ncfi"]:
        pv[nm] = T(cF(16))
    pB = [T(cF(256)) for _ in range(2)]
    pBb = [T(cF(256)) for _ in range(2)]
    prep_stg = T(cF(1024))
    uph = T(cF(NSTEP * 3 * 16))
    prep_tiles = [prep_ws] + prep_x + prep_cst + list(pv.values()) + pB + pBb + [prep_stg, uph]

    def mm(ps_ap, pairs, reads, writes, tp=None):
        def fn():
            n = len(pairs)
            ins = None
            for i, (lt, rh) in enumerate(pairs):
                kw = {}
                if tp is not None:
                    kw["tile_position"] = tp
                ins = PE.matmul(ps_ap, lhsT=lt, rhs=rh, start=(i == 0), stop=(i == n - 1), **kw)
            return ins
        S.op(pe, fn, reads=reads, writes=writes)

    cp_flip = [0]

    def evac(out_ap, in_ap, reads, writes, eng=None):
        if eng is None:
            cp_flip[0] ^= 1
            eng = act if cp_flip[0] else dve
        if eng is act:
            S.op(act, lambda: A.copy(out=out_ap, in_=in_ap), reads=reads, writes=writes)
        elif eng is dve:
            S.op(dve, lambda: V.tensor_copy(out=out_ap, in_=in_ap), reads=reads, writes=writes)
        else:
            S.op(pool, lambda: G.tensor_copy(out=out_ap, in_=in_ap), reads=reads, writes=writes)

    def tt_op(E, out_ap, a, b, op, reads, writes):
        e = V if E is dve else G
        S.op(E, lambda: e.tensor_tensor(out=out_ap, in0=a, in1=b, op=op), reads=reads, writes=writes)

    def ts_op(E, out_ap, a, s1, s2, op0, op1, reads, writes):
        e = V if E is dve else G
        if op1 is None:
            S.op(E, lambda: e.tensor_scalar(out=out_ap, in0=a, scalar1=s1, scalar2=None, op0=op0), reads=reads, writes=writes)
        else:
            S.op(E, lambda: e.tensor_scalar(out=out_ap, in0=a, scalar1=s1, scalar2=s2, op0=op0, op1=op1), reads=reads, writes=writes)

    def stt(out_ap, a, s, b, op0, op1, reads, writes):
        S.op(dve, lambda: V.scalar_tensor_tensor(out=out_ap, in0=a, scalar=s, in1=b, op0=op0, op1=op1), reads=reads, writes=writes)

    def actf(out_ap, in_ap, func, reads, writes, bias=None, scale=None, accum=None):
        kw = {}
        if bias is not None:
            kw["bias"] = bias
        if scale is not None:
            kw["scale"] = scale
        if accum is not None:
            kw["accum_out"] = accum
        S.op(act, lambda: A.activation(out=out_ap, in_=in_ap, func=func, **kw), reads=reads, writes=writes)

    def transpose(ps_ap, in_ap, npart, reads, writes):
        S.op(pe, lambda: PE.transpose(ps_ap, in_ap, ident[:npart, :npart]), reads=list(reads) + [cstt], writes=writes)

    def memset(t, ap, val=0.0):
        S.op(pool, lambda: G.memset(ap, val), reads=[], writes=[t])

    tiles_order = [("p", i) for i in range(4)] + [("s", 0)]
    NL = L
    if DBG is not None:
        tiles_order = DBG["tiles"]
        NL = DBG["layers"]
    gchunks = []
    gcidx = []
    for _t in tiles_order:
        for l in range(NL):
            lc = layer_chunks(l)
            gchunks += lc
            gcidx += [(l, i) for i in range(len(lc))]

    def dbg(name, ap, reads):
        if DBG is None:
            return
        shp = list(ap.shape)
        d = nc.dram_tensor("dbg_" + name, shp, ap.dtype, kind="ExternalOutput").ap()
        S.dma(d, ap, reads=reads, st=None)
    wstate = {"next_load": 0, "next_use": 0, "next_cast": 0}

    def w_load(idx):
        chunk = gchunks[idx]
        stg = wstg[idx % NSTG]
        o = sum(nkt * nc_ for (wn, l, r0, nkt, c0, nc_) in chunk)
        assert o <= SLOT
        l, ci = gcidx[idx]
        S.dma(stg.ap[:, 0:o], wpack[l, ci, :, 0:o], writes=[stg], st=stg, track=False)
        return o

    wsize = {}

    def w_cast(idx, eng):
        n = wsize[idx]
        stg = wstg[idx % NSTG]; slot = wslots[idx % NSLOT]
        evac(slot.ap[:, 0:n], stg.ap[:, 0:n], [stg], [slot], eng=eng)

    def w_next(cast_eng=None):
        cast_eng = cast_eng or act
        idx = wstate["next_use"]
        n = len(gchunks)
        if idx == 0:
            for k in range(min(NSTG, n)):
                wsize[k] = w_load(k)
            wstate["next_load"] = min(NSTG, n)
            w_cast(0, cast_eng)
            wstate["next_cast"] = 1
        if wstate["next_cast"] <= idx + 1 and wstate["next_cast"] < n:
            k = wstate["next_cast"]
            w_cast(k, cast_eng)
            wstate["next_cast"] = k + 1
        while wstate["next_load"] < n and wstate["next_load"] - NSTG < wstate["next_cast"]:
            k = wstate["next_load"]
            wsize[k] = w_load(k)
            wstate["next_load"] += 1
        wstate["next_use"] += 1
        slot = wslots[idx % NSLOT]
        views = []
        o = 0
        for (wn, l, r0, nkt, c0, nc_) in gchunks[idx]:
            views.append(slot.ap[:, o:o + nkt * nc_].rearrange("p (k c) -> p k c", k=nkt))
            o += nkt * nc_
        return slot, views

    S.dma(cstt.ap, cst, writes=[cstt], st=None, track=False)
    cst_stg = scr.get()
    S.dma(cst_stg.ap[:, 0:384], cst_r, writes=[cst_stg], st=None, track=False)
    evac(cstr.ap, cst_stg.ap[:, 0:384], [cst_stg], [cstr], eng=dve)
    nonc = nc.allow_non_contiguous_dma(reason="small param loads")
    nonc.__enter__()

    def pdma(dst_tile, dst_ap, src_ap):
        S.dma(dst_ap, src_ap, writes=[dst_tile], st=None, track=False)

    def featvec(dst_tile, dst_ap, src_vec):
        pdma(dst_tile, dst_ap, src_vec.rearrange("(f p) -> p f", p=128))

    def v16(nm):
        return pv[nm].ap

    for l in range(L):
        p = par[l]
        featvec(p["gains"], p["gains"].ap[:, 0, :], g_mix[l])
        featvec(p["gains"], p["gains"].ap[:, 1, :], g_ffn[l])
        featvec(p["gains"], p["gains"].ap[:, 2, :], g_ple[l])
        for i, v_ in enumerate([conv_a_b, ln_a_g, ln_a_b, ssm_d]):
            featvec(p["avec"], p["avec"].ap[:, i, :], v_[l])
        pdma(p["lnb"], p["lnb"].ap[:, 0, :], ln_b_g[l].partition_broadcast(128))
        pdma(p["lnb"], p["lnb"].ap[:, 1, :], ln_b_b[l].partition_broadcast(128))
        bstg = small.get()
        bs_stg = scr.get()
        pdma(bs_stg, bs_stg.ap[0:1, 0:512], b_s_r[l:l + 1, :, :].rearrange("o g i -> o (g i)"))
        evac(p["bs1"].ap[0:1, :, :].rearrange("o g i -> o (g i)"), bs_stg.ap[0:1, 0:512], [bs_stg], [p["bs1"]], eng=dve)
        pdma(p["bs0"], p["bs0"].ap, b_s[l, :, 0].partition_broadcast(128))
        pdma(p["w00"], p["w00"].ap, w_s[l, :, 0, 0].partition_broadcast(128))
        pdma(prep_stg, prep_stg.ap[:31, 0:512], conv_a_w[l])
        pst = P.get()
        for c in range(4):
            transpose(pst.ap[:, 32 * c:32 * c + 31], prep_stg.ap[:31, 128 * c:128 * (c + 1)], 31, [prep_stg], [pst])
        evac(p["caw"].ap, pst.ap[:, 0:128].rearrange("p (c k) -> p c k", k=32)[:, :, 0:31], [pst], [p["caw"]])
        P.put(pst)
        for j4 in range(11):
            stg = scr.get()
            pdma(stg, stg.ap[0:3, :], conv_f_w[l][:, 512 * j4:512 * (j4 + 1)])
            pdma(stg, stg.ap[3:4, :], conv_f_b[l][:, 512 * j4:512 * (j4 + 1)])
            pst = P.get()
            for jj in range(4):
                transpose(pst.ap[:, 4 * jj:4 * jj + 4], stg.ap[0:4, 128 * jj:128 * (jj + 1)], 4, [stg], [pst])
            evac(p["cf"].ap[:, 4 * j4:4 * j4 + 4, :], pst.ap[:, 0:16].rearrange("p (j k) -> p j k", k=4), [pst], [p["cf"]])
            P.put(pst)
        pdma(prep_ws, prep_ws.ap.rearrange("p (g j) -> p g j", g=4), w_s[l].rearrange("g i j -> i g j"))
        for g in range(4):
            tt_op(dve, prep_ws.ap[:, g * 128:(g + 1) * 128], prep_ws.ap[:, g * 128:(g + 1) * 128], tril, ALU.mult,
                  reads=[prep_ws, cstt], writes=[prep_ws])
        pst = P.get()
        for g in range(4):
            transpose(pst.ap[:, g * 128:(g + 1) * 128], prep_ws.ap[:, g * 128:(g + 1) * 128], 128, [prep_ws], [pst])
        evac(p["WsT"].ap.rearrange("p g i -> p (g i)"), pst.ap, [pst], [p["WsT"]])
        P.put(pst)
        for g in range(4):
            ts_op(dve, p["Wsm"].ap[:, g, :], ident[:16, :16], p["w00"].ap[:16, g:g + 1], None, ALU.mult, None,
                  reads=[cstt, p["w00"]], writes=[p["Wsm"]])
        for gl in range(2):
            sl = slice(64 * gl, 64 * gl + 64)
            pdma(pv["are"], v16("are")[sl, :], ssm_a_re[l].rearrange("(q g) n -> g n q", g=2)[gl])
            pdma(pv["aim"], v16("aim")[sl, :], ssm_a_im[l].rearrange("(q g) n -> g n q", g=2)[gl])
            pdma(pv["ldt"], v16("ldt")[sl, :], ssm_log_dt[l].rearrange("(q g) -> g q", g=2)[gl].partition_broadcast(64))
            for ri, src in enumerate([ssm_b_re, ssm_b_im]):
                pdma(pB[ri], pB[ri].ap[sl, :].rearrange("p (q h) -> p q h", q=16),
                     src[l].rearrange("(q g) n h -> g n q h", g=2)[gl])
        actf(v16("dt"), v16("ldt"), AF.Exp, [pv["ldt"]], [pv["dt"]])
        tt_op(dve, v16("zr"), v16("are"), v16("dt"), ALU.mult, [pv["are"], pv["dt"]], [pv["zr"]])
        tt_op(dve, v16("th"), v16("aim"), v16("dt"), ALU.mult, [pv["aim"], pv["dt"]], [pv["th"]])
        ts_op(dve, v16("p"), v16("zr"), 1.0 / 6.0, 1.0, ALU.mult, ALU.add, [pv["zr"]], [pv["p"]])
        for k in [5.0, 4.0, 3.0, 2.0]:
            tt_op(dve, v16("p"), v16("p"), v16("zr"), ALU.mult, [pv["p"], pv["zr"]], [pv["p"]])
            ts_op(dve, v16("p"), v16("p"), 1.0 / k, 1.0, ALU.mult, ALU.add, [pv["p"]], [pv["p"]])
        tt_op(dve, v16("p"), v16("p"), v16("zr"), ALU.mult, [pv["p"], pv["zr"]], [pv["p"]])
        ts_op(dve, v16("er"), v16("p"), 1.0, None, ALU.add, None, [pv["p"]], [pv["er"]])
        actf(v16("s"), v16("th"), AF.Sin, [pv["th"]], [pv["s"]], scale=1.0 / 16.0)
        ts_op(dve, v16("t1"), v16("th"), 1.0 / 16.0, float(np.pi / 2), ALU.mult, ALU.add, [pv["th"]], [pv["t1"]])
        actf(v16("c"), v16("t1"), AF.Sin, [pv["t1"]], [pv["c"]])
        for _ in range(4):
            tt_op(dve, v16("t1"), v16("c"), v16("c"), ALU.mult, [pv["c"]], [pv["t1"]])
            tt_op(dve, v16("t2"), v16("s"), v16("s"), ALU.mult, [pv["s"]], [pv["t2"]])
            tt_op(dve, v16("s"), v16("s"), v16("c"), ALU.mult, [pv["s"], pv["c"]], [pv["s"]])
            ts_op(dve, v16("s"), v16("s"), 2.0, None, ALU.mult, None, [pv["s"]], [pv["s"]])
            tt_op(dve, v16("c"), v16("t1"), v16("t2"), ALU.subtract, [pv["t1"], pv["t2"]], [pv["c"]])
        tt_op(dve, v16("abr"), v16("er"), v16("c"), ALU.mult, [pv["er"], pv["c"]], [pv["abr"]])
        tt_op(dve, v16("abi"), v16("er"), v16("s"), ALU.mult, [pv["er"], pv["s"]], [pv["abi"]])
        H = p["HSC"]
        evac(H.ap[:, 0, 0, :], v16("abr"), [pv["abr"]], [H], eng=dve)
        evac(H.ap[:, 0, 1, :], v16("abi"), [pv["abi"]], [H], eng=dve)
        ts_op(dve, H.ap[:, 0, 2, :], v16("abi"), -1.0, None, ALU.mult, None, [pv["abi"]], [H])
        evac(p["RHO"].ap, v16("er"), [pv["er"]], [p["RHO"]], eng=dve)
        Uv = uph.ap.rearrange("p (k c q) -> p k c q", k=NSTEP, c=3)
        evac(Uv[:, 0, 0, :], v16("c"), [pv["c"]], [uph], eng=dve)
        evac(Uv[:, 0, 1, :], v16("s"), [pv["s"]], [uph], eng=dve)
        for k in range(1, NSTEP):
            tt_op(dve, v16("t1"), Uv[:, k - 1, 0, :], Uv[:, k - 1, 0, :], ALU.mult, [uph], [pv["t1"]])
            tt_op(dve, v16("t2"), Uv[:, k - 1, 1, :], Uv[:, k - 1, 1, :], ALU.mult, [uph], [pv["t2"]])
            tt_op(dve, Uv[:, k, 0, :], v16("t1"), v16("t2"), ALU.subtract, [pv["t1"], pv["t2"]], [uph])
            tt_op(dve, v16("t1"), Uv[:, k - 1, 0, :], Uv[:, k - 1, 1, :], ALU.mult, [uph], [pv["t1"]])
            ts_op(dve, Uv[:, k, 1, :], v16("t1"), 2.0, None, ALU.mult, None, [pv["t1"]], [uph])
        for k in range(NSTEP):
            ts_op(dve, Uv[:, k, 2, :], Uv[:, k, 1, :], -1.0, None, ALU.mult, None, [uph], [uph])
        evac(p["UPH"].ap, Uv[:, 0, :, :], [uph], [p["UPH"]], eng=dve)
        Cr = prep_x[0].ap.rearrange("p (q r) -> p q r", q=16); Sr = prep_x[1].ap.rearrange("p (q r) -> p q r", q=16)
        Bc = pBb[0].ap.rearrange("p (q r) -> p q r", q=16); Bs = pBb[1].ap.rearrange("p (q r) -> p q r", q=16)
        T1 = prep_cst[0].ap.rearrange("p (q r) -> p q r", q=16); T2 = prep_cst[1].ap.rearrange("p (q r) -> p q r", q=16)
        tset = [prep_x[0], prep_x[1], pBb[0], pBb[1], prep_cst[0], prep_cst[1], uph]

        def dbl(Ctab, Stab, k0, nsteps):
            S.op(dve, lambda: V.memset(Ctab[:, :, 0:1], 1.0), reads=[], writes=tset)
            S.op(dve, lambda: V.memset(Stab[:, :, 0:1], 0.0), reads=[], writes=tset)
            for i in range(nsteps):
                d = 1 << i
                ckb = Uv[:, k0 + i, 0, :].unsqueeze(2).to_broadcast([128, 16, d])
                skb = Uv[:, k0 + i, 1, :].unsqueeze(2).to_broadcast([128, 16, d])
                tt_op(dve, T1[:, :, 0:d], Ctab[:, :, 0:d], ckb, ALU.mult, tset, tset)
                tt_op(dve, T2[:, :, 0:d], Stab[:, :, 0:d], skb, ALU.mult, tset, tset)
                tt_op(dve, Ctab[:, :, d:2 * d], T1[:, :, 0:d], T2[:, :, 0:d], ALU.subtract, tset, tset)
                tt_op(dve, T1[:, :, 0:d], Stab[:, :, 0:d], ckb, ALU.mult, tset, tset)
                tt_op(dve, T2[:, :, 0:d], Ctab[:, :, 0:d], skb, ALU.mult, tset, tset)
                tt_op(dve, Stab[:, :, d:2 * d], T1[:, :, 0:d], T2[:, :, 0:d], ALU.add, tset, tset)
        dbl(Cr, Sr, 0, 5)
        dbl(Bc[:, :, 0:16], Bs[:, :, 0:16], 5, 4)
        for q in range(16):
            def bm(tab):
                return tab[:, q, 0:16].unsqueeze(2).to_broadcast([128, 16, 32])

            def br(tab):
                return tab[:, q, :].unsqueeze(1).to_broadcast([128, 16, 32])
            c1 = scr.get(); c2 = scr.get(); s1_ = scr.get(); s2_ = scr.get()

            def v3(t):
                return t.ap.rearrange("p (m r) -> p m r", m=16)
            tt_op(dve, v3(c1), bm(Bc), br(Cr), ALU.mult, tset, [c1])
            tt_op(dve, v3(c2), bm(Bs), br(Sr), ALU.mult, tset, [c2])
            tt_op(dve, c1.ap, c1.ap, c2.ap, ALU.subtract, [c1, c2], [c1])
            tt_op(pool, v3(s1_), bm(Bs), br(Cr), ALU.mult, tset, [s1_])
            tt_op(pool, v3(s2_), bm(Bc), br(Sr), ALU.mult, tset, [s2_])
            tt_op(pool, s1_.ap, s1_.ap, s2_.ap, ALU.add, [s1_, s2_], [s1_])
            S.dma(tab_d[l, q, :, 0:512], c1.ap, reads=[c1], st=None, track=False)
            S.dma(tab_d[l, q, :, 512:1024], s1_.ap, reads=[s1_], st=None, track=False)
        ts_op(dve, v16("pp"), v16("abr"), -1.0, None, ALU.add, None, [pv["abr"]], [pv["pp"]])
        tt_op(dve, v16("t1"), v16("are"), v16("are"), ALU.mult, [pv["are"]], [pv["t1"]])
        tt_op(dve, v16("t2"), v16("aim"), v16("aim"), ALU.mult, [pv["aim"]], [pv["t2"]])
        tt_op(dve, v16("den"), v16("t1"), v16("t2"), ALU.add, [pv["t1"], pv["t2"]], [pv["den"]])
        S.op(dve, lambda: V.reciprocal(out=v16("den"), in_=v16("den")), reads=[pv["den"]], writes=[pv["den"]])
        tt_op(dve, v16("t1"), v16("pp"), v16("are"), ALU.mult, [pv["pp"], pv["are"]], [pv["t1"]])
        tt_op(dve, v16("t2"), v16("abi"), v16("aim"), ALU.mult, [pv["abi"], pv["aim"]], [pv["t2"]])
        tt_op(dve, v16("t1"), v16("t1"), v16("t2"), ALU.add, [pv["t1"], pv["t2"]], [pv["t1"]])
        tt_op(dve, v16("cfr"), v16("t1"), v16("den"), ALU.mult, [pv["t1"], pv["den"]], [pv["cfr"]])
        tt_op(dve, v16("t1"), v16("abi"), v16("are"), ALU.mult, [pv["abi"], pv["are"]], [pv["t1"]])
        tt_op(dve, v16("t2"), v16("pp"), v16("aim"), ALU.mult, [pv["pp"], pv["aim"]], [pv["t2"]])
        tt_op(dve, v16("t1"), v16("t1"), v16("t2"), ALU.subtract, [pv["t1"], pv["t2"]], [pv["t1"]])
        tt_op(dve, v16("cfi"), v16("t1"), v16("den"), ALU.mult, [pv["t1"], pv["den"]], [pv["cfi"]])
        ts_op(dve, v16("ncfi"), v16("cfi"), -1.0, None, ALU.mult, None, [pv["cfi"]], [pv["ncfi"]])
        for q in range(16):
            bq = slice(16 * q, 16 * q + 16)
            ts_op(dve, pBb[0].ap[:, bq], pB[0].ap[:, bq], v16("cfr")[:, q:q + 1], None, ALU.mult, None,
                  [pB[0], pv["cfr"]], [pBb[0]])
            stt(pBb[0].ap[:, bq], pB[1].ap[:, bq], v16("ncfi")[:, q:q + 1], pBb[0].ap[:, bq], ALU.mult, ALU.add,
                [pB[1], pv["ncfi"], pBb[0]], [pBb[0]])
            ts_op(dve, pBb[1].ap[:, bq], pB[1].ap[:, bq], v16("cfr")[:, q:q + 1], None, ALU.mult, None,
                  [pB[1], pv["cfr"]], [pBb[1]])
            stt(pBb[1].ap[:, bq], pB[0].ap[:, bq], v16("cfi")[:, q:q + 1], pBb[1].ap[:, bq], ALU.mult, ALU.add,
                [pB[0], pv["cfi"], pBb[1]], [pBb[1]])
        for ri in range(2):
            X = prep_x[ri]
            memset(X, X.ap)
            Xv = X.ap.rearrange("p (q g h) -> p q g h", q=16, g=2)
            Bv = pBb[ri].ap.rearrange("p (q h) -> p q h", q=16)
            evac(Xv[0:64, :, 0, :], Bv[0:64, :, :], [pBb[ri]], [X], eng=dve)
            evac(Xv[64:128, :, 1, :], Bv[64:128, :, :], [pBb[ri]], [X], eng=dve)
            pst = P.get()
            for c in range(4):
                transpose(pst.ap[:, c * 128:(c + 1) * 128], X.ap[:, c * 128:(c + 1) * 128], 128, [X], [pst])
            dstT = p["BbRe"] if ri == 0 else p["BbIm"]
            evac(dstT.ap.rearrange("p c m -> p (c m)"), pst.ap, [pst], [dstT])
            P.put(pst)
        for ri, src in enumerate([ssm_c_re, ssm_c_im]):
            Cs = prep_cst[ri]
            memset(Cs, Cs.ap)
            Cv = Cs.ap.rearrange("p (c m) -> p c m", c=4)
            for c in range(4):
                for gi in range(8):
                    S.dma(Cv[16 * gi:16 * gi + 16, c, 64 * (gi % 2):64 * (gi % 2) + 64], src[l, 8 * c + gi],
                          writes=[Cs], st=Cs, track=False, group=True)
            pst = P.get()
            for c in range(4):
                transpose(pst.ap[:, c * 128:(c + 1) * 128], Cs.ap[:, c * 128:(c + 1) * 128], 128, [Cs], [pst])
            if ri == 0:
                evac(p["CTre"].ap.rearrange("p c m -> p (c m)"), pst.ap, [pst], [p["CTre"]], eng=dve)
            else:
                ts_op(dve, p["CTim"].ap.rearrange("p c m -> p (c m)"), pst.ap, -1.0, None, ALU.mult, None, [pst], [p["CTim"]])
            P.put(pst)
    featvec(gfin, gfin.ap, g_final)
    nonc.__exit__(None, None, None)
    for i in range(2):
        memset(Cpad_re[i], Cpad_re[i].ap)
        memset(Cpad_im[i], Cpad_im[i].ap)

    def load_cpad(l, c):
        i = c % 2
        for j in range(4):
            evac(Cpad_re[i].ap[:, j, 32 * j:32 * j + 32], par[l]["CTre"].ap[:, c, 32 * j:32 * j + 32],
                 [par[l]["CTre"]], [Cpad_re[i]], eng=pool)
            evac(Cpad_im[i].ap[:, j, 32 * j:32 * j + 32], par[l]["CTim"].ap[:, c, 32 * j:32 * j + 32],
                 [par[l]["CTim"]], [Cpad_im[i]], eng=pool)

    def S5_ENG():
        return dve if S5_USE_DVE else pool

    tabstate = {"n": 0, "drained": False}

    def tab_load(l, q):
        if not tabstate["drained"]:
            S.drain_pool(S.sp)
            tabstate["drained"] = True
        slot = tabring[tabstate["n"] % 2]
        tabstate["n"] += 1
        S.dma(slot.ap, tab_d[l, q], writes=[slot], st=slot, track=False)
        return slot

    def rmsnorm(gcol_tile, gcol, dst, NT):
        pss = P.get()
        for f in range(8):
            sq = scr_r.get()
            actf(sq.ap[:, :NT], x[f].ap[:, :NT], AF.Square, [x[f]], [sq])
            S.op(pe, lambda: PE.matmul(pss.ap[:, :NT], lhsT=ones_d, rhs=sq.ap[:, :NT], start=(f == 0), stop=(f == 7)),
                 reads=[cstr, sq], writes=[pss])
        sd = scr.get()
        actf(sd.ap[:, :NT], pss.ap[:, :NT], AF.Sqrt, [pss], [sd], bias=EPS)
        P.put(pss)
        S.op(dve, lambda: V.reciprocal(out=sd.ap[:, :NT], in_=sd.ap[:, :NT]), reads=[sd], writes=[sd])
        for f in range(8):
            stt(dst[f].ap[:, :NT], x[f].ap[:, :NT], gcol(f), sd.ap[:, :NT], ALU.mult, ALU.mult,
                [x[f], gcol_tile, sd], [dst[f]])

    def layer(l, kind, ti, NT):
        p = par[l]
        samp = (kind == "s")
        first = (kind == "p" and ti == 0)
        last = (kind == "p" and ti == 3)
        R = 1 if samp else 4
        MT = NS if samp else 128
        gains = p["gains"]
        cf = p["cf"]
        rmsnorm(gains, lambda f: gains.ap[:, 0, f:f + 1], h, NT)
        tg = "%s%dl%d_" % (kind, ti, l)
        dbg(tg + "h0", h[0].ap[:, :NT], [h[0]])

        def a3(c):
            return aext[c].ap[:, 0:496].rearrange("p (b k) -> p b k", k=31)

        def build_dg(c):
            dgt_ = dg[c % 2]
            S.op(pool, lambda: G.tensor_tensor(out=dgt_.ap, in0=ident.unsqueeze(1).to_broadcast([128, 31, 128]),
                                               in1=p["caw"].ap[:, c, :].unsqueeze(2).to_broadcast([128, 31, 128]), op=ALU.mult),
                 reads=[cstt, p["caw"]], writes=[dgt_])
        build_dg(0)
        build_dg(1)

        if samp:
            rows = st_ca[l].rearrange("b k c -> (b k) c")
            for r4 in range(4):
                S.dma(hsAB[r4].ap[:120, :], rows[120 * r4:120 * r4 + 120, :], writes=[hsAB[r4]], st=hsAB[r4])
            for c in range(4):
                pst = P.get()
                for r4 in range(4):
                    transpose(pst.ap[:, r4 * 120:(r4 + 1) * 120], hsAB[r4].ap[:120, c * 128:(c + 1) * 128], 120, [hsAB[r4]], [pst])
                evac(a3(c)[:, :, 0:30], pst.ap[:, 0:480].rearrange("p (b k) -> p b k", k=30), [pst], [aext[c]])
                P.put(pst)
            S.dma(o_ca_s[l, :, 0:29, :], st_ca[l, :, 1:30, :], st=None)
        else:
            for c in range(4):
                if first:
                    memset(aext[c], aext[c].ap[:, 0:30])
                else:
                    evac(aext[c].ap[:, 0:30], p["carA"].ap[:, c, :], [p["carA"]], [aext[c]], eng=pool)
        for c in range(4):
            slot, (vl, vg) = w_next()
            psl = P.get(); psg = P.get()
            mm(psl.ap[:, :NT], [(vl[:, kt, :], h[kt].ap[:, :NT]) for kt in range(8)], [slot] + h, [psl])
            mm(psg.ap[:, :NT], [(vg[:, kt, :], h[kt].ap[:, :NT]) for kt in range(8)], [slot] + h, [psg])
            sg = scr.get()
            actf(sg.ap[:, :NT], psg.ap[:, :NT], AF.Sigmoid, [psg], [sg])
            dsta = a3(c)[:, :, 30] if samp else aext[c].ap[:, 30:30 + NT]
            tt_op(dve, dsta, psl.ap[:, :NT], sg.ap[:, :NT], ALU.mult, [psl, sg], [aext[c]])
            if samp:
                tt_op(dve, alast.ap[:, c, 0:NS], psl.ap[:, :NS], sg.ap[:, :NS], ALU.mult, [psl, sg], [alast])
            elif last:
                tt_op(dve, alast.ap[:, c, 0:30], psl.ap[:, NT - 30:NT], sg.ap[:, NT - 30:NT], ALU.mult, [psl, sg], [alast])
            P.put(psl, psg)
        if samp or last:
            pst = P.get()
            nr = NS if samp else 30
            for c in range(4):
                transpose(pst.ap[:nr, c * 128:(c + 1) * 128], alast.ap[:, c, 0:nr], 128, [alast], [pst])
            so = scr.get()
            evac(so.ap[:nr, :], pst.ap[:nr, :], [pst], [so])
            P.put(pst)
            if samp:
                S.dma(o_ca_s[l, :, 29, :], so.ap[:NS, :], reads=[so], st=so)
            else:
                S.dma(o_ca_p[l], so.ap[:30, :], reads=[so], st=so)
        if not samp and not last:
            for c in range(4):
                evac(p["carA"].ap[:, c, :], aext[c].ap[:, NT:NT + 30], [aext[c]], [p["carA"]], eng=pool)
        accs = hsAB
        for c in range(4):
            acc = accs[c]
            dgt = dg[c % 2]
            if c >= 2:
                build_dg(c)

            def tap(k):
                return a3(c)[:, :, k] if samp else aext[c].ap[:, k:k + NT]
            psc = P.get()
            mm(psc.ap[:, :NT], [(dgt.ap[:, k, :], tap(k)) for k in range(31)], [dgt, aext[c]], [psc])
            actf(acc.ap[:, :NT], psc.ap[:, :NT], AF.Identity, [psc, p["avec"]], [acc], bias=p["avec"].ap[:, 0, c:c + 1])
            P.put(psc)
        for c2 in range(2):
            slot, (vu,) = w_next()
            for cc in range(2):
                c = 2 * c2 + cc
                psu = P.get()
                mm(psu.ap[:, :NT], [(vu[:, kt, cc * 128:(cc + 1) * 128], h[kt].ap[:, :NT]) for kt in range(8)], [slot] + h, [psu])
                evac(ub[c].ap[:, :NT], psu.ap[:, :NT], [psu], [ub[c]])
                P.put(psu)
        psv = [P.get() for r in range(R)]
        for kh in range(2):
            slot, (vv,) = w_next()
            for r in range(R):
                def fn():
                    ins = None
                    for kk in range(4):
                        kt = 4 * kh + kk
                        ins = PE.matmul(psv[r].ap[:MT, :], lhsT=h[kt].ap[:, r * 128:r * 128 + MT], rhs=vv[:, kk, :],
                                        start=(kh == 0 and kk == 0), stop=(kh == 1 and kk == 3))
                    return ins
                S.op(pe, fn, reads=[slot] + h, writes=[psv[r]])
        lnb = p["lnb"]
        for r in range(R):
            st1 = small.get()
            vs = scr.get()
            actf(vs.ap[:MT, :], psv[r].ap[:MT, :], AF.Identity, [psv[r]], [vs, st1], accum=st1.ap[:MT, 0:1])
            junk = scr.get()
            actf(junk.ap[:MT, :], psv[r].ap[:MT, :], AF.Square, [psv[r]], [junk, st1], accum=st1.ap[:MT, 1:2])
            P.put(psv[r])
            ts_op(dve, st1.ap[:MT, 2:3], st1.ap[:MT, 0:1], 1.0 / DA, None, ALU.mult, None, [st1], [st1])
            tt_op(dve, st1.ap[:MT, 3:4], st1.ap[:MT, 2:3], st1.ap[:MT, 2:3], ALU.mult, [st1], [st1])
            stt(st1.ap[:MT, 4:5], st1.ap[:MT, 1:2], 1.0 / DA, st1.ap[:MT, 3:4], ALU.mult, ALU.subtract, [st1], [st1])
            actf(st1.ap[:MT, 5:6], st1.ap[:MT, 4:5], AF.Sqrt, [st1], [st1], bias=EPS)
            S.op(dve, lambda: V.reciprocal(out=st1.ap[:MT, 6:7], in_=st1.ap[:MT, 5:6]), reads=[st1], writes=[st1])
            stt(st1.ap[:MT, 7:8], st1.ap[:MT, 2:3], -1.0, st1.ap[:MT, 6:7], ALU.mult, ALU.mult, [st1], [st1])
            ts_op(dve, vs.ap[:MT, :], vs.ap[:MT, :], st1.ap[:MT, 6:7], st1.ap[:MT, 7:8], ALU.mult, ALU.add, [vs, st1], [vs])
            tt_op(pool, vs.ap[:MT, :], vs.ap[:MT, :], lnb.ap[:MT, 0, :], ALU.mult, [vs, lnb], [vs])
            if samp or (last and r == 3):
                tt_op(pool, vs.ap[:MT, :], vs.ap[:MT, :], lnb.ap[:MT, 1, :], ALU.add, [vs, lnb], [vs])
                evac(vtok[r].ap[:MT, :], vs.ap[:MT, :], [vs], [vtok[r]], eng=act)
                S.dma(o_cv_s[l] if samp else o_cv_p[l], vs.ap[:MT, :], reads=[vs], st=vs)
            else:
                tt_op(pool, vtok[r].ap[:MT, :], vs.ap[:MT, :], lnb.ap[:MT, 1, :], ALU.add, [vs, lnb], [vtok[r]])
        dbg(tg + "vtok0", vtok[0].ap[:MT, :], [vtok[0]])
        for g in range(4):
            psm_ = P.get()
            for r in range(R):
                if samp:
                    S.op(pe, lambda: PE.matmul(psm_.ap[:, :NS], lhsT=vtok[0].ap[:NS, g * 128:(g + 1) * 128],
                                               rhs=p["Wsm"].ap[:NS, g, :], start=True, stop=True),
                         reads=[vtok[0], p["Wsm"]], writes=[psm_])
                else:
                    def fn():
                        PE.matmul(psm_.ap[:, r * 128:(r + 1) * 128], lhsT=vtok[r].ap[:, g * 128:(g + 1) * 128],
                                  rhs=p["WsT"].ap[:, g, :], start=True, stop=False)
                        return PE.matmul(psm_.ap[:, r * 128:(r + 1) * 128], lhsT=ones_row,
                                         rhs=p["bs1"].ap[0:1, g, :], start=False, stop=True)
                    S.op(pe, fn, reads=[vtok[r], p["WsT"], p["bs1"], cstr], writes=[psm_])
            if samp:
                tmp = scr.get()
                ts_op(dve, tmp.ap[:, :NS], psm_.ap[:, :NS], p["bs0"].ap[:, g:g + 1], None, ALU.add, None, [psm_, p["bs0"]], [tmp])
                tt_op(dve, ub[g].ap[:, :NS], ub[g].ap[:, :NS], tmp.ap[:, :NS], ALU.mult, [ub[g], tmp], [ub[g]])
            else:
                tt_op(dve, ub[g].ap[:, :NT], ub[g].ap[:, :NT], psm_.ap[:, :NT], ALU.mult, [ub[g], psm_], [ub[g]])
            P.put(psm_)
        dbg(tg + "ub0", ub[0].ap[:, :NT], [ub[0]])

        for c2 in range(2):
            slot, (vz,) = w_next()
            for cc in range(2):
                c = 2 * c2 + cc
                psz = P.get()
                mm(psz.ap[:, :NT], [(vz[:, kt, cc * 128:(cc + 1) * 128], h[kt].ap[:, :NT]) for kt in range(8)], [slot] + h, [psz])
                evac(zc[c].ap[:, :NT], psz.ap[:, :NT], [psz], [zc[c]])
                P.put(psz)
        dbg(tg + "zc0", zc[0].ap[:, :NT], [zc[0]])
        ps1 = P.get(); ps2 = P.get()
        for c in range(4):
            a_r = scr_r.get()
            evac(a_r.ap[:, :NT], accs[c].ap[:, :NT], [accs[c]], [a_r], eng=act)
            S.op(pe, lambda: PE.matmul(ps1.ap[:, :NT], lhsT=ones_c, rhs=a_r.ap[:, :NT], start=(c == 0), stop=(c == 3)),
                 reads=[cstr, a_r], writes=[ps1])
            sq = scr_r.get()
            actf(sq.ap[:, :NT], accs[c].ap[:, :NT], AF.Square, [accs[c]], [sq])
            S.op(pe, lambda: PE.matmul(ps2.ap[:, :NT], lhsT=ones_c, rhs=sq.ap[:, :NT], start=(c == 0), stop=(c == 3)),
                 reads=[cstr, sq], writes=[ps2])
        mean = scr.get()
        evac(mean.ap[:, :NT], ps1.ap[:, :NT], [ps1], [mean], eng=act)
        var = scr.get()
        tt_op(dve, var.ap[:, :NT], mean.ap[:, :NT], mean.ap[:, :NT], ALU.mult, [mean], [var])
        tt_op(dve, var.ap[:, :NT], ps2.ap[:, :NT], var.ap[:, :NT], ALU.subtract, [ps2, var], [var])
        P.put(ps1, ps2)
        actf(var.ap[:, :NT], var.ap[:, :NT], AF.Sqrt, [var], [var], bias=EPS)
        S.op(dve, lambda: V.reciprocal(out=var.ap[:, :NT], in_=var.ap[:, :NT]), reads=[var], writes=[var])
        for c in range(4):
            tt_op(pool, accs[c].ap[:, :NT], accs[c].ap[:, :NT], mean.ap[:, :NT], ALU.subtract, [accs[c], mean], [accs[c]])
            tt_op(dve, accs[c].ap[:, :NT], accs[c].ap[:, :NT], var.ap[:, :NT], ALU.mult, [accs[c], var], [accs[c]])
            actf(acs[c].ap[:, :NT], accs[c].ap[:, :NT], AF.Silu, [accs[c], p["avec"]], [acs[c]],
                 scale=p["avec"].ap[:, 1, c:c + 1], bias=p["avec"].ap[:, 2, c:c + 1])
        dbg(tg + "acs0", acs[0].ap[:, :NT], [acs[0]])

        def merge1_pe(f):
            s1_, (vgA, vgB) = w_next(dve)
            pgA = P.get(); pgB = P.get()
            mm(pgA.ap[:, :NT], [(vgA[:, kt, :], h[kt].ap[:, :NT]) for kt in range(8)], [s1_] + h, [pgA])
            mm(pgB.ap[:, :NT], [(vgB[:, kt, :], h[kt].ap[:, :NT]) for kt in range(8)], [s1_] + h, [pgB])
            s2_, (vao, vbo) = w_next(dve)
            pao = P.get(); pbo = P.get()
            mm(pao.ap[:, :NT], [(vao[:, kt, :], acs[kt].ap[:, :NT]) for kt in range(4)], [s2_] + acs, [pao])
            mm(pbo.ap[:, :NT], [(vbo[:, kt, :], ub[kt].ap[:, :NT]) for kt in range(4)], [s2_] + ub, [pbo])
            sA = scr.get(); sB = scr.get()
            actf(sA.ap[:, :NT], pgA.ap[:, :NT], AF.Sigmoid, [pgA], [sA])
            actf(sB.ap[:, :NT], pgB.ap[:, :NT], AF.Sigmoid, [pgB], [sB])
            P.put(pgA, pgB)
            return (sA, sB, pao, pbo)

        def merge1_rest(f, st_):
            sA, sB, pao, pbo = st_
            tt_op(dve, sA.ap[:, :NT], pao.ap[:, :NT], sA.ap[:, :NT], ALU.mult, [pao, sA], [sA])
            tt_op(dve, sB.ap[:, :NT], pbo.ap[:, :NT], sB.ap[:, :NT], ALU.mult, [pbo, sB], [sB])
            P.put(pao, pbo)
            tt_op(pool, m[f].ap[:, :NT], sA.ap[:, :NT], sB.ap[:, :NT], ALU.add, [sA, sB], [m[f]])

        H = p["HSC"]
        carS = p["carS"]
        s0T = xstage[0]
        if samp:
            for ri, srcst in enumerate([st_re, st_im]):
                for i4 in range(4):
                    S.dma(hsAB[i4].ap[:NS, :], srcst[l][:, 512 * i4:512 * (i4 + 1)], writes=[hsAB[i4]], st=hsAB[i4])
                pst = P.get()
                for q in range(16):
                    transpose(pst.ap[:, 16 * q:16 * q + 16], hsAB[q // 4].ap[:NS, 128 * (q % 4):128 * (q % 4 + 1)], NS,
                              [hsAB[q // 4]], [pst])
                evac(s0T.ap[:, 256 * ri:256 * ri + 256], pst.ap[:, 0:256], [pst], [s0T])
                P.put(pst)
        psy = None
        srow = {}
        tabs = {}
        bu = {}
        if not samp:
            tabs[0] = tab_load(l, 0)
        for q in range(16):
            c = q // 4; j = q % 4
            if j == 0:
                load_cpad(l, c)
            def emit_bu(qq):
                cc_ = qq // 4; jj_ = qq % 4
                rs_ = slice(32 * jj_, 32 * jj_ + 32)
                a_ = P.get(); b_ = P.get()
                S.op(pe, lambda: PE.matmul(a_.ap[:, :NT], lhsT=p["BbRe"].ap[rs_, cc_, :], rhs=zc[cc_].ap[rs_, :NT], start=True,
                                           stop=True, tile_position=(32 * jj_, 0)), reads=[p["BbRe"], zc[cc_]], writes=[a_])
                S.op(pe, lambda: PE.matmul(b_.ap[:, :NT], lhsT=p["BbIm"].ap[rs_, cc_, :], rhs=zc[cc_].ap[rs_, :NT], start=True,
                                           stop=True, tile_position=(32 * jj_, 0)), reads=[p["BbIm"], zc[cc_]], writes=[b_])
                return a_, b_
            def pre(qq):
                a_, b_ = bu.pop(qq)
                tq = tabs[qq]
                Cq = tq.ap[:, 0:NT]; Sq = tq.ap[:, 512:512 + NT]
                v0, v1, v2, v3 = (hsAB if qq % 2 == 0 else hsCD)
                tt_op(dve, v0.ap[:, :NT], a_.ap[:, :NT], Cq, ALU.mult, [a_, tq], [v0])
                tt_op(dve, v1.ap[:, :NT], b_.ap[:, :NT], Sq, ALU.mult, [b_, tq], [v1])
                tt_op(dve, v2.ap[:, :NT], b_.ap[:, :NT], Cq, ALU.mult, [b_, tq], [v2])
                tt_op(dve, v3.ap[:, :NT], a_.ap[:, :NT], Sq, ALU.mult, [a_, tq], [v3])
                P.put(a_, b_)
            if q == 0:
                bu[0] = emit_bu(0)
                if not samp:
                    pre(0)
            if samp:
                psr, psi = bu.pop(q)
            c1 = H.ap[:, 0, 0, q:q + 1]; s1 = H.ap[:, 0, 1, q:q + 1]; ns1 = H.ap[:, 0, 2, q:q + 1]
            m1st = None
            if (not samp) and q % 2 == 0:
                m1st = merge1_pe(q // 2)
            if samp:
                cur = hsAB[0:2]
                evac(cur[0].ap[:, :NT], psr.ap[:, :NT], [psr], [cur[0]], eng=act)
                evac(cur[1].ap[:, :NT], psi.ap[:, :NT], [psi], [cur[1]], eng=act)
                P.put(psr, psi)
                sre = s0T.ap[:, 0:256].rearrange("p (q b) -> p q b", q=16)[:, q, :]
                sim = s0T.ap[:, 256:512].rearrange("p (q b) -> p q b", q=16)[:, q, :]
                stt(cur[0].ap[:, :NT], sre, c1, cur[0].ap[:, :NT], ALU.mult, ALU.add, [s0T, H, cur[0]], [cur[0]])
                stt(cur[0].ap[:, :NT], sim, ns1, cur[0].ap[:, :NT], ALU.mult, ALU.add, [s0T, H, cur[0]], [cur[0]])
                stt(cur[1].ap[:, :NT], sim, c1, cur[1].ap[:, :NT], ALU.mult, ALU.add, [s0T, H, cur[1]], [cur[1]])
                stt(cur[1].ap[:, :NT], sre, s1, cur[1].ap[:, :NT], ALU.mult, ALU.add, [s0T, H, cur[1]], [cur[1]])
                fin = hsF
                for ri in range(2):
                    evac(hsF[ri].ap[:, :NT], cur[ri].ap[:, :NT], [cur[ri]], [hsF[ri]], eng=act)
                    if j == 0:
                        srow[ri] = P.get()
                    transpose(srow[ri].ap[:NS, 128 * j:128 * (j + 1)], cur[ri].ap[:, :NS], 128, [cur[ri]], [srow[ri]])
                    if j == 3:
                        so = scr.get()
                        evac(so.ap[:NS, :], srow[ri].ap[:NS, :], [srow[ri]], [so])
                        P.put(srow[ri])
                        dsto = (o_re_s if ri == 0 else o_im_s)[l, :, q - 3:q + 1, :]
                        S.dma(dsto, so.ap[:NS, :].rearrange("b (q m) -> b q m", q=4), reads=[so], st=so)
            else:
                tb = tabs[q]
                if q + 1 < 16:
                    tabs[q + 1] = tab_load(l, q + 1)
                Ct = tb.ap[:, 0:NT]; Sn = tb.ap[:, 512:512 + NT]
                w0, w1, w2, w3 = (hsAB if q % 2 == 0 else hsCD)
                hf = hsF2[q % 2]
                tt_op(dve, w0.ap[:, :NT], w0.ap[:, :NT], w1.ap[:, :NT], ALU.add, [w0, w1], [w0])
                tt_op(dve, w2.ap[:, :NT], w2.ap[:, :NT], w3.ap[:, :NT], ALU.subtract, [w2, w3], [w2])
                rho = p["RHO"].ap[:, q:q + 1].to_broadcast([128, NT])
                if first:
                    ini_re = 0.0; ini_im = 0.0
                    ird = []
                else:
                    st0 = small.get()
                    U = p["UPH"]
                    sre = carS.ap[:, 0, q:q + 1]; sim = carS.ap[:, 1, q:q + 1]
                    uc = U.ap[:, 0, q:q + 1]; us = U.ap[:, 1, q:q + 1]; uns = U.ap[:, 2, q:q + 1]
                    ts_op(dve, st0.ap[:, 0:1], sre, uc, None, ALU.mult, None, [carS, U], [st0])
                    stt(st0.ap[:, 0:1], sim, uns, st0.ap[:, 0:1], ALU.mult, ALU.add, [carS, U, st0], [st0])
                    ts_op(dve, st0.ap[:, 1:2], sim, uc, None, ALU.mult, None, [carS, U], [st0])
                    stt(st0.ap[:, 1:2], sre, us, st0.ap[:, 1:2], ALU.mult, ALU.add, [carS, U, st0], [st0])
                    ini_re = st0.ap[:, 0:1]; ini_im = st0.ap[:, 1:2]
                    ird = [st0]
                S.op(dve, lambda: V.tensor_tensor_scan(out=w1.ap[:, :NT], data0=rho, data1=w0.ap[:, :NT], initial=ini_re,
                                                       op0=ALU.mult, op1=ALU.add), reads=[w0, p["RHO"]] + ird, writes=[w1])
                S.op(dve, lambda: V.tensor_tensor_scan(out=w3.ap[:, :NT], data0=rho, data1=w2.ap[:, :NT], initial=ini_im,
                                                       op0=ALU.mult, op1=ALU.add), reads=[w2, p["RHO"]] + ird, writes=[w3])
                if q + 1 < 16:
                    bu[q + 1] = emit_bu(q + 1)
                    pre(q + 1)
                pa = scr.get(); pb = scr.get()
                tt_op(pool, pa.ap[:, :NT], w3.ap[:, :NT], Ct, ALU.mult, [w3, tb], [pa])
                tt_op(pool, pb.ap[:, :NT], w1.ap[:, :NT], Sn, ALU.mult, [w1, tb], [pb])
                tt_op(dve, w0.ap[:, :NT], w1.ap[:, :NT], Ct, ALU.mult, [w1, tb], [w0])
                pd = scr.get()
                tt_op(pool, pd.ap[:, :NT], w3.ap[:, :NT], Sn, ALU.mult, [w3, tb], [pd])
                tt_op(dve, w0.ap[:, :NT], w0.ap[:, :NT], pd.ap[:, :NT], ALU.subtract, [w0, pd], [w0])
                tt_op(pool, pa.ap[:, :NT], pa.ap[:, :NT], pb.ap[:, :NT], ALU.add, [pa, pb], [pa])
                evac(hf[0].ap[:, :NT], w0.ap[:, :NT], [w0], [hf[0]], eng=act)
                evac(hf[1].ap[:, :NT], pa.ap[:, :NT], [pa], [hf[1]], eng=act)
                fin = hf
                evac(carS.ap[:, 0, q:q + 1], w0.ap[:, NT - 1:NT], [w0], [carS], eng=pool)
                evac(carS.ap[:, 1, q:q + 1], pa.ap[:, NT - 1:NT], [pa], [carS], eng=pool)
            if samp and q + 1 < 16:
                bu[q + 1] = emit_bu(q + 1)
            if m1st is not None:
                merge1_rest(q // 2, m1st)
            if j == 0:
                psy = P.get()
            cpr = Cpad_re[c % 2]; cpi = Cpad_im[c % 2]
            if q == 0:
                dbg(tg + "sre0", fin[0].ap[:, :NT], [fin[0]])
                dbg(tg + "sim0", fin[1].ap[:, :NT], [fin[1]])

            def fny():
                PE.matmul(psy.ap[:, :NT], lhsT=cpr.ap[:, j, :], rhs=fin[0].ap[:, :NT], start=(j == 0), stop=False)
                return PE.matmul(psy.ap[:, :NT], lhsT=cpi.ap[:, j, :], rhs=fin[1].ap[:, :NT], start=False, stop=(j == 3))
            S.op(pe, fny, reads=[cpr, cpi, fin[0], fin[1]], writes=[psy])
            if j == 3:
                ysb = scr.get()
                stt(ysb.ap[:, :NT], zc[c].ap[:, :NT], p["avec"].ap[:, 3, c:c + 1], psy.ap[:, :NT], ALU.mult, ALU.add,
                    [zc[c], p["avec"], psy], [ysb])
                P.put(psy)
                actf(zc[c].ap[:, :NT], ysb.ap[:, :NT], AF.Gelu_apprx_tanh, [ysb], [zc[c]])
                if c == 0:
                    dbg(tg + "gy0", zc[0].ap[:, :NT], [zc[0]])
        if last:
            for ri in range(2):
                pst = P.get()
                transpose(pst.ap[:16, 0:128], carS.ap[:, ri, :], 128, [carS], [pst])
                so = scr.get()
                evac(so.ap[:16, 0:128], pst.ap[:16, 0:128], [pst], [so])
                P.put(pst)
                S.dma((o_re_p if ri == 0 else o_im_p)[l], so.ap[:16, 0:128], reads=[so], st=so)

        if samp:
            for f in range(8):
                st_ = merge1_pe(f)
                merge1_rest(f, st_)
        for f in range(8):
            s3_, (vgC, vcl, vcg) = w_next(dve)
            pgC = P.get(); pcl = P.get(); pcg = P.get()
            mm(pgC.ap[:, :NT], [(vgC[:, kt, :], h[kt].ap[:, :NT]) for kt in range(8)], [s3_] + h, [pgC])
            mm(pcl.ap[:, :NT], [(vcl[:, kt, :], zc[kt].ap[:, :NT]) for kt in range(4)], [s3_] + zc, [pcl])
            mm(pcg.ap[:, :NT], [(vcg[:, kt, :], zc[kt].ap[:, :NT]) for kt in range(4)], [s3_] + zc, [pcg])
            sC = scr.get(); sG = scr.get()
            actf(sC.ap[:, :NT], pgC.ap[:, :NT], AF.Sigmoid, [pgC], [sC])
            actf(sG.ap[:, :NT], pcg.ap[:, :NT], AF.Sigmoid, [pcg], [sG])
            tt_op(dve, sG.ap[:, :NT], pcl.ap[:, :NT], sG.ap[:, :NT], ALU.mult, [pcl, sG], [sG])
            P.put(pgC, pcl, pcg)
            tt_op(pool, sG.ap[:, :NT], sG.ap[:, :NT], sC.ap[:, :NT], ALU.mult, [sG, sC], [sG])
            tt_op(dve, m[f].ap[:, :NT], m[f].ap[:, :NT], sG.ap[:, :NT], ALU.add, [m[f], sG], [m[f]])
        dbg(tg + "m0", m[0].ap[:, :NT], [m[0]])
        for c4 in range(4):
            slot, (vo,) = w_next(dve)
            for cc in range(2):
                f = 2 * c4 + cc
                pso = P.get()
                mm(pso.ap[:, :NT], [(vo[:, kt, cc * 128:(cc + 1) * 128], m[kt].ap[:, :NT]) for kt in range(8)], [slot] + m, [pso])
                tt_op(dve, x[f].ap[:, :NT], x[f].ap[:, :NT], pso.ap[:, :NT], ALU.add, [x[f], pso], [x[f]])
                P.put(pso)

        S.phase_barrier(ffn_tiles)
        dbg(tg + "xmix0", x[0].ap[:, :NT], [x[0]])
        rmsnorm(gains, lambda f: gains.ap[:, 1, f:f + 1], h, NT)
        carF = p["carF"]
        hist = pesb[0:3]
        hist_ap = arenaF[:, pesb_off:pesb_off + 44 * 32].rearrange("p (j b k) -> p j b k", j=44, k=2)
        if samp:
            rows = st_cf[l].rearrange("b k c -> (b k) c")
            S.dma(o_cf_s[l, :, 0, :], st_cf[l, :, 1, :], st=None)
            for j4 in range(11):
                rw = scr.get()
                S.dma(rw.ap[:32, :], rows[:, 512 * j4:512 * (j4 + 1)], writes=[rw], st=rw)
                pst = P.get()
                for jj in range(4):
                    transpose(pst.ap[:, 32 * jj:32 * jj + 32], rw.ap[:32, 128 * jj:128 * (jj + 1)], 32, [rw], [pst])
                evac(arenaF[:, pesb_off + 128 * j4: pesb_off + 128 * (j4 + 1)], pst.ap[:, 0:128], [pst], hist)
                P.put(pst)
        for j in range(NJ):
            slot, (vg_, vv_) = w_next(dve if j % 2 else act)
            outs = []
            for half, (vw, idx, ebuf) in enumerate([(vg_, j, eg[j % 2]), (vv_, NJ + j, ev[j % 2])]):
                psu = P.get()
                mm(psu.ap[:, :NT], [(vw[:, kt, :], h[kt].ap[:, :NT]) for kt in range(8)], [slot] + h, [psu])
                t0 = scr.get()
                actf(t0.ap[:, :NT], psu.ap[:, :NT], AF.Identity, [psu, cf], [t0],
                     scale=cf.ap[:, idx, 2:3], bias=cf.ap[:, idx, 3:4])
                if samp:
                    hv = hist_ap[:, idx, :, :]
                    stt(t0.ap[:, :NT], hv[:, :, 1], cf.ap[:, idx, 1:2], t0.ap[:, :NT], ALU.mult, ALU.add, hist + [cf, t0], [t0])
                    stt(t0.ap[:, :NT], hv[:, :, 0], cf.ap[:, idx, 0:1], t0.ap[:, :NT], ALU.mult, ALU.add, hist + [cf, t0], [t0])
                    rawc = ebuf
                    evac(rawc.ap[:, :NT], psu.ap[:, :NT], [psu], [rawc], eng=act)
                    P.put(psu)
                    upst = P.get()
                    transpose(upst.ap[:NS, 0:128], rawc.ap[:, :NS], 128, [rawc], [upst])
                    so2 = scr.get()
                    evac(so2.ap[:NS, 0:128], upst.ap[:NS, 0:128], [upst], [so2])
                    P.put(upst)
                    S.dma(o_cf_s[l, :, 1, 128 * idx:128 * (idx + 1)], so2.ap[:NS, 0:128], reads=[so2], st=so2)
                else:
                    evac(ebuf.ap[:, 2:2 + NT], psu.ap[:, :NT], [psu], [ebuf], eng=act)
                    P.put(psu)
                    if first:
                        memset(ebuf, ebuf.ap[:, 0:2])
                    else:
                        evac(ebuf.ap[:, 0:2], carF.ap[:, idx, :], [carF], [ebuf], eng=pool)
                    stt(t0.ap[:, :NT], ebuf.ap[:, 1:1 + NT], cf.ap[:, idx, 1:2], t0.ap[:, :NT], ALU.mult, ALU.add, [ebuf, cf, t0], [t0])
                    stt(t0.ap[:, :NT], ebuf.ap[:, 0:NT], cf.ap[:, idx, 0:1], t0.ap[:, :NT], ALU.mult, ALU.add, [ebuf, cf, t0], [t0])
                    evac(carF.ap[:, idx, :], ebuf.ap[:, NT:NT + 2], [ebuf], [carF], eng=pool)
                outs.append(t0)
            gl_ = scr.get()
            actf(gl_.ap[:, :NT], outs[0].ap[:, :NT], AF.Gelu_apprx_tanh, [outs[0]], [gl_])
            tt_op(dve, actt[j].ap[:, :NT], gl_.ap[:, :NT], outs[1].ap[:, :NT], ALU.mult, [gl_, outs[1]], [actt[j]])
        if last:
            pst = P.get()
            transpose(pst.ap[:88, 0:128], carF.ap.rearrange("p j k -> p (j k)"), 128, [carF], [pst])
            so = scr.get()
            evac(so.ap[:88, 0:128], pst.ap[:88, 0:128], [pst], [so])
            P.put(pst)
            S.dma(o_cf_p[l].rearrange("k (j c) -> j k c", c=128), so.ap[:88, 0:128], reads=[so], st=so)
        for fh in range(2):
            pacc = [P.get() for _ in range(4)]
            for j0 in range(0, NJ, 4):
                nj = min(4, NJ - j0)
                slot, (vd,) = w_next(dve)

                def fn():
                    ins = None
                    for jj in range(nj):
                        for fi in range(4):
                            ins = PE.matmul(pacc[fi].ap[:, :NT], lhsT=vd[:, jj, fi * 128:(fi + 1) * 128],
                                            rhs=actt[j0 + jj].ap[:, :NT], start=(j0 + jj == 0), stop=(j0 + jj == NJ - 1))
                    return ins
                S.op(pe, fn, reads=[slot] + actt[j0:j0 + nj], writes=pacc)
            for fi in range(4):
                f = 4 * fh + fi
                tt_op(dve, x[f].ap[:, :NT], x[f].ap[:, :NT], pacc[fi].ap[:, :NT], ALU.add, [x[f], pacc[fi]], [x[f]])
                P.put(pacc[fi])

        dbg(tg + "act0", actt[0].ap[:, :NT], [actt[0]])
        dbg(tg + "xffn0", x[0].ap[:, :NT], [x[0]])
        rmsnorm(gains, lambda f: gains.ap[:, 2, f:f + 1], h, NT)
        if samp:
            pstg = scr.get()
            S.dma(pstg.ap[:NS, 0:256], psm[l], writes=[pstg], st=pstg)
            pst = P.get()
            for kt in range(2):
                transpose(pst.ap[:, kt * 16:kt * 16 + 16], pstg.ap[:NS, kt * 128:(kt + 1) * 128], NS, [pstg], [pst])
            for kt in range(2):
                evac(pT[kt].ap[:, :NS], pst.ap[:, kt * 16:kt * 16 + 16], [pst], [pT[kt]])
            P.put(pst)
        else:
            pstg = [scr.get(), scr.get()]
            for r in range(4):
                tk = TT * ti + 128 * r
                S.dma(pstg[r // 2].ap[:, (r % 2) * 256:(r % 2) * 256 + 256], pp[l, tk:tk + 128, :], writes=[pstg[r // 2]], st=pstg[r // 2])
            for kt in range(2):
                pst = P.get()
                for r in range(4):
                    transpose(pst.ap[:, r * 128:(r + 1) * 128],
                              pstg[r // 2].ap[:, (r % 2) * 256 + kt * 128:(r % 2) * 256 + (kt + 1) * 128], 128, [pstg[r // 2]], [pst])
                evac(pT[kt].ap[:, :NT], pst.ap[:, :NT], [pst], [pT[kt]])
                P.put(pst)
        slot, (vpe,) = w_next(dve)
        for f in range(8):
            pse = P.get()
            mm(pse.ap[:, :NT], [(vpe[:, kt, f * 128:(f + 1) * 128], pT[kt].ap[:, :NT]) for kt in range(2)], [slot] + pT, [pse])
            evac(pesb[f].ap[:, :NT], pse.ap[:, :NT], [pse], [pesb[f]])
            P.put(pse)
        for c4 in range(4):
            slot, (vpg,) = w_next(dve)
            for cc in range(2):
                f = 2 * c4 + cc
                psg = P.get()
                mm(psg.ap[:, :NT], [(vpg[:, kt, cc * 128:(cc + 1) * 128], h[kt].ap[:, :NT]) for kt in range(8)], [slot] + h, [psg])
                sg = scr.get()
                actf(sg.ap[:, :NT], psg.ap[:, :NT], AF.Sigmoid, [psg], [sg])
                P.put(psg)
                tt_op(pool, sg.ap[:, :NT], sg.ap[:, :NT], pesb[f].ap[:, :NT], ALU.mult, [sg, pesb[f]], [sg])
                tt_op(dve, x[f].ap[:, :NT], x[f].ap[:, :NT], sg.ap[:, :NT], ALU.add, [x[f], sg], [x[f]])
        dbg(tg + "xple0", x[0].ap[:, :NT], [x[0]])

    for (kind, ti) in tiles_order:
        samp = (kind == "s")
        NT = NS if samp else TT
        R = 1 if samp else 4
        MT = NS if samp else 128
        S.phase_barrier(mixer_tiles)
        for r in range(R):
            xst = xstage[r % 2]
            if samp:
                S.dma(xst.ap[:NS, :], xs, writes=[xst], st=xst)
            else:
                S.dma(xst.ap[:, :], xp[TT * ti + 128 * r: TT * ti + 128 * (r + 1), :], writes=[xst], st=xst)
            for fg in range(2):
                pst = P.get()
                for k in range(4):
                    f = 4 * fg + k
                    transpose(pst.ap[:, k * 128:k * 128 + MT], xst.ap[:MT, f * 128:(f + 1) * 128], MT, [xst], [pst])
                evac(xbig[:, 4 * fg:4 * fg + 4, r * 128:r * 128 + MT],
                     pst.ap.rearrange("p (k t) -> p k t", k=4)[:, :, 0:MT], [pst], x[4 * fg:4 * fg + 4])
                P.put(pst)
        dbg("%s%d_x0" % (kind, ti), x[0].ap[:, :NT], [x[0]])
        for l in range(NL):
            if l > 0:
                S.phase_barrier(mixer_tiles)
            layer(l, kind, ti, NT)
        S.phase_barrier(fin_tiles)
        rmsnorm(gfin, lambda f: gfin.ap[:, f:f + 1], yT, NT)
        for r in range(R):
            ys = ystage[r % 2]
            for fg in range(2):
                pst = P.get()
                for k in range(4):
                    f = 4 * fg + k
                    transpose(pst.ap[:MT, k * 128:(k + 1) * 128], yT[f].ap[:, r * 128:r * 128 + MT], 128, [yT[f]], [pst])
                evac(ys.ap[:MT, fg * 512:(fg + 1) * 512], pst.ap[:MT, :], [pst], [ys])
                P.put(pst)
            if samp:
                S.dma(y_s, ys.ap[:NS, :], reads=[ys], st=ys)
            else:
                S.dma(y_p[TT * ti + 128 * r: TT * ti + 128 * (r + 1), :], ys.ap[:, :], reads=[ys], st=ys)
    assert wstate["next_use"] == len(gchunks), (wstate, len(gchunks))
    S.finish()
    print("SBUF bytes remaining:", nc.sbuf_bytes_remaining, "sems:", S.nsem,
          "ops:", {E.name: E.count for E in S.compute})


_NC_CACHE = {}


def pack_weights(g):
    nch = len(layer_chunks(0))
    out = np.zeros((L, nch, 128, SLOT), np.float32)
    for l in range(L):
        for ci, chunk in enumerate(layer_chunks(l)):
            o = 0
            for (wn, l_, r0, nkt, c0, nc_) in chunk:
                blk = g[wn][l, r0:r0 + 128 * nkt, c0:c0 + nc_].reshape(nkt, 128, nc_)
                out[l, ci, :, o:o + nkt * nc_] = blk.transpose(1, 0, 2).reshape(128, nkt * nc_)
                o += nkt * nc_
    return out


def kernel(**inp):
    f32 = np.float32
    g = {k: np.ascontiguousarray(np.asarray(v), dtype=f32) for k, v in inp.items()}
    if "nc" not in _NC_CACHE:
        _NC_CACHE["nc"] = build_nc()
    nc = _NC_CACHE["nc"]
    cst = np.zeros((128, 256), f32)
    cst[:, 0:128] = np.eye(128, dtype=f32)
    cst[:, 128:256] = np.tril(np.ones((128, 128), f32))
    cst_r = np.zeros((128, 384), f32)
    cst_r[:, 0:128] = 1.0 / 1024.0
    cst_r[:, 128:256] = 1.0 / 512.0
    cst_r[:, 256:384] = 1.0
    shared = {
        "cst": cst, "cst_r": cst_r, "b_s_r": g["b_s"],
    }
    shared["wpack"] = pack_weights(g)
    for k in ["g_mix", "g_ffn", "g_ple",
              "g_final", "conv_a_w", "conv_a_b", "ln_a_g", "ln_a_b", "ln_b_g", "ln_b_b", "w_s", "b_s", "ssm_a_re", "ssm_a_im",
              "ssm_log_dt", "ssm_b_re", "ssm_b_im", "ssm_c_re", "ssm_c_im", "ssm_d", "conv_f_w", "conv_f_b"]:
        shared[k] = g[k]
    shared["conv_f_b"] = np.ascontiguousarray(g["conv_f_b"].reshape(L, 1, 2 * DFF))
    in_maps = []
    for c in range(8):
        sl = slice(NS * c, NS * (c + 1))
        d = dict(shared)
        d["xp"] = g["x_prompt"][c]
        d["xs"] = np.ascontiguousarray(g["x_sample"][sl, 0, :])
        d["pp"] = np.ascontiguousarray(g["p_prompt"][:, c])
        d["psm"] = np.ascontiguousarray(g["p_sample"][:, sl, 0, :])
        d["st_ca"] = np.ascontiguousarray(g["state_conv_a"][:, sl])
        d["st_re"] = np.ascontiguousarray(g["state_ssm_re"][:, sl].reshape(L, NS, 2048))
        d["st_im"] = np.ascontiguousarray(g["state_ssm_im"][:, sl].reshape(L, NS, 2048))
        d["st_cf"] = np.ascontiguousarray(g["state_conv_ffn"][:, sl])
        in_maps.append(d)
    res = run_bass_kernel_spmd(nc, in_maps, core_ids=list(range(8)))
    rs = res.results

    def cat(name, axis, shape=None):
        a = np.concatenate([np.asarray(r[name], dtype=f32) for r in rs], axis=axis)
        return a

    y_prompt = np.stack([np.asarray(r["y_p"], f32) for r in rs], 0)
    y_sample = cat("y_s", 0).reshape(128, 1, D)
    conv_a_p = np.stack([np.asarray(r["o_ca_p"], f32) for r in rs], 1)
    conv_a_s = cat("o_ca_s", 1)
    chunk_v_p = np.stack([np.asarray(r["o_cv_p"], f32) for r in rs], 1)
    chunk_v_s = cat("o_cv_s", 1).reshape(L, 128, 1, DA)
    re_p = np.stack([np.asarray(r["o_re_p"], f32).reshape(L, 32, 64) for r in rs], 1)
    im_p = np.stack([np.asarray(r["o_im_p"], f32).reshape(L, 32, 64) for r in rs], 1)
    re_s = cat("o_re_s", 1).reshape(L, 128, 32, 64)
    im_s = cat("o_im_s", 1).reshape(L, 128, 32, 64)
    cf_p = np.stack([np.asarray(r["o_cf_p"], f32) for r in rs], 1)
    cf_s = cat("o_cf_s", 1)
    return (y_prompt, y_sample, conv_a_p, conv_a_s, chunk_v_p, chunk_v_s, re_p, im_p, re_s, im_s, cf_p, cf_s)
```

```python
import numpy as np
from contextlib import ExitStack
import concourse.bass as bass
import concourse.mybir as mybir
from concourse.bass_utils import run_bass_kernel_spmd

F32 = mybir.dt.float32
F32R = mybir.dt.float32r
BF = mybir.dt.bfloat16
ALU = mybir.AluOpType
AF = mybir.ActivationFunctionType

D = 1024
SEQ = 2048
NS = 16
L = 2
TT = 512
DA = 512
DFF = 2816
NJ = 22
EPS = 1e-6
O1 = 1024
O2 = 2048
O3 = 2560
SLOT = 2048
NSLOT = 3
NSTEP = 9
S5_POOL = False
S5_USE_DVE = False
DBG = None


class Res:
    __slots__ = ("lastw", "readers")

    def __init__(self):
        self.lastw = None
        self.readers = []


class Tile:
    def __init__(self, ap, psum=False):
        self.ap = ap
        self.res = [Res()]
        self.dsem = None
        self.psum = psum

    def __getitem__(self, k):
        return self.ap[k]


class Eng:
    def __init__(self, name, eng, sem, inorder=False):
        self.name = name
        self.eng = eng
        self.sem = sem
        self.count = 0
        self.waited = {}
        self.inorder = inorder


class Sched:
    def __init__(self, nc, es):
        self.nc = nc
        self.es = es
        self.nsem = 0
        self.dma_toks = {}
        self.all_toks = {}
        self.dpool = []
        self.dpi = 0
        self.pe = Eng("pe", nc.tensor, self.newsem("s_pe"), inorder=True)
        self.act = Eng("act", nc.scalar, self.newsem("s_act"))
        self.dve = Eng("dve", nc.vector, self.newsem("s_dve"))
        self.pool = Eng("pool", nc.gpsimd, self.newsem("s_pool"))
        self.sp = Eng("sp", nc.sync, None)
        self.compute = [self.pe, self.act, self.dve, self.pool]

    def newsem(self, name):
        self.nsem += 1
        return self.es.enter_context(self.nc.semaphore(name))

    def _need(self, E, tok, needs):
        if tok is None:
            return
        sem, val, src = tok
        if src is E and E.inorder:
            return
        key = id(sem)
        if E.waited.get(key, 0) >= val:
            return
        if needs.get(key, (None, 0))[1] < val:
            needs[key] = (sem, val)

    def _deps(self, E, reads, writes):
        needs = {}
        for t in reads:
            for r in t.res:
                self._need(E, r.lastw, needs)
                if t.psum:
                    for rd in r.readers:
                        if rd[2] is not E:
                            self._need(E, rd, needs)
        for t in writes:
            for r in t.res:
                self._need(E, r.lastw, needs)
                for rd in r.readers:
                    self._need(E, rd, needs)
        for key, (sem, val) in needs.items():
            E.eng.wait_ge(sem, val)
            E.waited[key] = val

    def _mark(self, tok, reads, writes):
        for t in writes:
            for r in t.res:
                r.lastw = tok
                r.readers = []
        for t in reads:
            for r in t.res:
                r.readers.append(tok)

    def op(self, E, fn, reads=(), writes=()):
        self._deps(E, reads, writes)
        ins = fn()
        E.count += 1
        ins.then_inc(E.sem, 1)
        self._mark((E.sem, E.count, E), reads, writes)

    def dma(self, out_ap, in_ap, reads=(), writes=(), st=None, E=None, track=True, group=False):
        E = E or self.sp
        if group and st is not None and st.dsem is not None:
            saved = []
            for t in writes:
                for r in t.res:
                    if r.lastw is not None and r.lastw[0] is st.dsem[0]:
                        saved.append((r, r.lastw))
                        r.lastw = None
            self._deps(E, reads, writes)
            for r, lw in saved:
                r.lastw = lw
        else:
            self._deps(E, reads, writes)
        if st is None:
            if not self.dpool:
                self.dpool = [Tile(None) for _ in range(8)]
                for t in self.dpool:
                    t.dsem = [self.newsem("dp%d" % self.nsem), 0]
            st = self.dpool[self.dpi % len(self.dpool)]
            self.dpi += 1
            if st.dsem[1] > 0 and E.waited.get(id(st.dsem[0]), 0) < st.dsem[1]:
                E.eng.wait_ge(st.dsem[0], st.dsem[1])
                E.waited[id(st.dsem[0])] = st.dsem[1]
        if st.dsem is None:
            st.dsem = [self.newsem("d%d" % self.nsem), 0]
        ins = E.eng.dma_start(out=out_ap, in_=in_ap)
        st.dsem[1] += 16
        ins.then_inc(st.dsem[0], 16)
        tok = (st.dsem[0], st.dsem[1], None)
        self.all_toks[id(st.dsem[0])] = tok
        if track:
            self.dma_toks[id(st.dsem[0])] = tok
        self._mark(tok, reads, writes)
        return tok

    def drain_pool(self, E):
        for t in self.dpool:
            sem, val = t.dsem
            if val > 0 and E.waited.get(id(sem), 0) < val:
                E.eng.wait_ge(sem, val)
                E.waited[id(sem)] = val

    def phase_barrier(self, tiles):
        toks = [(E.sem, E.count, None) for E in self.compute if E.count > 0]
        toks += list(self.dma_toks.values())
        for t in tiles:
            for r in t.res:
                r.lastw = None
                r.readers = list(toks)

    def finish(self):
        for E in self.compute:
            if E.count > 0 and self.sp.waited.get(id(E.sem), 0) < E.count:
                self.sp.eng.wait_ge(E.sem, E.count)
        for tok in self.all_toks.values():
            sem, val, _ = tok
            if self.sp.waited.get(id(sem), 0) < val:
                self.sp.eng.wait_ge(sem, val)
                self.sp.waited[id(sem)] = val


class Pool_:
    def __init__(self, tiles):
        self.free = list(tiles)

    def get(self):
        assert self.free, "pool exhausted"
        return self.free.pop(0)

    def put(self, *ts):
        for t in ts:
            self.free.append(t)


class Ring:
    def __init__(self, tiles):
        self.tiles = tiles
        self.i = 0

    def get(self):
        t = self.tiles[self.i % len(self.tiles)]
        self.i += 1
        return t


def layer_chunks(l):
    ch = []
    for c in range(4):
        ch.append([("w_in", l, 0, 8, 128 * c, 128), ("w_in", l, 0, 8, 512 + 128 * c, 128)])
    for c2 in range(2):
        ch.append([("w_in", l, 0, 8, O1 + 256 * c2, 256)])
    for kh in range(2):
        ch.append([("w_in", l, 512 * kh, 4, O1 + 512, 512)])
    for c2 in range(2):
        ch.append([("w_in", l, 0, 8, O2 + 256 * c2, 256)])
    for f in range(8):
        ch.append([("w_in", l, 0, 8, O3 + 128 * f, 128), ("w_in", l, 0, 8, O3 + 1024 + 128 * f, 128)])
        ch.append([("w_a_out", l, 0, 4, 128 * f, 128), ("w_b_out", l, 0, 4, 128 * f, 128)])
    for f in range(8):
        ch.append([("w_in", l, 0, 8, O3 + 2048 + 128 * f, 128), ("w_c_glu", l, 0, 4, 128 * f, 128),
                   ("w_c_glu", l, 0, 4, 1024 + 128 * f, 128)])
    for c4 in range(4):
        ch.append([("w_out", l, 0, 8, 256 * c4, 256)])
    for j in range(NJ):
        ch.append([("w_up", l, 0, 8, 128 * j, 128), ("w_up", l, 0, 8, DFF + 128 * j, 128)])
    for fh in range(2):
        for j0 in range(0, NJ, 4):
            nj = min(4, NJ - j0)
            ch.append([("w_down", l, 128 * j0, nj, 512 * fh, 512)])
    ch.append([("w_pe", l, 0, 2, 0, 1024)])
    for c4 in range(4):
        ch.append([("w_pg", l, 0, 8, 256 * c4, 256)])
    return ch


def build_nc():
    nc = bass.Bass("TRN2", target_bir_lowering=False)
    nc.dge_precook = False
    es = ExitStack()
    with es:
        _build(nc, es)
    return nc


def _build(nc, es):
    def din(name, shape, dt=F32):
        return nc.dram_tensor(name, list(shape), dt, kind="ExternalInput").ap()

    def dout(name, shape):
        return nc.dram_tensor(name, list(shape), F32, kind="ExternalOutput").ap()

    xp = din("xp", [SEQ, D]); xs = din("xs", [NS, D])
    pp = din("pp", [L, SEQ, 256]); psm = din("psm", [L, NS, 256])
    st_ca = din("st_ca", [L, NS, 30, DA]); st_re = din("st_re", [L, NS, 2048]); st_im = din("st_im", [L, NS, 2048])
    st_cf = din("st_cf", [L, NS, 2, 2 * DFF])
    NCH = len(layer_chunks(0))
    wpack = din("wpack", [L, NCH, 128, SLOT])
    g_mix = din("g_mix", [L, D]); g_ffn = din("g_ffn", [L, D]); g_ple = din("g_ple", [L, D]); g_final = din("g_final", [D])
    conv_a_w = din("conv_a_w", [L, 31, DA]); conv_a_b = din("conv_a_b", [L, DA])
    ln_a_g = din("ln_a_g", [L, DA]); ln_a_b = din("ln_a_b", [L, DA])
    ln_b_g = din("ln_b_g", [L, DA]); ln_b_b = din("ln_b_b", [L, DA])
    w_s = din("w_s", [L, 4, 128, 128]); b_s = din("b_s", [L, 4, 128]); b_s_r = din("b_s_r", [L, 4, 128])
    ssm_a_re = din("ssm_a_re", [L, 32, 64]); ssm_a_im = din("ssm_a_im", [L, 32, 64]); ssm_log_dt = din("ssm_log_dt", [L, 32])
    ssm_b_re = din("ssm_b_re", [L, 32, 64, 16]); ssm_b_im = din("ssm_b_im", [L, 32, 64, 16])
    ssm_c_re = din("ssm_c_re", [L, 32, 16, 64]); ssm_c_im = din("ssm_c_im", [L, 32, 16, 64])
    ssm_d = din("ssm_d", [L, DA])
    conv_f_w = din("conv_f_w", [L, 3, 2 * DFF]); conv_f_b = din("conv_f_b", [L, 1, 2 * DFF])
    cst = din("cst", [128, 256])
    cst_r = din("cst_r", [128, 384])

    y_p = dout("y_p", [SEQ, D]); y_s = dout("y_s", [NS, D])
    o_ca_p = dout("o_ca_p", [L, 30, DA]); o_ca_s = dout("o_ca_s", [L, NS, 30, DA])
    o_cv_p = dout("o_cv_p", [L, 128, DA]); o_cv_s = dout("o_cv_s", [L, NS, DA])
    o_re_p = dout("o_re_p", [L, 16, 128]); o_im_p = dout("o_im_p", [L, 16, 128])
    o_re_s = dout("o_re_s", [L, NS, 16, 128]); o_im_s = dout("o_im_s", [L, NS, 16, 128])
    o_cf_p = dout("o_cf_p", [L, 2, 2 * DFF]); o_cf_s = dout("o_cf_s", [L, NS, 2, 2 * DFF])

    S = Sched(nc, es)
    pe, act, dve, pool = S.pe, S.act, S.dve, S.pool
    V = nc.vector; A = nc.scalar; G = nc.gpsimd; PE = nc.tensor

    def sbt(name, shape, dt=F32):
        return es.enter_context(nc.sbuf_tensor(name, list(shape), dt)).ap()

    T = Tile

    cstt = T(sbt("cstt", [128, 256]))
    ident = cstt.ap[:, 0:128]
    tril = cstt.ap[:, 128:256]
    cstr = T(sbt("cstr", [128, 384], BF))
    ones_d = cstr.ap[:, 0:128]
    ones_c = cstr.ap[:, 128:256]
    ones_row = cstr.ap[0:1, 256:384]
    xbig = sbt("xT", [128, 8, TT]); x = [T(xbig[:, f, :]) for f in range(8)]
    hbig = sbt("hT", [128, 8, TT], BF); h = [T(hbig[:, f, :]) for f in range(8)]
    wstg_ap = sbt("wstg", [128, NSLOT, SLOT])
    wstg = [T(wstg_ap[:, i, :]) for i in range(NSLOT)]
    wring_ap = sbt("wring", [128, NSLOT, SLOT], BF)
    wslots = [T(wring_ap[:, i, :]) for i in range(NSLOT)]
    psb = [Tile(es.enter_context(nc.psum_tensor("ps%d" % i, [128, 512], F32)).ap(), psum=True) for i in range(8)]
    P = Pool_(psb)
    NSCR = 8
    scr_ap = sbt("scr", [128, NSCR, TT])
    scr = Ring([T(scr_ap[:, i, :]) for i in range(NSCR)])
    scr_r_ap = sbt("scrr", [128, 3, TT], BF)
    scr_r = Ring([T(scr_r_ap[:, i, :]) for i in range(3)])
    sm_ap = sbt("small", [128, 16, 8])
    small = Ring([T(sm_ap[:, i, :]) for i in range(16)])

    par = []
    for l in range(L):
        p = {}
        p["gains"] = T(sbt("gains%d" % l, [128, 3, 8]))
        p["caw"] = T(sbt("caw%d" % l, [128, 4, 31]))
        p["avec"] = T(sbt("avec%d" % l, [128, 4, 4]))
        p["lnb"] = T(sbt("lnb%d" % l, [128, 2, DA]))
        p["WsT"] = T(sbt("WsT%d" % l, [128, 4, 128], BF))
        p["bs1"] = T(sbt("bs1%d" % l, [1, 4, 128], BF))
        p["bs0"] = T(sbt("bs0%d" % l, [128, 4]))
        p["w00"] = T(sbt("w00%d" % l, [128, 4]))
        p["Wsm"] = T(sbt("Wsm%d" % l, [16, 4, 16], BF))
        p["BbRe"] = T(sbt("BbRe%d" % l, [128, 4, 128], BF))
        p["BbIm"] = T(sbt("BbIm%d" % l, [128, 4, 128], BF))
        p["CTre"] = T(sbt("CTre%d" % l, [128, 4, 128]))
        p["CTim"] = T(sbt("CTim%d" % l, [128, 4, 128]))
        p["HSC"] = T(sbt("HSC%d" % l, [128, 1, 3, 16]))
        p["UPH"] = T(sbt("UPH%d" % l, [128, 3, 16]))
        p["RHO"] = T(sbt("RHO%d" % l, [128, 16]))
        p["cf"] = T(sbt("cf%d" % l, [128, 44, 4]))
        p["carA"] = T(sbt("carA%d" % l, [128, 4, 30], BF))
        p["carF"] = T(sbt("carF%d" % l, [128, 44, 2]))
        p["carS"] = T(sbt("carS%d" % l, [128, 2, 16]))
        par.append(p)
    gfin = T(sbt("gfin", [128, 8]))
    dg_ap = sbt("dg", [128, 2, 31, 128], BF)
    dg = [T(dg_ap[:, i, :, :]) for i in range(2)]
    alast = T(sbt("alast", [128, 4, 32]))
    tabring_ap = sbt("tabring", [128, 3, 1024])
    tabring = [T(tabring_ap[:, i, :]) for i in range(3)]
    tab_d = nc.dram_tensor("tab_d", [L, 16, 128, 1024], F32, kind="Internal").ap()
    Cpad_re_ap = sbt("Cpad_re", [128, 2, 4, 128], BF); Cpad_re = [T(Cpad_re_ap[:, i, :, :]) for i in range(2)]
    Cpad_im_ap = sbt("Cpad_im", [128, 2, 4, 128], BF); Cpad_im = [T(Cpad_im_ap[:, i, :, :]) for i in range(2)]

    ARR = 16512
    ARF = 6272 + 2048
    arenaR = sbt("arenaR", [128, ARR], BF)
    arenaF = sbt("arenaF", [128, ARF])
    offR = [0]; offF = [0]

    def cR(n):
        a = arenaR[:, offR[0]:offR[0] + n]; offR[0] += n
        assert offR[0] <= ARR, offR[0]
        return a

    def cF(n):
        a = arenaF[:, offF[0]:offF[0] + n]; offF[0] += n
        assert offF[0] <= ARF, offF[0]
        return a

    m = [T(cR(TT)) for f in range(8)]
    acs = [T(cR(TT)) for c in range(4)]
    ub = [T(cR(TT)) for c in range(4)]
    vtok = [T(cR(DA)) for r in range(4)]
    zc = [T(cR(TT)) for c in range(4)]
    hsF2 = [[T(cR(TT)) for _ in range(2)] for _ in range(2)]
    hsF = hsF2[0]
    aext = [T(cR(544)) for c in range(4)]
    hsAB = [T(cF(TT)) for _ in range(4)]
    hsCD = [T(cF(TT)) for _ in range(4)]
    hsA = hsAB[0:2]; hsB = hsAB[2:4]
    xstage = [T(cF(D)) for _ in range(2)]
    mixer_tiles = m + acs + ub + vtok + zc + hsF2[0] + hsF2[1] + aext + hsAB + hsCD + xstage
    offR[0] = 0; offF[0] = 0
    actt = [T(cR(TT)) for j in range(NJ)]
    pT = [T(cR(TT)) for _ in range(2)]
    eg = [T(cF(520)) for _ in range(2)]
    ev = [T(cF(520)) for _ in range(2)]
    pesb_off = offF[0]
    pesb = [T(cF(TT)) for f in range(8)]
    ffn_tiles = actt + pT + eg + ev + pesb
    offF[0] = 0
    yT = [T(cF(TT)) for f in range(8)]
    ystage = [T(cF(D)) for _ in range(2)]
    fin_tiles = yT + ystage
    offF[0] = 0
    prep_ws = T(cF(512)); prep_x = [T(cF(512)) for _ in range(2)]
    prep_cst = [T(cF(512)) for _ in range(2)]
    pv = {}
    for nm in ["are", "aim", "ldt", "dt", "zr", "th", "p", "er", "c", "s", "t1", "t2", "abr", "abi", "pp", "den", "cfr", "cfi", "ncfi"]:
        pv[nm] = T(cF(16))
    pB = [T(cF(256)) for _ in range(2)]
    pBb = [T(cF(256)) for _ in range(2)]
    prep_stg = T(cF(1024))
    uph = T(cF(NSTEP * 3 * 16))
    prep_tiles = [prep_ws] + prep_x + prep_cst + list(pv.values()) + pB + pBb + [prep_stg, uph]

    def mm(ps_ap, pairs, reads, writes, tp=None):
        def fn():
            n = len(pairs)
            ins = None
            for i, (lt, rh) in enumerate(pairs):
                kw = {}
                if tp is not None:
                    kw["tile_position"] = tp
                ins = PE.matmul(ps_ap, lhsT=lt, rhs=rh, start=(i == 0), stop=(i == n - 1), **kw)
            return ins
        S.op(pe, fn, reads=reads, writes=writes)

    cp_flip = [0]

    def evac(out_ap, in_ap, reads, writes, eng=None):
        if eng is None:
            cp_flip[0] ^= 1
            eng = act if cp_flip[0] else dve
        if eng is act:
            S.op(act, lambda: A.copy(out=out_ap, in_=in_ap), reads=reads, writes=writes)
        elif eng is dve:
            S.op(dve, lambda: V.tensor_copy(out=out_ap, in_=in_ap), reads=reads, writes=writes)
        else:
            S.op(pool, lambda: G.tensor_copy(out=out_ap, in_=in_ap), reads=reads, writes=writes)

    def tt_op(E, out_ap, a, b, op, reads, writes):
        e = V if E is dve else G
        S.op(E, lambda: e.tensor_tensor(out=out_ap, in0=a, in1=b, op=op), reads=reads, writes=writes)

    def ts_op(E, out_ap, a, s1, s2, op0, op1, reads, writes):
        e = V if E is dve else G
        if op1 is None:
            S.op(E, lambda: e.tensor_scalar(out=out_ap, in0=a, scalar1=s1, scalar2=None, op0=op0), reads=reads, writes=writes)
        else:
            S.op(E, lambda: e.tensor_scalar(out=out_ap, in0=a, scalar1=s1, scalar2=s2, op0=op0, op1=op1), reads=reads, writes=writes)

    def stt(out_ap, a, s, b, op0, op1, reads, writes):
        S.op(dve, lambda: V.scalar_tensor_tensor(out=out_ap, in0=a, scalar=s, in1=b, op0=op0, op1=op1), reads=reads, writes=writes)

    def actf(out_ap, in_ap, func, reads, writes, bias=None, scale=None, accum=None):
        kw = {}
        if bias is not None:
            kw["bias"] = bias
        if scale is not None:
            kw["scale"] = scale
        if accum is not None:
            kw["accum_out"] = accum
        S.op(act, lambda: A.activation(out=out_ap, in_=in_ap, func=func, **kw), reads=reads, writes=writes)

    def transpose(ps_ap, in_ap, npart, reads, writes):
        S.op(pe, lambda: PE.transpose(ps_ap, in_ap, ident[:npart, :npart]), reads=list(reads) + [cstt], writes=writes)

    def memset(t, ap, val=0.0):
        S.op(pool, lambda: G.memset(ap, val), reads=[], writes=[t])

    tiles_order = [("p", i) for i in range(4)] + [("s", 0)]
    NL = L
    if DBG is not None:
        tiles_order = DBG["tiles"]
        NL = DBG["layers"]
    gchunks = []
    gcidx = []
    for _t in tiles_order:
        for l in range(NL):
            lc = layer_chunks(l)
            gchunks += lc
            gcidx += [(l, i) for i in range(len(lc))]

    def dbg(name, ap, reads):
        if DBG is None:
            return
        shp = list(ap.shape)
        d = nc.dram_tensor("dbg_" + name, shp, ap.dtype, kind="ExternalOutput").ap()
        S.dma(d, ap, reads=reads, st=None)
    wstate = {"next_load": 0, "next_use": 0, "next_cast": 0}

    def w_load(idx):
        chunk = gchunks[idx]
        stg = wstg[idx % NSLOT]
        o = sum(nkt * nc_ for (wn, l, r0, nkt, c0, nc_) in chunk)
        assert o <= SLOT
        l, ci = gcidx[idx]
        S.dma(stg.ap[:, 0:o], wpack[l, ci, :, 0:o], writes=[stg], st=stg, track=False)
        return o

    wsize = {}

    def w_cast(idx, eng):
        n = wsize[idx]
        stg = wstg[idx % NSLOT]; slot = wslots[idx % NSLOT]
        evac(slot.ap[:, 0:n], stg.ap[:, 0:n], [stg], [slot], eng=eng)

    def w_next(cast_eng=None):
        cast_eng = cast_eng or act
        idx = wstate["next_use"]
        n = len(gchunks)
        if idx == 0:
            for k in range(min(NSLOT, n)):
                wsize[k] = w_load(k)
            wstate["next_load"] = min(NSLOT, n)
            w_cast(0, cast_eng)
            wstate["next_cast"] = 1
        if wstate["next_cast"] <= idx + 1 and wstate["next_cast"] < n:
            k = wstate["next_cast"]
            w_cast(k, cast_eng)
            wstate["next_cast"] = k + 1
        while wstate["next_load"] < n and wstate["next_load"] - NSLOT < wstate["next_cast"]:
            k = wstate["next_load"]
            wsize[k] = w_load(k)
            wstate["next_load"] += 1
        wstate["next_use"] += 1
        slot = wslots[idx % NSLOT]
        views = []
        o = 0
        for (wn, l, r0, nkt, c0, nc_) in gchunks[idx]:
            views.append(slot.ap[:, o:o + nkt * nc_].rearrange("p (k c) -> p k c", k=nkt))
            o += nkt * nc_
        return slot, views

    S.dma(cstt.ap, cst, writes=[cstt], st=None, track=False)
    cst_stg = scr.get()
    S.dma(cst_stg.ap[:, 0:384], cst_r, writes=[cst_stg], st=None, track=False)
    evac(cstr.ap, cst_stg.ap[:, 0:384], [cst_stg], [cstr], eng=dve)
    nonc = nc.allow_non_contiguous_dma(reason="small param loads")
    nonc.__enter__()

    def pdma(dst_tile, dst_ap, src_ap):
        S.dma(dst_ap, src_ap, writes=[dst_tile], st=None, track=False)

    def featvec(dst_tile, dst_ap, src_vec):
        pdma(dst_tile, dst_ap, src_vec.rearrange("(f p) -> p f", p=128))

    def v16(nm):
        return pv[nm].ap

    for l in range(L):
        p = par[l]
        featvec(p["gains"], p["gains"].ap[:, 0, :], g_mix[l])
        featvec(p["gains"], p["gains"].ap[:, 1, :], g_ffn[l])
        featvec(p["gains"], p["gains"].ap[:, 2, :], g_ple[l])
        for i, v_ in enumerate([conv_a_b, ln_a_g, ln_a_b, ssm_d]):
            featvec(p["avec"], p["avec"].ap[:, i, :], v_[l])
        pdma(p["lnb"], p["lnb"].ap[:, 0, :], ln_b_g[l].partition_broadcast(128))
        pdma(p["lnb"], p["lnb"].ap[:, 1, :], ln_b_b[l].partition_broadcast(128))
        bstg = small.get()
        bs_stg = scr.get()
        pdma(bs_stg, bs_stg.ap[0:1, 0:512], b_s_r[l:l + 1, :, :].rearrange("o g i -> o (g i)"))
        evac(p["bs1"].ap[0:1, :, :].rearrange("o g i -> o (g i)"), bs_stg.ap[0:1, 0:512], [bs_stg], [p["bs1"]], eng=dve)
        pdma(p["bs0"], p["bs0"].ap, b_s[l, :, 0].partition_broadcast(128))
        pdma(p["w00"], p["w00"].ap, w_s[l, :, 0, 0].partition_broadcast(128))
        pdma(prep_stg, prep_stg.ap[:31, 0:512], conv_a_w[l])
        pst = P.get()
        for c in range(4):
            transpose(pst.ap[:, 32 * c:32 * c + 31], prep_stg.ap[:31, 128 * c:128 * (c + 1)], 31, [prep_stg], [pst])
        evac(p["caw"].ap, pst.ap[:, 0:128].rearrange("p (c k) -> p c k", k=32)[:, :, 0:31], [pst], [p["caw"]])
        P.put(pst)
        for j4 in range(11):
            stg = scr.get()
            pdma(stg, stg.ap[0:3, :], conv_f_w[l][:, 512 * j4:512 * (j4 + 1)])
            pdma(stg, stg.ap[3:4, :], conv_f_b[l][:, 512 * j4:512 * (j4 + 1)])
            pst = P.get()
            for jj in range(4):
                transpose(pst.ap[:, 4 * jj:4 * jj + 4], stg.ap[0:4, 128 * jj:128 * (jj + 1)], 4, [stg], [pst])
            evac(p["cf"].ap[:, 4 * j4:4 * j4 + 4, :], pst.ap[:, 0:16].rearrange("p (j k) -> p j k", k=4), [pst], [p["cf"]])
            P.put(pst)
        pdma(prep_ws, prep_ws.ap.rearrange("p (g j) -> p g j", g=4), w_s[l].rearrange("g i j -> i g j"))
        for g in range(4):
            tt_op(dve, prep_ws.ap[:, g * 128:(g + 1) * 128], prep_ws.ap[:, g * 128:(g + 1) * 128], tril, ALU.mult,
                  reads=[prep_ws, cstt], writes=[prep_ws])
        pst = P.get()
        for g in range(4):
            transpose(pst.ap[:, g * 128:(g + 1) * 128], prep_ws.ap[:, g * 128:(g + 1) * 128], 128, [prep_ws], [pst])
        evac(p["WsT"].ap.rearrange("p g i -> p (g i)"), pst.ap, [pst], [p["WsT"]])
        P.put(pst)
        for g in range(4):
            ts_op(dve, p["Wsm"].ap[:, g, :], ident[:16, :16], p["w00"].ap[:16, g:g + 1], None, ALU.mult, None,
                  reads=[cstt, p["w00"]], writes=[p["Wsm"]])
        for gl in range(2):
            sl = slice(64 * gl, 64 * gl + 64)
            pdma(pv["are"], v16("are")[sl, :], ssm_a_re[l].rearrange("(q g) n -> g n q", g=2)[gl])
            pdma(pv["aim"], v16("aim")[sl, :], ssm_a_im[l].rearrange("(q g) n -> g n q", g=2)[gl])
            pdma(pv["ldt"], v16("ldt")[sl, :], ssm_log_dt[l].rearrange("(q g) -> g q", g=2)[gl].partition_broadcast(64))
            for ri, src in enumerate([ssm_b_re, ssm_b_im]):
                pdma(pB[ri], pB[ri].ap[sl, :].rearrange("p (q h) -> p q h", q=16),
                     src[l].rearrange("(q g) n h -> g n q h", g=2)[gl])
        actf(v16("dt"), v16("ldt"), AF.Exp, [pv["ldt"]], [pv["dt"]])
        tt_op(dve, v16("zr"), v16("are"), v16("dt"), ALU.mult, [pv["are"], pv["dt"]], [pv["zr"]])
        tt_op(dve, v16("th"), v16("aim"), v16("dt"), ALU.mult, [pv["aim"], pv["dt"]], [pv["th"]])
        ts_op(dve, v16("p"), v16("zr"), 1.0 / 6.0, 1.0, ALU.mult, ALU.add, [pv["zr"]], [pv["p"]])
        for k in [5.0, 4.0, 3.0, 2.0]:
            tt_op(dve, v16("p"), v16("p"), v16("zr"), ALU.mult, [pv["p"], pv["zr"]], [pv["p"]])
            ts_op(dve, v16("p"), v16("p"), 1.0 / k, 1.0, ALU.mult, ALU.add, [pv["p"]], [pv["p"]])
        tt_op(dve, v16("p"), v16("p"), v16("zr"), ALU.mult, [pv["p"], pv["zr"]], [pv["p"]])
        ts_op(dve, v16("er"), v16("p"), 1.0, None, ALU.add, None, [pv["p"]], [pv["er"]])
        actf(v16("s"), v16("th"), AF.Sin, [pv["th"]], [pv["s"]], scale=1.0 / 16.0)
        ts_op(dve, v16("t1"), v16("th"), 1.0 / 16.0, float(np.pi / 2), ALU.mult, ALU.add, [pv["th"]], [pv["t1"]])
        actf(v16("c"), v16("t1"), AF.Sin, [pv["t1"]], [pv["c"]])
        for _ in range(4):
            tt_op(dve, v16("t1"), v16("c"), v16("c"), ALU.mult, [pv["c"]], [pv["t1"]])
            tt_op(dve, v16("t2"), v16("s"), v16("s"), ALU.mult, [pv["s"]], [pv["t2"]])
            tt_op(dve, v16("s"), v16("s"), v16("c"), ALU.mult, [pv["s"], pv["c"]], [pv["s"]])
            ts_op(dve, v16("s"), v16("s"), 2.0, None, ALU.mult, None, [pv["s"]], [pv["s"]])
            tt_op(dve, v16("c"), v16("t1"), v16("t2"), ALU.subtract, [pv["t1"], pv["t2"]], [pv["c"]])
        tt_op(dve, v16("abr"), v16("er"), v16("c"), ALU.mult, [pv["er"], pv["c"]], [pv["abr"]])
        tt_op(dve, v16("abi"), v16("er"), v16("s"), ALU.mult, [pv["er"], pv["s"]], [pv["abi"]])
        H = p["HSC"]
        evac(H.ap[:, 0, 0, :], v16("abr"), [pv["abr"]], [H], eng=dve)
        evac(H.ap[:, 0, 1, :], v16("abi"), [pv["abi"]], [H], eng=dve)
        ts_op(dve, H.ap[:, 0, 2, :], v16("abi"), -1.0, None, ALU.mult, None, [pv["abi"]], [H])
        evac(p["RHO"].ap, v16("er"), [pv["er"]], [p["RHO"]], eng=dve)
        Uv = uph.ap.rearrange("p (k c q) -> p k c q", k=NSTEP, c=3)
        evac(Uv[:, 0, 0, :], v16("c"), [pv["c"]], [uph], eng=dve)
        evac(Uv[:, 0, 1, :], v16("s"), [pv["s"]], [uph], eng=dve)
        for k in range(1, NSTEP):
            tt_op(dve, v16("t1"), Uv[:, k - 1, 0, :], Uv[:, k - 1, 0, :], ALU.mult, [uph], [pv["t1"]])
            tt_op(dve, v16("t2"), Uv[:, k - 1, 1, :], Uv[:, k - 1, 1, :], ALU.mult, [uph], [pv["t2"]])
            tt_op(dve, Uv[:, k, 0, :], v16("t1"), v16("t2"), ALU.subtract, [pv["t1"], pv["t2"]], [uph])
            tt_op(dve, v16("t1"), Uv[:, k - 1, 0, :], Uv[:, k - 1, 1, :], ALU.mult, [uph], [pv["t1"]])
            ts_op(dve, Uv[:, k, 1, :], v16("t1"), 2.0, None, ALU.mult, None, [pv["t1"]], [uph])
        for k in range(NSTEP):
            ts_op(dve, Uv[:, k, 2, :], Uv[:, k, 1, :], -1.0, None, ALU.mult, None, [uph], [uph])
        evac(p["UPH"].ap, Uv[:, 0, :, :], [uph], [p["UPH"]], eng=dve)
        Cr = prep_x[0].ap.rearrange("p (q r) -> p q r", q=16); Sr = prep_x[1].ap.rearrange("p (q r) -> p q r", q=16)
        Bc = pBb[0].ap.rearrange("p (q r) -> p q r", q=16); Bs = pBb[1].ap.rearrange("p (q r) -> p q r", q=16)
        T1 = prep_cst[0].ap.rearrange("p (q r) -> p q r", q=16); T2 = prep_cst[1].ap.rearrange("p (q r) -> p q r", q=16)
        tset = [prep_x[0], prep_x[1], pBb[0], pBb[1], prep_cst[0], prep_cst[1], uph]

        def dbl(Ctab, Stab, k0, nsteps):
            S.op(dve, lambda: V.memset(Ctab[:, :, 0:1], 1.0), reads=[], writes=tset)
            S.op(dve, lambda: V.memset(Stab[:, :, 0:1], 0.0), reads=[], writes=tset)
            for i in range(nsteps):
                d = 1 << i
                ckb = Uv[:, k0 + i, 0, :].unsqueeze(2).to_broadcast([128, 16, d])
                skb = Uv[:, k0 + i, 1, :].unsqueeze(2).to_broadcast([128, 16, d])
                tt_op(dve, T1[:, :, 0:d], Ctab[:, :, 0:d], ckb, ALU.mult, tset, tset)
                tt_op(dve, T2[:, :, 0:d], Stab[:, :, 0:d], skb, ALU.mult, tset, tset)
                tt_op(dve, Ctab[:, :, d:2 * d], T1[:, :, 0:d], T2[:, :, 0:d], ALU.subtract, tset, tset)
                tt_op(dve, T1[:, :, 0:d], Stab[:, :, 0:d], ckb, ALU.mult, tset, tset)
                tt_op(dve, T2[:, :, 0:d], Ctab[:, :, 0:d], skb, ALU.mult, tset, tset)
                tt_op(dve, Stab[:, :, d:2 * d], T1[:, :, 0:d], T2[:, :, 0:d], ALU.add, tset, tset)
        dbl(Cr, Sr, 0, 5)
        dbl(Bc[:, :, 0:16], Bs[:, :, 0:16], 5, 4)
        for q in range(16):
            def bm(tab):
                return tab[:, q, 0:16].unsqueeze(2).to_broadcast([128, 16, 32])

            def br(tab):
                return tab[:, q, :].unsqueeze(1).to_broadcast([128, 16, 32])
            c1 = scr.get(); c2 = scr.get(); s1_ = scr.get(); s2_ = scr.get()

            def v3(t):
                return t.ap.rearrange("p (m r) -> p m r", m=16)
            tt_op(dve, v3(c1), bm(Bc), br(Cr), ALU.mult, tset, [c1])
            tt_op(dve, v3(c2), bm(Bs), br(Sr), ALU.mult, tset, [c2])
            tt_op(dve, c1.ap, c1.ap, c2.ap, ALU.subtract, [c1, c2], [c1])
            tt_op(pool, v3(s1_), bm(Bs), br(Cr), ALU.mult, tset, [s1_])
            tt_op(pool, v3(s2_), bm(Bc), br(Sr), ALU.mult, tset, [s2_])
            tt_op(pool, s1_.ap, s1_.ap, s2_.ap, ALU.add, [s1_, s2_], [s1_])
            S.dma(tab_d[l, q, :, 0:512], c1.ap, reads=[c1], st=None, track=False)
            S.dma(tab_d[l, q, :, 512:1024], s1_.ap, reads=[s1_], st=None, track=False)
        ts_op(dve, v16("pp"), v16("abr"), -1.0, None, ALU.add, None, [pv["abr"]], [pv["pp"]])
        tt_op(dve, v16("t1"), v16("are"), v16("are"), ALU.mult, [pv["are"]], [pv["t1"]])
        tt_op(dve, v16("t2"), v16("aim"), v16("aim"), ALU.mult, [pv["aim"]], [pv["t2"]])
        tt_op(dve, v16("den"), v16("t1"), v16("t2"), ALU.add, [pv["t1"], pv["t2"]], [pv["den"]])
        S.op(dve, lambda: V.reciprocal(out=v16("den"), in_=v16("den")), reads=[pv["den"]], writes=[pv["den"]])
        tt_op(dve, v16("t1"), v16("pp"), v16("are"), ALU.mult, [pv["pp"], pv["are"]], [pv["t1"]])
        tt_op(dve, v16("t2"), v16("abi"), v16("aim"), ALU.mult, [pv["abi"], pv["aim"]], [pv["t2"]])
        tt_op(dve, v16("t1"), v16("t1"), v16("t2"), ALU.add, [pv["t1"], pv["t2"]], [pv["t1"]])
        tt_op(dve, v16("cfr"), v16("t1"), v16("den"), ALU.mult, [pv["t1"], pv["den"]], [pv["cfr"]])
        tt_op(dve, v16("t1"), v16("abi"), v16("are"), ALU.mult, [pv["abi"], pv["are"]], [pv["t1"]])
        tt_op(dve, v16("t2"), v16("pp"), v16("aim"), ALU.mult, [pv["pp"], pv["aim"]], [pv["t2"]])
        tt_op(dve, v16("t1"), v16("t1"), v16("t2"), ALU.subtract, [pv["t1"], pv["t2"]], [pv["t1"]])
        tt_op(dve, v16("cfi"), v16("t1"), v16("den"), ALU.mult, [pv["t1"], pv["den"]], [pv["cfi"]])
        ts_op(dve, v16("ncfi"), v16("cfi"), -1.0, None, ALU.mult, None, [pv["cfi"]], [pv["ncfi"]])
        for q in range(16):
            bq = slice(16 * q, 16 * q + 16)
            ts_op(dve, pBb[0].ap[:, bq], pB[0].ap[:, bq], v16("cfr")[:, q:q + 1], None, ALU.mult, None,
                  [pB[0], pv["cfr"]], [pBb[0]])
            stt(pBb[0].ap[:, bq], pB[1].ap[:, bq], v16("ncfi")[:, q:q + 1], pBb[0].ap[:, bq], ALU.mult, ALU.add,
                [pB[1], pv["ncfi"], pBb[0]], [pBb[0]])
            ts_op(dve, pBb[1].ap[:, bq], pB[1].ap[:, bq], v16("cfr")[:, q:q + 1], None, ALU.mult, None,
                  [pB[1], pv["cfr"]], [pBb[1]])
            stt(pBb[1].ap[:, bq], pB[0].ap[:, bq], v16("cfi")[:, q:q + 1], pBb[1].ap[:, bq], ALU.mult, ALU.add,
                [pB[0], pv["cfi"], pBb[1]], [pBb[1]])
        for ri in range(2):
            X = prep_x[ri]
            memset(X, X.ap)
            Xv = X.ap.rearrange("p (q g h) -> p q g h", q=16, g=2)
            Bv = pBb[ri].ap.rearrange("p (q h) -> p q h", q=16)
            evac(Xv[0:64, :, 0, :], Bv[0:64, :, :], [pBb[ri]], [X], eng=dve)
            evac(Xv[64:128, :, 1, :], Bv[64:128, :, :], [pBb[ri]], [X], eng=dve)
            pst = P.get()
            for c in range(4):
                transpose(pst.ap[:, c * 128:(c + 1) * 128], X.ap[:, c * 128:(c + 1) * 128], 128, [X], [pst])
            dstT = p["BbRe"] if ri == 0 else p["BbIm"]
            evac(dstT.ap.rearrange("p c m -> p (c m)"), pst.ap, [pst], [dstT])
            P.put(pst)
        for ri, src in enumerate([ssm_c_re, ssm_c_im]):
            Cs = prep_cst[ri]
            memset(Cs, Cs.ap)
            Cv = Cs.ap.rearrange("p (c m) -> p c m", c=4)
            for c in range(4):
                for gi in range(8):
                    S.dma(Cv[16 * gi:16 * gi + 16, c, 64 * (gi % 2):64 * (gi % 2) + 64], src[l, 8 * c + gi],
                          writes=[Cs], st=Cs, track=False, group=True)
            pst = P.get()
            for c in range(4):
                transpose(pst.ap[:, c * 128:(c + 1) * 128], Cs.ap[:, c * 128:(c + 1) * 128], 128, [Cs], [pst])
            if ri == 0:
                evac(p["CTre"].ap.rearrange("p c m -> p (c m)"), pst.ap, [pst], [p["CTre"]], eng=dve)
            else:
                ts_op(dve, p["CTim"].ap.rearrange("p c m -> p (c m)"), pst.ap, -1.0, None, ALU.mult, None, [pst], [p["CTim"]])
            P.put(pst)
    featvec(gfin, gfin.ap, g_final)
    nonc.__exit__(None, None, None)
    for i in range(2):
        memset(Cpad_re[i], Cpad_re[i].ap)
        memset(Cpad_im[i], Cpad_im[i].ap)

    def load_cpad(l, c):
        i = c % 2
        for j in range(4):
            evac(Cpad_re[i].ap[:, j, 32 * j:32 * j + 32], par[l]["CTre"].ap[:, c, 32 * j:32 * j + 32],
                 [par[l]["CTre"]], [Cpad_re[i]], eng=pool)
            evac(Cpad_im[i].ap[:, j, 32 * j:32 * j + 32], par[l]["CTim"].ap[:, c, 32 * j:32 * j + 32],
                 [par[l]["CTim"]], [Cpad_im[i]], eng=pool)

    def S5_ENG():
        return dve if S5_USE_DVE else pool

    tabstate = {"n": 0, "drained": False}

    def tab_load(l, q):
        if not tabstate["drained"]:
            S.drain_pool(S.sp)
            tabstate["drained"] = True
        slot = tabring[tabstate["n"] % 3]
        tabstate["n"] += 1
        S.dma(slot.ap, tab_d[l, q], writes=[slot], st=slot, track=False)
        return slot

    def rmsnorm(gcol_tile, gcol, dst, NT):
        pss = P.get()
        for f in range(8):
            sq = scr_r.get()
            actf(sq.ap[:, :NT], x[f].ap[:, :NT], AF.Square, [x[f]], [sq])
            S.op(pe, lambda: PE.matmul(pss.ap[:, :NT], lhsT=ones_d, rhs=sq.ap[:, :NT], start=(f == 0), stop=(f == 7)),
                 reads=[cstr, sq], writes=[pss])
        sd = scr.get()
        actf(sd.ap[:, :NT], pss.ap[:, :NT], AF.Sqrt, [pss], [sd], bias=EPS)
        P.put(pss)
        S.op(dve, lambda: V.reciprocal(out=sd.ap[:, :NT], in_=sd.ap[:, :NT]), reads=[sd], writes=[sd])
        for f in range(8):
            stt(dst[f].ap[:, :NT], x[f].ap[:, :NT], gcol(f), sd.ap[:, :NT], ALU.mult, ALU.mult,
                [x[f], gcol_tile, sd], [dst[f]])

    def layer(l, kind, ti, NT):
        p = par[l]
        samp = (kind == "s")
        first = (kind == "p" and ti == 0)
        last = (kind == "p" and ti == 3)
        R = 1 if samp else 4
        MT = NS if samp else 128
        gains = p["gains"]
        cf = p["cf"]
        rmsnorm(gains, lambda f: gains.ap[:, 0, f:f + 1], h, NT)
        tg = "%s%dl%d_" % (kind, ti, l)
        dbg(tg + "h0", h[0].ap[:, :NT], [h[0]])

        def a3(c):
            return aext[c].ap[:, 0:496].rearrange("p (b k) -> p b k", k=31)

        def build_dg(c):
            dgt_ = dg[c % 2]
            S.op(pool, lambda: G.tensor_tensor(out=dgt_.ap, in0=ident.unsqueeze(1).to_broadcast([128, 31, 128]),
                                               in1=p["caw"].ap[:, c, :].unsqueeze(2).to_broadcast([128, 31, 128]), op=ALU.mult),
                 reads=[cstt, p["caw"]], writes=[dgt_])
        build_dg(0)
        build_dg(1)

        if samp:
            rows = st_ca[l].rearrange("b k c -> (b k) c")
            for r4 in range(4):
                S.dma(hsAB[r4].ap[:120, :], rows[120 * r4:120 * r4 + 120, :], writes=[hsAB[r4]], st=hsAB[r4])
            for c in range(4):
                pst = P.get()
                for r4 in range(4):
                    transpose(pst.ap[:, r4 * 120:(r4 + 1) * 120], hsAB[r4].ap[:120, c * 128:(c + 1) * 128], 120, [hsAB[r4]], [pst])
                evac(a3(c)[:, :, 0:30], pst.ap[:, 0:480].rearrange("p (b k) -> p b k", k=30), [pst], [aext[c]])
                P.put(pst)
            S.dma(o_ca_s[l, :, 0:29, :], st_ca[l, :, 1:30, :], st=None)
        else:
            for c in range(4):
                if first:
                    memset(aext[c], aext[c].ap[:, 0:30])
                else:
                    evac(aext[c].ap[:, 0:30], p["carA"].ap[:, c, :], [p["carA"]], [aext[c]], eng=pool)
        for c in range(4):
            slot, (vl, vg) = w_next(act if samp else dve)
            psl = P.get(); psg = P.get()
            mm(psl.ap[:, :NT], [(vl[:, kt, :], h[kt].ap[:, :NT]) for kt in range(8)], [slot] + h, [psl])
            mm(psg.ap[:, :NT], [(vg[:, kt, :], h[kt].ap[:, :NT]) for kt in range(8)], [slot] + h, [psg])
            sg = scr.get()
            actf(sg.ap[:, :NT], psg.ap[:, :NT], AF.Sigmoid, [psg], [sg])
            dsta = a3(c)[:, :, 30] if samp else aext[c].ap[:, 30:30 + NT]
            tt_op(dve, dsta, psl.ap[:, :NT], sg.ap[:, :NT], ALU.mult, [psl, sg], [aext[c]])
            if samp:
                tt_op(dve, alast.ap[:, c, 0:NS], psl.ap[:, :NS], sg.ap[:, :NS], ALU.mult, [psl, sg], [alast])
            elif last:
                tt_op(dve, alast.ap[:, c, 0:30], psl.ap[:, NT - 30:NT], sg.ap[:, NT - 30:NT], ALU.mult, [psl, sg], [alast])
            P.put(psl, psg)
        if samp or last:
            pst = P.get()
            nr = NS if samp else 30
            for c in range(4):
                transpose(pst.ap[:nr, c * 128:(c + 1) * 128], alast.ap[:, c, 0:nr], 128, [alast], [pst])
            so = scr.get()
            evac(so.ap[:nr, :], pst.ap[:nr, :], [pst], [so])
            P.put(pst)
            if samp:
                S.dma(o_ca_s[l, :, 29, :], so.ap[:NS, :], reads=[so], st=so)
            else:
                S.dma(o_ca_p[l], so.ap[:30, :], reads=[so], st=so)
        if not samp and not last:
            for c in range(4):
                evac(p["carA"].ap[:, c, :], aext[c].ap[:, NT:NT + 30], [aext[c]], [p["carA"]], eng=pool)
        accs = hsAB
        for c in range(4):
            acc = accs[c]
            dgt = dg[c % 2]
            if c >= 2:
                build_dg(c)

            def tap(k):
                return a3(c)[:, :, k] if samp else aext[c].ap[:, k:k + NT]
            psc = P.get()
            mm(psc.ap[:, :NT], [(dgt.ap[:, k, :], tap(k)) for k in range(31)], [dgt, aext[c]], [psc])
            actf(acc.ap[:, :NT], psc.ap[:, :NT], AF.Identity, [psc, p["avec"]], [acc], bias=p["avec"].ap[:, 0, c:c + 1])
            P.put(psc)
        for c2 in range(2):
            slot, (vu,) = w_next(act if samp else dve)
            for cc in range(2):
                c = 2 * c2 + cc
                psu = P.get()
                mm(psu.ap[:, :NT], [(vu[:, kt, cc * 128:(cc + 1) * 128], h[kt].ap[:, :NT]) for kt in range(8)], [slot] + h, [psu])
                evac(ub[c].ap[:, :NT], psu.ap[:, :NT], [psu], [ub[c]])
                P.put(psu)
        psv = [P.get() for r in range(R)]
        for kh in range(2):
            slot, (vv,) = w_next(act if samp else dve)
            for r in range(R):
                def fn():
                    ins = None
                    for kk in range(4):
                        kt = 4 * kh + kk
                        ins = PE.matmul(psv[r].ap[:MT, :], lhsT=h[kt].ap[:, r * 128:r * 128 + MT], rhs=vv[:, kk, :],
                                        start=(kh == 0 and kk == 0), stop=(kh == 1 and kk == 3))
                    return ins
                S.op(pe, fn, reads=[slot] + h, writes=[psv[r]])
        lnb = p["lnb"]
        for r in range(R):
            st1 = small.get()
            vs = scr.get()
            actf(vs.ap[:MT, :], psv[r].ap[:MT, :], AF.Identity, [psv[r]], [vs, st1], accum=st1.ap[:MT, 0:1])
            junk = scr.get()
            actf(junk.ap[:MT, :], psv[r].ap[:MT, :], AF.Square, [psv[r]], [junk, st1], accum=st1.ap[:MT, 1:2])
            P.put(psv[r])
            ts_op(dve, st1.ap[:MT, 2:3], st1.ap[:MT, 0:1], 1.0 / DA, None, ALU.mult, None, [st1], [st1])
            tt_op(dve, st1.ap[:MT, 3:4], st1.ap[:MT, 2:3], st1.ap[:MT, 2:3], ALU.mult, [st1], [st1])
            stt(st1.ap[:MT, 4:5], st1.ap[:MT, 1:2], 1.0 / DA, st1.ap[:MT, 3:4], ALU.mult, ALU.subtract, [st1], [st1])
            actf(st1.ap[:MT, 5:6], st1.ap[:MT, 4:5], AF.Sqrt, [st1], [st1], bias=EPS)
            S.op(dve, lambda: V.reciprocal(out=st1.ap[:MT, 6:7], in_=st1.ap[:MT, 5:6]), reads=[st1], writes=[st1])
            stt(st1.ap[:MT, 7:8], st1.ap[:MT, 2:3], -1.0, st1.ap[:MT, 6:7], ALU.mult, ALU.mult, [st1], [st1])
            ts_op(dve, vs.ap[:MT, :], vs.ap[:MT, :], st1.ap[:MT, 6:7], st1.ap[:MT, 7:8], ALU.mult, ALU.add, [vs, st1], [vs])
            tt_op(pool, vs.ap[:MT, :], vs.ap[:MT, :], lnb.ap[:MT, 0, :], ALU.mult, [vs, lnb], [vs])
            if samp or (last and r == 3):
                tt_op(pool, vs.ap[:MT, :], vs.ap[:MT, :], lnb.ap[:MT, 1, :], ALU.add, [vs, lnb], [vs])
                evac(vtok[r].ap[:MT, :], vs.ap[:MT, :], [vs], [vtok[r]], eng=act)
                S.dma(o_cv_s[l] if samp else o_cv_p[l], vs.ap[:MT, :], reads=[vs], st=vs)
            else:
                tt_op(pool, vtok[r].ap[:MT, :], vs.ap[:MT, :], lnb.ap[:MT, 1, :], ALU.add, [vs, lnb], [vtok[r]])
        dbg(tg + "vtok0", vtok[0].ap[:MT, :], [vtok[0]])
        for g in range(4):
            psm_ = P.get()
            for r in range(R):
                if samp:
                    S.op(pe, lambda: PE.matmul(psm_.ap[:, :NS], lhsT=vtok[0].ap[:NS, g * 128:(g + 1) * 128],
                                               rhs=p["Wsm"].ap[:NS, g, :], start=True, stop=True),
                         reads=[vtok[0], p["Wsm"]], writes=[psm_])
                else:
                    def fn():
                        PE.matmul(psm_.ap[:, r * 128:(r + 1) * 128], lhsT=vtok[r].ap[:, g * 128:(g + 1) * 128],
                                  rhs=p["WsT"].ap[:, g, :], start=True, stop=False)
                        return PE.matmul(psm_.ap[:, r * 128:(r + 1) * 128], lhsT=ones_row,
                                         rhs=p["bs1"].ap[0:1, g, :], start=False, stop=True)
                    S.op(pe, fn, reads=[vtok[r], p["WsT"], p["bs1"], cstr], writes=[psm_])
            if samp:
                tmp = scr.get()
                ts_op(dve, tmp.ap[:, :NS], psm_.ap[:, :NS], p["bs0"].ap[:, g:g + 1], None, ALU.add, None, [psm_, p["bs0"]], [tmp])
                tt_op(dve, ub[g].ap[:, :NS], ub[g].ap[:, :NS], tmp.ap[:, :NS], ALU.mult, [ub[g], tmp], [ub[g]])
            else:
                tt_op(dve, ub[g].ap[:, :NT], ub[g].ap[:, :NT], psm_.ap[:, :NT], ALU.mult, [ub[g], psm_], [ub[g]])
            P.put(psm_)
        dbg(tg + "ub0", ub[0].ap[:, :NT], [ub[0]])

        for c2 in range(2):
            slot, (vz,) = w_next(act if samp else dve)
            for cc in range(2):
                c = 2 * c2 + cc
                psz = P.get()
                mm(psz.ap[:, :NT], [(vz[:, kt, cc * 128:(cc + 1) * 128], h[kt].ap[:, :NT]) for kt in range(8)], [slot] + h, [psz])
                evac(zc[c].ap[:, :NT], psz.ap[:, :NT], [psz], [zc[c]])
                P.put(psz)
        dbg(tg + "zc0", zc[0].ap[:, :NT], [zc[0]])
        ps1 = P.get(); ps2 = P.get()
        for c in range(4):
            a_r = scr_r.get()
            evac(a_r.ap[:, :NT], accs[c].ap[:, :NT], [accs[c]], [a_r], eng=act)
            S.op(pe, lambda: PE.matmul(ps1.ap[:, :NT], lhsT=ones_c, rhs=a_r.ap[:, :NT], start=(c == 0), stop=(c == 3)),
                 reads=[cstr, a_r], writes=[ps1])
            sq = scr_r.get()
            actf(sq.ap[:, :NT], accs[c].ap[:, :NT], AF.Square, [accs[c]], [sq])
            S.op(pe, lambda: PE.matmul(ps2.ap[:, :NT], lhsT=ones_c, rhs=sq.ap[:, :NT], start=(c == 0), stop=(c == 3)),
                 reads=[cstr, sq], writes=[ps2])
        mean = scr.get()
        evac(mean.ap[:, :NT], ps1.ap[:, :NT], [ps1], [mean], eng=act)
        var = scr.get()
        tt_op(dve, var.ap[:, :NT], mean.ap[:, :NT], mean.ap[:, :NT], ALU.mult, [mean], [var])
        tt_op(dve, var.ap[:, :NT], ps2.ap[:, :NT], var.ap[:, :NT], ALU.subtract, [ps2, var], [var])
        P.put(ps1, ps2)
        actf(var.ap[:, :NT], var.ap[:, :NT], AF.Sqrt, [var], [var], bias=EPS)
        S.op(dve, lambda: V.reciprocal(out=var.ap[:, :NT], in_=var.ap[:, :NT]), reads=[var], writes=[var])
        for c in range(4):
            tt_op(pool, accs[c].ap[:, :NT], accs[c].ap[:, :NT], mean.ap[:, :NT], ALU.subtract, [accs[c], mean], [accs[c]])
            tt_op(dve, accs[c].ap[:, :NT], accs[c].ap[:, :NT], var.ap[:, :NT], ALU.mult, [accs[c], var], [accs[c]])
            actf(acs[c].ap[:, :NT], accs[c].ap[:, :NT], AF.Silu, [accs[c], p["avec"]], [acs[c]],
                 scale=p["avec"].ap[:, 1, c:c + 1], bias=p["avec"].ap[:, 2, c:c + 1])
        dbg(tg + "acs0", acs[0].ap[:, :NT], [acs[0]])

        def merge1_pe(f):
            s1_, (vgA, vgB) = w_next(dve)
            pgA = P.get(); pgB = P.get()
            mm(pgA.ap[:, :NT], [(vgA[:, kt, :], h[kt].ap[:, :NT]) for kt in range(8)], [s1_] + h, [pgA])
            mm(pgB.ap[:, :NT], [(vgB[:, kt, :], h[kt].ap[:, :NT]) for kt in range(8)], [s1_] + h, [pgB])
            s2_, (vao, vbo) = w_next(dve)
            pao = P.get(); pbo = P.get()
            mm(pao.ap[:, :NT], [(vao[:, kt, :], acs[kt].ap[:, :NT]) for kt in range(4)], [s2_] + acs, [pao])
            mm(pbo.ap[:, :NT], [(vbo[:, kt, :], ub[kt].ap[:, :NT]) for kt in range(4)], [s2_] + ub, [pbo])
            sA = scr.get(); sB = scr.get()
            actf(sA.ap[:, :NT], pgA.ap[:, :NT], AF.Sigmoid, [pgA], [sA])
            actf(sB.ap[:, :NT], pgB.ap[:, :NT], AF.Sigmoid, [pgB], [sB])
            P.put(pgA, pgB)
            return (sA, sB, pao, pbo)

        def merge1_rest(f, st_):
            sA, sB, pao, pbo = st_
            tt_op(dve, sA.ap[:, :NT], pao.ap[:, :NT], sA.ap[:, :NT], ALU.mult, [pao, sA], [sA])
            tt_op(dve, sB.ap[:, :NT], pbo.ap[:, :NT], sB.ap[:, :NT], ALU.mult, [pbo, sB], [sB])
            P.put(pao, pbo)
            tt_op(pool, m[f].ap[:, :NT], sA.ap[:, :NT], sB.ap[:, :NT], ALU.add, [sA, sB], [m[f]])

        H = p["HSC"]
        carS = p["carS"]
        s0T = xstage[0]
        if samp:
            for ri, srcst in enumerate([st_re, st_im]):
                for i4 in range(4):
                    S.dma(hsAB[i4].ap[:NS, :], srcst[l][:, 512 * i4:512 * (i4 + 1)], writes=[hsAB[i4]], st=hsAB[i4])
                pst = P.get()
                for q in range(16):
                    transpose(pst.ap[:, 16 * q:16 * q + 16], hsAB[q // 4].ap[:NS, 128 * (q % 4):128 * (q % 4 + 1)], NS,
                              [hsAB[q // 4]], [pst])
                evac(s0T.ap[:, 256 * ri:256 * ri + 256], pst.ap[:, 0:256], [pst], [s0T])
                P.put(pst)
        psy = None
        srow = {}
        pend_y = []

        def flush_y(c_, psy_):
            ysb = scr.get()
            stt(ysb.ap[:, :NT], zc[c_].ap[:, :NT], p["avec"].ap[:, 3, c_:c_ + 1], psy_.ap[:, :NT], ALU.mult, ALU.add,
                [zc[c_], p["avec"], psy_], [ysb])
            P.put(psy_)
            actf(zc[c_].ap[:, :NT], ysb.ap[:, :NT], AF.Gelu_apprx_tanh, [ysb], [zc[c_]])
            if c_ == 0:
                dbg(tg + "gy0", zc[0].ap[:, :NT], [zc[0]])
        tabs = {}
        bu = {}
        if not samp:
            tabs[0] = tab_load(l, 0)
            tabs[1] = tab_load(l, 1)
        for q in range(16):
            c = q // 4; j = q % 4
            if j == 0:
                load_cpad(l, c)
            def emit_bu(qq):
                cc_ = qq // 4; jj_ = qq % 4
                rs_ = slice(32 * jj_, 32 * jj_ + 32)
                a_ = P.get(); b_ = P.get()
                S.op(pe, lambda: PE.matmul(a_.ap[:, :NT], lhsT=p["BbRe"].ap[rs_, cc_, :], rhs=zc[cc_].ap[rs_, :NT], start=True,
                                           stop=True, tile_position=(32 * jj_, 0)), reads=[p["BbRe"], zc[cc_]], writes=[a_])
                S.op(pe, lambda: PE.matmul(b_.ap[:, :NT], lhsT=p["BbIm"].ap[rs_, cc_, :], rhs=zc[cc_].ap[rs_, :NT], start=True,
                                           stop=True, tile_position=(32 * jj_, 0)), reads=[p["BbIm"], zc[cc_]], writes=[b_])
                return a_, b_
            def pre(qq):
                a_, b_ = bu.pop(qq)
                tq = tabs[qq]
                Cq = tq.ap[:, 0:NT]; Sq = tq.ap[:, 512:512 + NT]
                v0, v1, v2, v3 = (hsAB if qq % 2 == 0 else hsCD)
                tt_op(dve, v0.ap[:, :NT], a_.ap[:, :NT], Cq, ALU.mult, [a_, tq], [v0])
                tt_op(dve, v1.ap[:, :NT], b_.ap[:, :NT], Sq, ALU.mult, [b_, tq], [v1])
                tt_op(dve, v2.ap[:, :NT], b_.ap[:, :NT], Cq, ALU.mult, [b_, tq], [v2])
                tt_op(dve, v3.ap[:, :NT], a_.ap[:, :NT], Sq, ALU.mult, [a_, tq], [v3])
                P.put(a_, b_)
            if q == 0:
                bu[0] = emit_bu(0)
                if not samp:
                    pre(0)
            if samp:
                psr, psi = bu.pop(q)
            c1 = H.ap[:, 0, 0, q:q + 1]; s1 = H.ap[:, 0, 1, q:q + 1]; ns1 = H.ap[:, 0, 2, q:q + 1]
            m1st = None
            if (not samp) and q % 2 == 0:
                m1st = merge1_pe(q // 2)
            if samp:
                cur = hsAB[0:2]
                evac(cur[0].ap[:, :NT], psr.ap[:, :NT], [psr], [cur[0]], eng=act)
                evac(cur[1].ap[:, :NT], psi.ap[:, :NT], [psi], [cur[1]], eng=act)
                P.put(psr, psi)
                sre = s0T.ap[:, 0:256].rearrange("p (q b) -> p q b", q=16)[:, q, :]
                sim = s0T.ap[:, 256:512].rearrange("p (q b) -> p q b", q=16)[:, q, :]
                stt(cur[0].ap[:, :NT], sre, c1, cur[0].ap[:, :NT], ALU.mult, ALU.add, [s0T, H, cur[0]], [cur[0]])
                stt(cur[0].ap[:, :NT], sim, ns1, cur[0].ap[:, :NT], ALU.mult, ALU.add, [s0T, H, cur[0]], [cur[0]])
                stt(cur[1].ap[:, :NT], sim, c1, cur[1].ap[:, :NT], ALU.mult, ALU.add, [s0T, H, cur[1]], [cur[1]])
                stt(cur[1].ap[:, :NT], sre, s1, cur[1].ap[:, :NT], ALU.mult, ALU.add, [s0T, H, cur[1]], [cur[1]])
                fin = hsF
                for ri in range(2):
                    evac(hsF[ri].ap[:, :NT], cur[ri].ap[:, :NT], [cur[ri]], [hsF[ri]], eng=act)
                    if j == 0:
                        srow[ri] = P.get()
                    transpose(srow[ri].ap[:NS, 128 * j:128 * (j + 1)], cur[ri].ap[:, :NS], 128, [cur[ri]], [srow[ri]])
                    if j == 3:
                        so = scr.get()
                        evac(so.ap[:NS, :], srow[ri].ap[:NS, :], [srow[ri]], [so])
                        P.put(srow[ri])
                        dsto = (o_re_s if ri == 0 else o_im_s)[l, :, q - 3:q + 1, :]
                        S.dma(dsto, so.ap[:NS, :].rearrange("b (q m) -> b q m", q=4), reads=[so], st=so)
            else:
                tb = tabs[q]
                if q + 2 < 16:
                    tabs[q + 2] = tab_load(l, q + 2)
                Ct = tb.ap[:, 0:NT]; Sn = tb.ap[:, 512:512 + NT]
                w0, w1, w2, w3 = (hsAB if q % 2 == 0 else hsCD)
                hf = hsF2[q % 2]
                tt_op(dve, w0.ap[:, :NT], w0.ap[:, :NT], w1.ap[:, :NT], ALU.add, [w0, w1], [w0])
                tt_op(dve, w2.ap[:, :NT], w2.ap[:, :NT], w3.ap[:, :NT], ALU.subtract, [w2, w3], [w2])
                rho = p["RHO"].ap[:, q:q + 1].to_broadcast([128, NT])
                if first:
                    ini_re = 0.0; ini_im = 0.0
                    ird = []
                else:
                    st0 = small.get()
                    U = p["UPH"]
                    sre = carS.ap[:, 0, q:q + 1]; sim = carS.ap[:, 1, q:q + 1]
                    uc = U.ap[:, 0, q:q + 1]; us = U.ap[:, 1, q:q + 1]; uns = U.ap[:, 2, q:q + 1]
                    ts_op(dve, st0.ap[:, 0:1], sre, uc, None, ALU.mult, None, [carS, U], [st0])
                    stt(st0.ap[:, 0:1], sim, uns, st0.ap[:, 0:1], ALU.mult, ALU.add, [carS, U, st0], [st0])
                    ts_op(dve, st0.ap[:, 1:2], sim, uc, None, ALU.mult, None, [carS, U], [st0])
                    stt(st0.ap[:, 1:2], sre, us, st0.ap[:, 1:2], ALU.mult, ALU.add, [carS, U, st0], [st0])
                    ini_re = st0.ap[:, 0:1]; ini_im = st0.ap[:, 1:2]
                    ird = [st0]
                S.op(dve, lambda: V.tensor_tensor_scan(out=w1.ap[:, :NT], data0=rho, data1=w0.ap[:, :NT], initial=ini_re,
                                                       op0=ALU.mult, op1=ALU.add), reads=[w0, p["RHO"]] + ird, writes=[w1])
                S.op(dve, lambda: V.tensor_tensor_scan(out=w3.ap[:, :NT], data0=rho, data1=w2.ap[:, :NT], initial=ini_im,
                                                       op0=ALU.mult, op1=ALU.add), reads=[w2, p["RHO"]] + ird, writes=[w3])
                if q + 1 < 16:
                    bu[q + 1] = emit_bu(q + 1)
                    pre(q + 1)
                while pend_y:
                    flush_y(*pend_y.pop(0))
                pa = scr.get(); pb = scr.get(); pd = scr.get()
                tt_op(pool, pd.ap[:, :NT], w3.ap[:, :NT], Sn, ALU.mult, [w3, tb], [pd])
                tt_op(pool, pa.ap[:, :NT], w3.ap[:, :NT], Ct, ALU.mult, [w3, tb], [pa])
                tt_op(pool, pb.ap[:, :NT], w1.ap[:, :NT], Sn, ALU.mult, [w1, tb], [pb])
                tt_op(dve, w0.ap[:, :NT], w1.ap[:, :NT], Ct, ALU.mult, [w1, tb], [w0])
                tt_op(dve, w0.ap[:, :NT], w0.ap[:, :NT], pd.ap[:, :NT], ALU.subtract, [w0, pd], [w0])
                tt_op(pool, pa.ap[:, :NT], pa.ap[:, :NT], pb.ap[:, :NT], ALU.add, [pa, pb], [pa])
                evac(hf[0].ap[:, :NT], w0.ap[:, :NT], [w0], [hf[0]], eng=act)
                evac(hf[1].ap[:, :NT], pa.ap[:, :NT], [pa], [hf[1]], eng=act)
                fin = hf
                evac(carS.ap[:, 0, q:q + 1], w0.ap[:, NT - 1:NT], [w0], [carS], eng=pool)
                evac(carS.ap[:, 1, q:q + 1], pa.ap[:, NT - 1:NT], [pa], [carS], eng=pool)
            if samp and q + 1 < 16:
                bu[q + 1] = emit_bu(q + 1)
            if m1st is not None:
                merge1_rest(q // 2, m1st)
            if j == 0:
                psy = P.get()
            cpr = Cpad_re[c % 2]; cpi = Cpad_im[c % 2]
            if q == 0:
                dbg(tg + "sre0", fin[0].ap[:, :NT], [fin[0]])
                dbg(tg + "sim0", fin[1].ap[:, :NT], [fin[1]])

            def fny():
                PE.matmul(psy.ap[:, :NT], lhsT=cpr.ap[:, j, :], rhs=fin[0].ap[:, :NT], start=(j == 0), stop=False)
                return PE.matmul(psy.ap[:, :NT], lhsT=cpi.ap[:, j, :], rhs=fin[1].ap[:, :NT], start=False, stop=(j == 3))
            S.op(pe, fny, reads=[cpr, cpi, fin[0], fin[1]], writes=[psy])
            if j == 3:
                if samp:
                    flush_y(c, psy)
                else:
                    pend_y.append((c, psy))
        while pend_y:
            flush_y(*pend_y.pop(0))
        if last:
            for ri in range(2):
                pst = P.get()
                transpose(pst.ap[:16, 0:128], carS.ap[:, ri, :], 128, [carS], [pst])
                so = scr.get()
                evac(so.ap[:16, 0:128], pst.ap[:16, 0:128], [pst], [so])
                P.put(pst)
                S.dma((o_re_p if ri == 0 else o_im_p)[l], so.ap[:16, 0:128], reads=[so], st=so)

        if samp:
            for f in range(8):
                st_ = merge1_pe(f)
                merge1_rest(f, st_)
        for f in range(8):
            s3_, (vgC, vcl, vcg) = w_next(dve)
            pgC = P.get(); pcl = P.get(); pcg = P.get()
            mm(pgC.ap[:, :NT], [(vgC[:, kt, :], h[kt].ap[:, :NT]) for kt in range(8)], [s3_] + h, [pgC])
            mm(pcl.ap[:, :NT], [(vcl[:, kt, :], zc[kt].ap[:, :NT]) for kt in range(4)], [s3_] + zc, [pcl])
            mm(pcg.ap[:, :NT], [(vcg[:, kt, :], zc[kt].ap[:, :NT]) for kt in range(4)], [s3_] + zc, [pcg])
            sC = scr.get(); sG = scr.get()
            actf(sC.ap[:, :NT], pgC.ap[:, :NT], AF.Sigmoid, [pgC], [sC])
            actf(sG.ap[:, :NT], pcg.ap[:, :NT], AF.Sigmoid, [pcg], [sG])
            tt_op(dve, sG.ap[:, :NT], pcl.ap[:, :NT], sG.ap[:, :NT], ALU.mult, [pcl, sG], [sG])
            P.put(pgC, pcl, pcg)
            tt_op(pool, sG.ap[:, :NT], sG.ap[:, :NT], sC.ap[:, :NT], ALU.mult, [sG, sC], [sG])
            tt_op(dve, m[f].ap[:, :NT], m[f].ap[:, :NT], sG.ap[:, :NT], ALU.add, [m[f], sG], [m[f]])
        dbg(tg + "m0", m[0].ap[:, :NT], [m[0]])
        for c4 in range(4):
            slot, (vo,) = w_next(dve)
            for cc in range(2):
                f = 2 * c4 + cc
                pso = P.get()
                mm(pso.ap[:, :NT], [(vo[:, kt, cc * 128:(cc + 1) * 128], m[kt].ap[:, :NT]) for kt in range(8)], [slot] + m, [pso])
                tt_op(dve, x[f].ap[:, :NT], x[f].ap[:, :NT], pso.ap[:, :NT], ALU.add, [x[f], pso], [x[f]])
                P.put(pso)

        S.phase_barrier(ffn_tiles)
        dbg(tg + "xmix0", x[0].ap[:, :NT], [x[0]])
        rmsnorm(gains, lambda f: gains.ap[:, 1, f:f + 1], h, NT)
        carF = p["carF"]
        hist = pesb[0:3]
        hist_ap = arenaF[:, pesb_off:pesb_off + 44 * 32].rearrange("p (j b k) -> p j b k", j=44, k=2)
        if samp:
            rows = st_cf[l].rearrange("b k c -> (b k) c")
            S.dma(o_cf_s[l, :, 0, :], st_cf[l, :, 1, :], st=None)
            for j4 in range(11):
                rw = scr.get()
                S.dma(rw.ap[:32, :], rows[:, 512 * j4:512 * (j4 + 1)], writes=[rw], st=rw)
                pst = P.get()
                for jj in range(4):
                    transpose(pst.ap[:, 32 * jj:32 * jj + 32], rw.ap[:32, 128 * jj:128 * (jj + 1)], 32, [rw], [pst])
                evac(arenaF[:, pesb_off + 128 * j4: pesb_off + 128 * (j4 + 1)], pst.ap[:, 0:128], [pst], hist)
                P.put(pst)
        for j in range(NJ):
            slot, (vg_, vv_) = w_next(dve if j % 2 else act)
            outs = []
            for half, (vw, idx, ebuf) in enumerate([(vg_, j, eg[j % 2]), (vv_, NJ + j, ev[j % 2])]):
                psu = P.get()
                mm(psu.ap[:, :NT], [(vw[:, kt, :], h[kt].ap[:, :NT]) for kt in range(8)], [slot] + h, [psu])
                t0 = scr.get()
                actf(t0.ap[:, :NT], psu.ap[:, :NT], AF.Identity, [psu, cf], [t0],
                     scale=cf.ap[:, idx, 2:3], bias=cf.ap[:, idx, 3:4])
                if samp:
                    hv = hist_ap[:, idx, :, :]
                    stt(t0.ap[:, :NT], hv[:, :, 1], cf.ap[:, idx, 1:2], t0.ap[:, :NT], ALU.mult, ALU.add, hist + [cf, t0], [t0])
                    stt(t0.ap[:, :NT], hv[:, :, 0], cf.ap[:, idx, 0:1], t0.ap[:, :NT], ALU.mult, ALU.add, hist + [cf, t0], [t0])
                    rawc = ebuf
                    evac(rawc.ap[:, :NT], psu.ap[:, :NT], [psu], [rawc], eng=act)
                    P.put(psu)
                    upst = P.get()
                    transpose(upst.ap[:NS, 0:128], rawc.ap[:, :NS], 128, [rawc], [upst])
                    so2 = scr.get()
                    evac(so2.ap[:NS, 0:128], upst.ap[:NS, 0:128], [upst], [so2])
                    P.put(upst)
                    S.dma(o_cf_s[l, :, 1, 128 * idx:128 * (idx + 1)], so2.ap[:NS, 0:128], reads=[so2], st=so2)
                else:
                    evac(ebuf.ap[:, 2:2 + NT], psu.ap[:, :NT], [psu], [ebuf], eng=act)
                    P.put(psu)
                    if first:
                        memset(ebuf, ebuf.ap[:, 0:2])
                    else:
                        evac(ebuf.ap[:, 0:2], carF.ap[:, idx, :], [carF], [ebuf], eng=pool)
                    stt(t0.ap[:, :NT], ebuf.ap[:, 1:1 + NT], cf.ap[:, idx, 1:2], t0.ap[:, :NT], ALU.mult, ALU.add, [ebuf, cf, t0], [t0])
                    stt(t0.ap[:, :NT], ebuf.ap[:, 0:NT], cf.ap[:, idx, 0:1], t0.ap[:, :NT], ALU.mult, ALU.add, [ebuf, cf, t0], [t0])
                    evac(carF.ap[:, idx, :], ebuf.ap[:, NT:NT + 2], [ebuf], [carF], eng=pool)
                outs.append(t0)
            gl_ = scr.get()
            actf(gl_.ap[:, :NT], outs[0].ap[:, :NT], AF.Gelu_apprx_tanh, [outs[0]], [gl_])
            tt_op(dve, actt[j].ap[:, :NT], gl_.ap[:, :NT], outs[1].ap[:, :NT], ALU.mult, [gl_, outs[1]], [actt[j]])
        if last:
            pst = P.get()
            transpose(pst.ap[:88, 0:128], carF.ap.rearrange("p j k -> p (j k)"), 128, [carF], [pst])
            so = scr.get()
            evac(so.ap[:88, 0:128], pst.ap[:88, 0:128], [pst], [so])
            P.put(pst)
            S.dma(o_cf_p[l].rearrange("k (j c) -> j k c", c=128), so.ap[:88, 0:128], reads=[so], st=so)
        for fh in range(2):
            pacc = [P.get() for _ in range(4)]
            for j0 in range(0, NJ, 4):
                nj = min(4, NJ - j0)
                slot, (vd,) = w_next(dve)

                def fn():
                    ins = None
                    for jj in range(nj):
                        for fi in range(4):
                            ins = PE.matmul(pacc[fi].ap[:, :NT], lhsT=vd[:, jj, fi * 128:(fi + 1) * 128],
                                            rhs=actt[j0 + jj].ap[:, :NT], start=(j0 + jj == 0), stop=(j0 + jj == NJ - 1))
                    return ins
                S.op(pe, fn, reads=[slot] + actt[j0:j0 + nj], writes=pacc)
            for fi in range(4):
                f = 4 * fh + fi
                tt_op(dve, x[f].ap[:, :NT], x[f].ap[:, :NT], pacc[fi].ap[:, :NT], ALU.add, [x[f], pacc[fi]], [x[f]])
                P.put(pacc[fi])

        dbg(tg + "act0", actt[0].ap[:, :NT], [actt[0]])
        dbg(tg + "xffn0", x[0].ap[:, :NT], [x[0]])
        rmsnorm(gains, lambda f: gains.ap[:, 2, f:f + 1], h, NT)
        if samp:
            pstg = scr.get()
            S.dma(pstg.ap[:NS, 0:256], psm[l], writes=[pstg], st=pstg)
            pst = P.get()
            for kt in range(2):
                transpose(pst.ap[:, kt * 16:kt * 16 + 16], pstg.ap[:NS, kt * 128:(kt + 1) * 128], NS, [pstg], [pst])
            for kt in range(2):
                evac(pT[kt].ap[:, :NS], pst.ap[:, kt * 16:kt * 16 + 16], [pst], [pT[kt]])
            P.put(pst)
        else:
            pstg = [scr.get(), scr.get()]
            for r in range(4):
                tk = TT * ti + 128 * r
                S.dma(pstg[r // 2].ap[:, (r % 2) * 256:(r % 2) * 256 + 256], pp[l, tk:tk + 128, :], writes=[pstg[r // 2]], st=pstg[r // 2])
            for kt in range(2):
                pst = P.get()
                for r in range(4):
                    transpose(pst.ap[:, r * 128:(r + 1) * 128],
                              pstg[r // 2].ap[:, (r % 2) * 256 + kt * 128:(r % 2) * 256 + (kt + 1) * 128], 128, [pstg[r // 2]], [pst])
                evac(pT[kt].ap[:, :NT], pst.ap[:, :NT], [pst], [pT[kt]])
                P.put(pst)
        slot, (vpe,) = w_next(dve)
        for f in range(8):
            pse = P.get()
            mm(pse.ap[:, :NT], [(vpe[:, kt, f * 128:(f + 1) * 128], pT[kt].ap[:, :NT]) for kt in range(2)], [slot] + pT, [pse])
            evac(pesb[f].ap[:, :NT], pse.ap[:, :NT], [pse], [pesb[f]])
            P.put(pse)
        for c4 in range(4):
            slot, (vpg,) = w_next(dve)
            for cc in range(2):
                f = 2 * c4 + cc
                psg = P.get()
                mm(psg.ap[:, :NT], [(vpg[:, kt, cc * 128:(cc + 1) * 128], h[kt].ap[:, :NT]) for kt in range(8)], [slot] + h, [psg])
                sg = scr.get()
                actf(sg.ap[:, :NT], psg.ap[:, :NT], AF.Sigmoid, [psg], [sg])
                P.put(psg)
                tt_op(pool, sg.ap[:, :NT], sg.ap[:, :NT], pesb[f].ap[:, :NT], ALU.mult, [sg, pesb[f]], [sg])
                tt_op(dve, x[f].ap[:, :NT], x[f].ap[:, :NT], sg.ap[:, :NT], ALU.add, [x[f], sg], [x[f]])
        dbg(tg + "xple0", x[0].ap[:, :NT], [x[0]])

    for (kind, ti) in tiles_order:
        samp = (kind == "s")
        NT = NS if samp else TT
        R = 1 if samp else 4
        MT = NS if samp else 128
        S.phase_barrier(mixer_tiles)
        for r in range(R):
            xst = xstage[r % 2]
            if samp:
                S.dma(xst.ap[:NS, :], xs, writes=[xst], st=xst)
            else:
                S.dma(xst.ap[:, :], xp[TT * ti + 128 * r: TT * ti + 128 * (r + 1), :], writes=[xst], st=xst)
            for fg in range(2):
                pst = P.get()
                for k in range(4):
                    f = 4 * fg + k
                    transpose(pst.ap[:, k * 128:k * 128 + MT], xst.ap[:MT, f * 128:(f + 1) * 128], MT, [xst], [pst])
                evac(xbig[:, 4 * fg:4 * fg + 4, r * 128:r * 128 + MT],
                     pst.ap.rearrange("p (k t) -> p k t", k=4)[:, :, 0:MT], [pst], x[4 * fg:4 * fg + 4])
                P.put(pst)
        dbg("%s%d_x0" % (kind, ti), x[0].ap[:, :NT], [x[0]])
        for l in range(NL):
            if l > 0:
                S.phase_barrier(mixer_tiles)
            layer(l, kind, ti, NT)
        S.phase_barrier(fin_tiles)
        rmsnorm(gfin, lambda f: gfin.ap[:, f:f + 1], yT, NT)
        for r in range(R):
            ys = ystage[r % 2]
            for fg in range(2):
                pst = P.get()
                for k in range(4):
                    f = 4 * fg + k
                    transpose(pst.ap[:MT, k * 128:(k + 1) * 128], yT[f].ap[:, r * 128:r * 128 + MT], 128, [yT[f]], [pst])
                evac(ys.ap[:MT, fg * 512:(fg + 1) * 512], pst.ap[:MT, :], [pst], [ys])
                P.put(pst)
            if samp:
                S.dma(y_s, ys.ap[:NS, :], reads=[ys], st=ys)
            else:
                S.dma(y_p[TT * ti + 128 * r: TT * ti + 128 * (r + 1), :], ys.ap[:, :], reads=[ys], st=ys)
    assert wstate["next_use"] == len(gchunks), (wstate, len(gchunks))
    S.finish()
    print("SBUF bytes remaining:", nc.sbuf_bytes_remaining, "sems:", S.nsem,
          "ops:", {E.name: E.count for E in S.compute})


_NC_CACHE = {}


def pack_weights(g):
    nch = len(layer_chunks(0))
    out = np.zeros((L, nch, 128, SLOT), np.float32)
    for l in range(L):
        for ci, chunk in enumerate(layer_chunks(l)):
            o = 0
            for (wn, l_, r0, nkt, c0, nc_) in chunk:
                blk = g[wn][l, r0:r0 + 128 * nkt, c0:c0 + nc_].reshape(nkt, 128, nc_)
                out[l, ci, :, o:o + nkt * nc_] = blk.transpose(1, 0, 2).reshape(128, nkt * nc_)
                o += nkt * nc_
    return out


def kernel(**inp):
    f32 = np.float32
    g = {k: np.ascontiguousarray(np.asarray(v), dtype=f32) for k, v in inp.items()}
    if "nc" not in _NC_CACHE:
        _NC_CACHE["nc"] = build_nc()
    nc = _NC_CACHE["nc"]
    cst = np.zeros((128, 256), f32)
    cst[:, 0:128] = np.eye(128, dtype=f32)
    cst[:, 128:256] = np.tril(np.ones((128, 128), f32))
    cst_r = np.zeros((128, 384), f32)
    cst_r[:, 0:128] = 1.0 / 1024.0
    cst_r[:, 128:256] = 1.0 / 512.0
    cst_r[:, 256:384] = 1.0
    shared = {
        "cst": cst, "cst_r": cst_r, "b_s_r": g["b_s"],
    }
    shared["wpack"] = pack_weights(g)
    for k in ["g_mix", "g_ffn", "g_ple",
              "g_final", "conv_a_w", "conv_a_b", "ln_a_g", "ln_a_b", "ln_b_g", "ln_b_b", "w_s", "b_s", "ssm_a_re", "ssm_a_im",
              "ssm_log_dt", "ssm_b_re", "ssm_b_im", "ssm_c_re", "ssm_c_im", "ssm_d", "conv_f_w", "conv_f_b"]:
        shared[k] = g[k]
    shared["conv_f_b"] = np.ascontiguousarray(g["conv_f_b"].reshape(L, 1, 2 * DFF))
    in_maps = []
    for c in range(8):
        sl = slice(NS * c, NS * (c + 1))
        d = dict(shared)
        d["xp"] = g["x_prompt"][c]
        d["xs"] = np.ascontiguousarray(g["x_sample"][sl, 0, :])
        d["pp"] = np.ascontiguousarray(g["p_prompt"][:, c])
        d["psm"] = np.ascontiguousarray(g["p_sample"][:, sl, 0, :])
        d["st_ca"] = np.ascontiguousarray(g["state_conv_a"][:, sl])
        d["st_re"] = np.ascontiguousarray(g["state_ssm_re"][:, sl].reshape(L, NS, 2048))
        d["st_im"] = np.ascontiguousarray(g["state_ssm_im"][:, sl].reshape(L, NS, 2048))
        d["st_cf"] = np.ascontiguousarray(g["state_conv_ffn"][:, sl])
        in_maps.append(d)
    res = run_bass_kernel_spmd(nc, in_maps, core_ids=list(range(8)))
    rs = res.results

    def cat(name, axis, shape=None):
        a = np.concatenate([np.asarray(r[name], dtype=f32) for r in rs], axis=axis)
        return a

    y_prompt = np.stack([np.asarray(r["y_p"], f32) for r in rs], 0)
    y_sample = cat("y_s", 0).reshape(128, 1, D)
    conv_a_p = np.stack([np.asarray(r["o_ca_p"], f32) for r in rs], 1)
    conv_a_s = cat("o_ca_s", 1)
    chunk_v_p = np.stack([np.asarray(r["o_cv_p"], f32) for r in rs], 1)
    chunk_v_s = cat("o_cv_s", 1).reshape(L, 128, 1, DA)
    re_p = np.stack([np.asarray(r["o_re_p"], f32).reshape(L, 32, 64) for r in rs], 1)
    im_p = np.stack([np.asarray(r["o_im_p"], f32).reshape(L, 32, 64) for r in rs], 1)
    re_s = cat("o_re_s", 1).reshape(L, 128, 32, 64)
    im_s = cat("o_im_s", 1).reshape(L, 128, 32, 64)
    cf_p = np.stack([np.asarray(r["o_cf_p"], f32) for r in rs], 1)
    cf_s = cat("o_cf_s", 1)
    return (y_prompt, y_sample, conv_a_p, conv_a_s, chunk_v_p, chunk_v_s, re_p, im_p, re_s, im_s, cf_p, cf_s)
```

```python
import numpy as np
from contextlib import ExitStack
import concourse.bass as bass
import concourse.mybir as mybir
from concourse.bass_utils import run_bass_kernel_spmd

F32 = mybir.dt.float32
F32R = mybir.dt.float32r
BF = mybir.dt.bfloat16
ALU = mybir.AluOpType
AF = mybir.ActivationFunctionType

D = 1024
SEQ = 2048
NS = 16
L = 2
TT = 512
DA = 512
DFF = 2816
NJ = 22
EPS = 1e-6
O1 = 1024
O2 = 2048
O3 = 2560
SLOT = 2048
NSLOT = 3
NSTEP = 9
S5_POOL = False
S5_USE_DVE = False
DBG = None


class Res:
    __slots__ = ("lastw", "readers")

    def __init__(self):
        self.lastw = None
        self.readers = []


class Tile:
    def __init__(self, ap, psum=False):
        self.ap = ap
        self.res = [Res()]
        self.dsem = None
        self.psum = psum

    def __getitem__(self, k):
        return self.ap[k]


class Eng:
    def __init__(self, name, eng, sem, inorder=False):
        self.name = name
        self.eng = eng
        self.sem = sem
        self.count = 0
        self.waited = {}
        self.inorder = inorder


class Sched:
    def __init__(self, nc, es):
        self.nc = nc
        self.es = es
        self.nsem = 0
        self.dma_toks = {}
        self.all_toks = {}
        self.dpool = []
        self.dpi = 0
        self.pe = Eng("pe", nc.tensor, self.newsem("s_pe"), inorder=True)
        self.act = Eng("act", nc.scalar, self.newsem("s_act"))
        self.dve = Eng("dve", nc.vector, self.newsem("s_dve"))
        self.pool = Eng("pool", nc.gpsimd, self.newsem("s_pool"))
        self.sp = Eng("sp", nc.sync, None)
        self.compute = [self.pe, self.act, self.dve, self.pool]

    def newsem(self, name):
        self.nsem += 1
        return self.es.enter_context(self.nc.semaphore(name))

    def _need(self, E, tok, needs):
        if tok is None:
            return
        sem, val, src = tok
        if src is E and E.inorder:
            return
        key = id(sem)
        if E.waited.get(key, 0) >= val:
            return
        if needs.get(key, (None, 0))[1] < val:
            needs[key] = (sem, val)

    def _deps(self, E, reads, writes):
        needs = {}
        for t in reads:
            for r in t.res:
                self._need(E, r.lastw, needs)
                if t.psum:
                    for rd in r.readers:
                        if rd[2] is not E:
                            self._need(E, rd, needs)
        for t in writes:
            for r in t.res:
                self._need(E, r.lastw, needs)
                for rd in r.readers:
                    self._need(E, rd, needs)
        for key, (sem, val) in needs.items():
            E.eng.wait_ge(sem, val)
            E.waited[key] = val

    def _mark(self, tok, reads, writes):
        for t in writes:
            for r in t.res:
                r.lastw = tok
                r.readers = []
        for t in reads:
            for r in t.res:
                r.readers.append(tok)

    def op(self, E, fn, reads=(), writes=()):
        self._deps(E, reads, writes)
        ins = fn()
        E.count += 1
        ins.then_inc(E.sem, 1)
        self._mark((E.sem, E.count, E), reads, writes)

    def dma(self, out_ap, in_ap, reads=(), writes=(), st=None, E=None, track=True, group=False):
        E = E or self.sp
        if group and st is not None and st.dsem is not None:
            saved = []
            for t in writes:
                for r in t.res:
                    if r.lastw is not None and r.lastw[0] is st.dsem[0]:
                        saved.append((r, r.lastw))
                        r.lastw = None
            self._deps(E, reads, writes)
            for r, lw in saved:
                r.lastw = lw
        else:
            self._deps(E, reads, writes)
        if st is None:
            if not self.dpool:
                self.dpool = [Tile(None) for _ in range(8)]
                for t in self.dpool:
                    t.dsem = [self.newsem("dp%d" % self.nsem), 0]
            st = self.dpool[self.dpi % len(self.dpool)]
            self.dpi += 1
            if st.dsem[1] > 0 and E.waited.get(id(st.dsem[0]), 0) < st.dsem[1]:
                E.eng.wait_ge(st.dsem[0], st.dsem[1])
                E.waited[id(st.dsem[0])] = st.dsem[1]
        if st.dsem is None:
            st.dsem = [self.newsem("d%d" % self.nsem), 0]
        ins = E.eng.dma_start(out=out_ap, in_=in_ap)
        st.dsem[1] += 16
        ins.then_inc(st.dsem[0], 16)
        tok = (st.dsem[0], st.dsem[1], None)
        self.all_toks[id(st.dsem[0])] = tok
        if track:
            self.dma_toks[id(st.dsem[0])] = tok
        self._mark(tok, reads, writes)
        return tok

    def drain_pool(self, E):
        for t in self.dpool:
            sem, val = t.dsem
            if val > 0 and E.waited.get(id(sem), 0) < val:
                E.eng.wait_ge(sem, val)
                E.waited[id(sem)] = val

    def phase_barrier(self, tiles):
        toks = [(E.sem, E.count, None) for E in self.compute if E.count > 0]
        toks += list(self.dma_toks.values())
        for t in tiles:
            for r in t.res:
                r.lastw = None
                r.readers = list(toks)

    def finish(self):
        for E in self.compute:
            if E.count > 0 and self.sp.waited.get(id(E.sem), 0) < E.count:
                self.sp.eng.wait_ge(E.sem, E.count)
        for tok in self.all_toks.values():
            sem, val, _ = tok
            if self.sp.waited.get(id(sem), 0) < val:
                self.sp.eng.wait_ge(sem, val)
                self.sp.waited[id(sem)] = val


class Pool_:
    def __init__(self, tiles):
        self.free = list(tiles)

    def get(self):
        assert self.free, "pool exhausted"
        return self.free.pop(0)

    def put(self, *ts):
        for t in ts:
            self.free.append(t)


class Ring:
    def __init__(self, tiles):
        self.tiles = tiles
        self.i = 0

    def get(self):
        t = self.tiles[self.i % len(self.tiles)]
        self.i += 1
        return t


def layer_chunks(l):
    ch = []
    for c in range(4):
        ch.append([("w_in", l, 0, 8, 128 * c, 128), ("w_in", l, 0, 8, 512 + 128 * c, 128)])
    for c2 in range(2):
        ch.append([("w_in", l, 0, 8, O1 + 256 * c2, 256)])
    for kh in range(2):
        ch.append([("w_in", l, 512 * kh, 4, O1 + 512, 512)])
    for c2 in range(2):
        ch.append([("w_in", l, 0, 8, O2 + 256 * c2, 256)])
    for f in range(8):
        ch.append([("w_in", l, 0, 8, O3 + 128 * f, 128), ("w_in", l, 0, 8, O3 + 1024 + 128 * f, 128)])
        ch.append([("w_a_out", l, 0, 4, 128 * f, 128), ("w_b_out", l, 0, 4, 128 * f, 128)])
    for f in range(8):
        ch.append([("w_in", l, 0, 8, O3 + 2048 + 128 * f, 128), ("w_c_glu", l, 0, 4, 128 * f, 128),
                   ("w_c_glu", l, 0, 4, 1024 + 128 * f, 128)])
    for c4 in range(4):
        ch.append([("w_out", l, 0, 8, 256 * c4, 256)])
    for j in range(NJ):
        ch.append([("w_up", l, 0, 8, 128 * j, 128), ("w_up", l, 0, 8, DFF + 128 * j, 128)])
    for fh in range(2):
        for j0 in range(0, NJ, 4):
            nj = min(4, NJ - j0)
            ch.append([("w_down", l, 128 * j0, nj, 512 * fh, 512)])
    ch.append([("w_pe", l, 0, 2, 0, 1024)])
    for c4 in range(4):
        ch.append([("w_pg", l, 0, 8, 256 * c4, 256)])
    return ch


def build_nc():
    nc = bass.Bass("TRN2", target_bir_lowering=False)
    nc.dge_precook = False
    es = ExitStack()
    with es:
        _build(nc, es)
    return nc


def _build(nc, es):
    def din(name, shape, dt=F32):
        return nc.dram_tensor(name, list(shape), dt, kind="ExternalInput").ap()

    def dout(name, shape):
        return nc.dram_tensor(name, list(shape), F32, kind="ExternalOutput").ap()

    xp = din("xp", [SEQ, D]); xs = din("xs", [NS, D])
    pp = din("pp", [L, SEQ, 256]); psm = din("psm", [L, NS, 256])
    st_ca = din("st_ca", [L, NS, 30, DA]); st_re = din("st_re", [L, NS, 2048]); st_im = din("st_im", [L, NS, 2048])
    st_cf = din("st_cf", [L, NS, 2, 2 * DFF])
    NCH = len(layer_chunks(0))
    wpack = din("wpack", [L, NCH, 128, SLOT])
    g_mix = din("g_mix", [L, D]); g_ffn = din("g_ffn", [L, D]); g_ple = din("g_ple", [L, D]); g_final = din("g_final", [D])
    conv_a_w = din("conv_a_w", [L, 31, DA]); conv_a_b = din("conv_a_b", [L, DA])
    ln_a_g = din("ln_a_g", [L, DA]); ln_a_b = din("ln_a_b", [L, DA])
    ln_b_g = din("ln_b_g", [L, DA]); ln_b_b = din("ln_b_b", [L, DA])
    w_s = din("w_s", [L, 4, 128, 128]); b_s = din("b_s", [L, 4, 128]); b_s_r = din("b_s_r", [L, 4, 128])
    ssm_a_re = din("ssm_a_re", [L, 32, 64]); ssm_a_im = din("ssm_a_im", [L, 32, 64]); ssm_log_dt = din("ssm_log_dt", [L, 32])
    ssm_b_re = din("ssm_b_re", [L, 32, 64, 16]); ssm_b_im = din("ssm_b_im", [L, 32, 64, 16])
    ssm_c_re = din("ssm_c_re", [L, 32, 16, 64]); ssm_c_im = din("ssm_c_im", [L, 32, 16, 64])
    ssm_d = din("ssm_d", [L, DA])
    conv_f_w = din("conv_f_w", [L, 3, 2 * DFF]); conv_f_b = din("conv_f_b", [L, 1, 2 * DFF])
    cst = din("cst", [128, 256])
    cst_r = din("cst_r", [128, 384])

    y_p = dout("y_p", [SEQ, D]); y_s = dout("y_s", [NS, D])
    o_ca_p = dout("o_ca_p", [L, 30, DA]); o_ca_s = dout("o_ca_s", [L, NS, 30, DA])
    o_cv_p = dout("o_cv_p", [L, 128, DA]); o_cv_s = dout("o_cv_s", [L, NS, DA])
    o_re_p = dout("o_re_p", [L, 16, 128]); o_im_p = dout("o_im_p", [L, 16, 128])
    o_re_s = dout("o_re_s", [L, NS, 16, 128]); o_im_s = dout("o_im_s", [L, NS, 16, 128])
    o_cf_p = dout("o_cf_p", [L, 2, 2 * DFF]); o_cf_s = dout("o_cf_s", [L, NS, 2, 2 * DFF])

    S = Sched(nc, es)
    pe, act, dve, pool = S.pe, S.act, S.dve, S.pool
    V = nc.vector; A = nc.scalar; G = nc.gpsimd; PE = nc.tensor

    def sbt(name, shape, dt=F32):
        return es.enter_context(nc.sbuf_tensor(name, list(shape), dt)).ap()

    T = Tile

    cstt = T(sbt("cstt", [128, 256]))
    ident = cstt.ap[:, 0:128]
    tril = cstt.ap[:, 128:256]
    cstr = T(sbt("cstr", [128, 384], BF))
    ones_d = cstr.ap[:, 0:128]
    ones_c = cstr.ap[:, 128:256]
    ones_row = cstr.ap[0:1, 256:384]
    xbig = sbt("xT", [128, 8, TT]); x = [T(xbig[:, f, :]) for f in range(8)]
    hbig = sbt("hT", [128, 8, TT], BF); h = [T(hbig[:, f, :]) for f in range(8)]
    wstg_ap = sbt("wstg", [128, NSLOT, SLOT])
    wstg = [T(wstg_ap[:, i, :]) for i in range(NSLOT)]
    wring_ap = sbt("wring", [128, NSLOT, SLOT], BF)
    wslots = [T(wring_ap[:, i, :]) for i in range(NSLOT)]
    psb = [Tile(es.enter_context(nc.psum_tensor("ps%d" % i, [128, 512], F32)).ap(), psum=True) for i in range(8)]
    P = Pool_(psb)
    NSCR = 8
    scr_ap = sbt("scr", [128, NSCR, TT])
    scr = Ring([T(scr_ap[:, i, :]) for i in range(NSCR)])
    scr_r_ap = sbt("scrr", [128, 3, TT], BF)
    scr_r = Ring([T(scr_r_ap[:, i, :]) for i in range(3)])
    sm_ap = sbt("small", [128, 16, 8])
    small = Ring([T(sm_ap[:, i, :]) for i in range(16)])

    par = []
    for l in range(L):
        p = {}
        p["gains"] = T(sbt("gains%d" % l, [128, 3, 8]))
        p["caw"] = T(sbt("caw%d" % l, [128, 4, 31]))
        p["avec"] = T(sbt("avec%d" % l, [128, 4, 4]))
        p["lnb"] = T(sbt("lnb%d" % l, [128, 2, DA]))
        p["WsT"] = T(sbt("WsT%d" % l, [128, 4, 128], BF))
        p["bs1"] = T(sbt("bs1%d" % l, [1, 4, 128], BF))
        p["bs0"] = T(sbt("bs0%d" % l, [128, 4]))
        p["w00"] = T(sbt("w00%d" % l, [128, 4]))
        p["Wsm"] = T(sbt("Wsm%d" % l, [16, 4, 16], BF))
        p["BbRe"] = T(sbt("BbRe%d" % l, [128, 4, 128], BF))
        p["BbIm"] = T(sbt("BbIm%d" % l, [128, 4, 128], BF))
        p["CTre"] = T(sbt("CTre%d" % l, [128, 4, 128]))
        p["CTim"] = T(sbt("CTim%d" % l, [128, 4, 128]))
        p["HSC"] = T(sbt("HSC%d" % l, [128, 1, 3, 16]))
        p["UPH"] = T(sbt("UPH%d" % l, [128, 3, 16]))
        p["RHO"] = T(sbt("RHO%d" % l, [128, 16]))
        p["cf"] = T(sbt("cf%d" % l, [128, 44, 4]))
        p["carA"] = T(sbt("carA%d" % l, [128, 4, 30], BF))
        p["carF"] = T(sbt("carF%d" % l, [128, 44, 2]))
        p["carS"] = T(sbt("carS%d" % l, [128, 2, 16]))
        par.append(p)
    gfin = T(sbt("gfin", [128, 8]))
    dg_ap = sbt("dg", [128, 2, 31, 128], BF)
    dg = [T(dg_ap[:, i, :, :]) for i in range(2)]
    alast = T(sbt("alast", [128, 4, 32]))
    tabring_ap = sbt("tabring", [128, 3, 1024])
    tabring = [T(tabring_ap[:, i, :]) for i in range(3)]
    tab_d = nc.dram_tensor("tab_d", [L, 16, 128, 1024], F32, kind="Internal").ap()
    Cpad_re_ap = sbt("Cpad_re", [128, 2, 4, 128], BF); Cpad_re = [T(Cpad_re_ap[:, i, :, :]) for i in range(2)]
    Cpad_im_ap = sbt("Cpad_im", [128, 2, 4, 128], BF); Cpad_im = [T(Cpad_im_ap[:, i, :, :]) for i in range(2)]

    ARR = 16512
    ARF = 6272 + 2048
    arenaR = sbt("arenaR", [128, ARR], BF)
    arenaF = sbt("arenaF", [128, ARF])
    offR = [0]; offF = [0]

    def cR(n):
        a = arenaR[:, offR[0]:offR[0] + n]; offR[0] += n
        assert offR[0] <= ARR, offR[0]
        return a

    def cF(n):
        a = arenaF[:, offF[0]:offF[0] + n]; offF[0] += n
        assert offF[0] <= ARF, offF[0]
        return a

    m = [T(cR(TT)) for f in range(8)]
    acs = [T(cR(TT)) for c in range(4)]
    ub = [T(cR(TT)) for c in range(4)]
    vtok = [T(cR(DA)) for r in range(4)]
    zc = [T(cR(TT)) for c in range(4)]
    hsF2 = [[T(cR(TT)) for _ in range(2)] for _ in range(2)]
    hsF = hsF2[0]
    aext = [T(cR(544)) for c in range(4)]
    hsAB = [T(cF(TT)) for _ in range(4)]
    hsCD = [T(cF(TT)) for _ in range(4)]
    hsA = hsAB[0:2]; hsB = hsAB[2:4]
    xstage = [T(cF(D)) for _ in range(2)]
    mixer_tiles = m + acs + ub + vtok + zc + hsF2[0] + hsF2[1] + aext + hsAB + hsCD + xstage
    offR[0] = 0; offF[0] = 0
    actt = [T(cR(TT)) for j in range(NJ)]
    pT = [T(cR(TT)) for _ in range(2)]
    eg = [T(cF(520)) for _ in range(2)]
    ev = [T(cF(520)) for _ in range(2)]
    pesb_off = offF[0]
    pesb = [T(cF(TT)) for f in range(8)]
    ffn_tiles = actt + pT + eg + ev + pesb
    offF[0] = 0
    yT = [T(cF(TT)) for f in range(8)]
    ystage = [T(cF(D)) for _ in range(2)]
    fin_tiles = yT + ystage
    offF[0] = 0
    prep_ws = T(cF(512)); prep_x = [T(cF(512)) for _ in range(2)]
    prep_cst = [T(cF(512)) for _ in range(2)]
    pv = {}
    for nm in ["are", "aim", "ldt", "dt", "zr", "th", "p", "er", "c", "s", "t1", "t2", "abr", "abi", "pp", "den", "cfr", "cfi", "ncfi"]:
        pv[nm] = T(cF(16))
    pB = [T(cF(256)) for _ in range(2)]
    pBb = [T(cF(256)) for _ in range(2)]
    prep_stg = T(cF(1024))
    uph = T(cF(NSTEP * 3 * 16))
    prep_tiles = [prep_ws] + prep_x + prep_cst + list(pv.values()) + pB + pBb + [prep_stg, uph]

    def mm(ps_ap, pairs, reads, writes, tp=None):
        def fn():
            n = len(pairs)
            ins = None
            for i, (lt, rh) in enumerate(pairs):
                kw = {}
                if tp is not None:
                    kw["tile_position"] = tp
                ins = PE.matmul(ps_ap, lhsT=lt, rhs=rh, start=(i == 0), stop=(i == n - 1), **kw)
            return ins
        S.op(pe, fn, reads=reads, writes=writes)

    cp_flip = [0]

    def evac(out_ap, in_ap, reads, writes, eng=None):
        if eng is None:
            cp_flip[0] ^= 1
            eng = act if cp_flip[0] else dve
        if eng is act:
            S.op(act, lambda: A.copy(out=out_ap, in_=in_ap), reads=reads, writes=writes)
        elif eng is dve:
            S.op(dve, lambda: V.tensor_copy(out=out_ap, in_=in_ap), reads=reads, writes=writes)
        else:
            S.op(pool, lambda: G.tensor_copy(out=out_ap, in_=in_ap), reads=reads, writes=writes)

    def tt_op(E, out_ap, a, b, op, reads, writes):
        e = V if E is dve else G
        S.op(E, lambda: e.tensor_tensor(out=out_ap, in0=a, in1=b, op=op), reads=reads, writes=writes)

    def ts_op(E, out_ap, a, s1, s2, op0, op1, reads, writes):
        e = V if E is dve else G
        if op1 is None:
            S.op(E, lambda: e.tensor_scalar(out=out_ap, in0=a, scalar1=s1, scalar2=None, op0=op0), reads=reads, writes=writes)
        else:
            S.op(E, lambda: e.tensor_scalar(out=out_ap, in0=a, scalar1=s1, scalar2=s2, op0=op0, op1=op1), reads=reads, writes=writes)

    def stt(out_ap, a, s, b, op0, op1, reads, writes):
        S.op(dve, lambda: V.scalar_tensor_tensor(out=out_ap, in0=a, scalar=s, in1=b, op0=op0, op1=op1), reads=reads, writes=writes)

    def actf(out_ap, in_ap, func, reads, writes, bias=None, scale=None, accum=None):
        kw = {}
        if bias is not None:
            kw["bias"] = bias
        if scale is not None:
            kw["scale"] = scale
        if accum is not None:
            kw["accum_out"] = accum
        S.op(act, lambda: A.activation(out=out_ap, in_=in_ap, func=func, **kw), reads=reads, writes=writes)

    def transpose(ps_ap, in_ap, npart, reads, writes):
        S.op(pe, lambda: PE.transpose(ps_ap, in_ap, ident[:npart, :npart]), reads=list(reads) + [cstt], writes=writes)

    def memset(t, ap, val=0.0):
        S.op(pool, lambda: G.memset(ap, val), reads=[], writes=[t])

    tiles_order = [("p", i) for i in range(4)] + [("s", 0)]
    NL = L
    if DBG is not None:
        tiles_order = DBG["tiles"]
        NL = DBG["layers"]
    gchunks = []
    gcidx = []
    for _t in tiles_order:
        for l in range(NL):
            lc = layer_chunks(l)
            gchunks += lc
            gcidx += [(l, i) for i in range(len(lc))]

    def dbg(name, ap, reads):
        if DBG is None:
            return
        shp = list(ap.shape)
        d = nc.dram_tensor("dbg_" + name, shp, ap.dtype, kind="ExternalOutput").ap()
        S.dma(d, ap, reads=reads, st=None)
    wstate = {"next_load": 0, "next_use": 0, "next_cast": 0}

    def w_load(idx):
        chunk = gchunks[idx]
        stg = wstg[idx % NSLOT]
        o = sum(nkt * nc_ for (wn, l, r0, nkt, c0, nc_) in chunk)
        assert o <= SLOT
        l, ci = gcidx[idx]
        S.dma(stg.ap[:, 0:o], wpack[l, ci, :, 0:o], writes=[stg], st=stg, track=False)
        return o

    wsize = {}

    def w_cast(idx, eng):
        n = wsize[idx]
        stg = wstg[idx % NSLOT]; slot = wslots[idx % NSLOT]
        evac(slot.ap[:, 0:n], stg.ap[:, 0:n], [stg], [slot], eng=eng)

    def w_next(cast_eng=None):
        cast_eng = cast_eng or act
        idx = wstate["next_use"]
        n = len(gchunks)
        if idx == 0:
            for k in range(min(NSLOT, n)):
                wsize[k] = w_load(k)
            wstate["next_load"] = min(NSLOT, n)
            w_cast(0, cast_eng)
            wstate["next_cast"] = 1
        if wstate["next_cast"] <= idx + 1 and wstate["next_cast"] < n:
            k = wstate["next_cast"]
            w_cast(k, cast_eng)
            wstate["next_cast"] = k + 1
        while wstate["next_load"] < n and wstate["next_load"] - NSLOT < wstate["next_cast"]:
            k = wstate["next_load"]
            wsize[k] = w_load(k)
            wstate["next_load"] += 1
        wstate["next_use"] += 1
        slot = wslots[idx % NSLOT]
        views = []
        o = 0
        for (wn, l, r0, nkt, c0, nc_) in gchunks[idx]:
            views.append(slot.ap[:, o:o + nkt * nc_].rearrange("p (k c) -> p k c", k=nkt))
            o += nkt * nc_
        return slot, views

    S.dma(cstt.ap, cst, writes=[cstt], st=None, track=False)
    cst_stg = scr.get()
    S.dma(cst_stg.ap[:, 0:384], cst_r, writes=[cst_stg], st=None, track=False)
    evac(cstr.ap, cst_stg.ap[:, 0:384], [cst_stg], [cstr], eng=dve)
    nonc = nc.allow_non_contiguous_dma(reason="small param loads")
    nonc.__enter__()

    def pdma(dst_tile, dst_ap, src_ap):
        S.dma(dst_ap, src_ap, writes=[dst_tile], st=None, track=False)

    def featvec(dst_tile, dst_ap, src_vec):
        pdma(dst_tile, dst_ap, src_vec.rearrange("(f p) -> p f", p=128))

    def v16(nm):
        return pv[nm].ap

    for l in range(L):
        p = par[l]
        featvec(p["gains"], p["gains"].ap[:, 0, :], g_mix[l])
        featvec(p["gains"], p["gains"].ap[:, 1, :], g_ffn[l])
        featvec(p["gains"], p["gains"].ap[:, 2, :], g_ple[l])
        for i, v_ in enumerate([conv_a_b, ln_a_g, ln_a_b, ssm_d]):
            featvec(p["avec"], p["avec"].ap[:, i, :], v_[l])
        pdma(p["lnb"], p["lnb"].ap[:, 0, :], ln_b_g[l].partition_broadcast(128))
        pdma(p["lnb"], p["lnb"].ap[:, 1, :], ln_b_b[l].partition_broadcast(128))
        bstg = small.get()
        bs_stg = scr.get()
        pdma(bs_stg, bs_stg.ap[0:1, 0:512], b_s_r[l:l + 1, :, :].rearrange("o g i -> o (g i)"))
        evac(p["bs1"].ap[0:1, :, :].rearrange("o g i -> o (g i)"), bs_stg.ap[0:1, 0:512], [bs_stg], [p["bs1"]], eng=dve)
        pdma(p["bs0"], p["bs0"].ap, b_s[l, :, 0].partition_broadcast(128))
        pdma(p["w00"], p["w00"].ap, w_s[l, :, 0, 0].partition_broadcast(128))
        pdma(prep_stg, prep_stg.ap[:31, 0:512], conv_a_w[l])
        pst = P.get()
        for c in range(4):
            transpose(pst.ap[:, 32 * c:32 * c + 31], prep_stg.ap[:31, 128 * c:128 * (c + 1)], 31, [prep_stg], [pst])
        evac(p["caw"].ap, pst.ap[:, 0:128].rearrange("p (c k) -> p c k", k=32)[:, :, 0:31], [pst], [p["caw"]])
        P.put(pst)
        for j4 in range(11):
            stg = scr.get()
            pdma(stg, stg.ap[0:3, :], conv_f_w[l][:, 512 * j4:512 * (j4 + 1)])
            pdma(stg, stg.ap[3:4, :], conv_f_b[l][:, 512 * j4:512 * (j4 + 1)])
            pst = P.get()
            for jj in range(4):
                transpose(pst.ap[:, 4 * jj:4 * jj + 4], stg.ap[0:4, 128 * jj:128 * (jj + 1)], 4, [stg], [pst])
            evac(p["cf"].ap[:, 4 * j4:4 * j4 + 4, :], pst.ap[:, 0:16].rearrange("p (j k) -> p j k", k=4), [pst], [p["cf"]])
            P.put(pst)
        pdma(prep_ws, prep_ws.ap.rearrange("p (g j) -> p g j", g=4), w_s[l].rearrange("g i j -> i g j"))
        for g in range(4):
            tt_op(dve, prep_ws.ap[:, g * 128:(g + 1) * 128], prep_ws.ap[:, g * 128:(g + 1) * 128], tril, ALU.mult,
                  reads=[prep_ws, cstt], writes=[prep_ws])
        pst = P.get()
        for g in range(4):
            transpose(pst.ap[:, g * 128:(g + 1) * 128], prep_ws.ap[:, g * 128:(g + 1) * 128], 128, [prep_ws], [pst])
        evac(p["WsT"].ap.rearrange("p g i -> p (g i)"), pst.ap, [pst], [p["WsT"]])
        P.put(pst)
        for g in range(4):
            ts_op(dve, p["Wsm"].ap[:, g, :], ident[:16, :16], p["w00"].ap[:16, g:g + 1], None, ALU.mult, None,
                  reads=[cstt, p["w00"]], writes=[p["Wsm"]])
        for gl in range(2):
            sl = slice(64 * gl, 64 * gl + 64)
            pdma(pv["are"], v16("are")[sl, :], ssm_a_re[l].rearrange("(q g) n -> g n q", g=2)[gl])
            pdma(pv["aim"], v16("aim")[sl, :], ssm_a_im[l].rearrange("(q g) n -> g n q", g=2)[gl])
            pdma(pv["ldt"], v16("ldt")[sl, :], ssm_log_dt[l].rearrange("(q g) -> g q", g=2)[gl].partition_broadcast(64))
            for ri, src in enumerate([ssm_b_re, ssm_b_im]):
                pdma(pB[ri], pB[ri].ap[sl, :].rearrange("p (q h) -> p q h", q=16),
                     src[l].rearrange("(q g) n h -> g n q h", g=2)[gl])
        actf(v16("dt"), v16("ldt"), AF.Exp, [pv["ldt"]], [pv["dt"]])
        tt_op(dve, v16("zr"), v16("are"), v16("dt"), ALU.mult, [pv["are"], pv["dt"]], [pv["zr"]])
        tt_op(dve, v16("th"), v16("aim"), v16("dt"), ALU.mult, [pv["aim"], pv["dt"]], [pv["th"]])
        ts_op(dve, v16("p"), v16("zr"), 1.0 / 6.0, 1.0, ALU.mult, ALU.add, [pv["zr"]], [pv["p"]])
        for k in [5.0, 4.0, 3.0, 2.0]:
            tt_op(dve, v16("p"), v16("p"), v16("zr"), ALU.mult, [pv["p"], pv["zr"]], [pv["p"]])
            ts_op(dve, v16("p"), v16("p"), 1.0 / k, 1.0, ALU.mult, ALU.add, [pv["p"]], [pv["p"]])
        tt_op(dve, v16("p"), v16("p"), v16("zr"), ALU.mult, [pv["p"], pv["zr"]], [pv["p"]])
        ts_op(dve, v16("er"), v16("p"), 1.0, None, ALU.add, None, [pv["p"]], [pv["er"]])
        actf(v16("s"), v16("th"), AF.Sin, [pv["th"]], [pv["s"]], scale=1.0 / 16.0)
        ts_op(dve, v16("t1"), v16("th"), 1.0 / 16.0, float(np.pi / 2), ALU.mult, ALU.add, [pv["th"]], [pv["t1"]])
        actf(v16("c"), v16("t1"), AF.Sin, [pv["t1"]], [pv["c"]])
        for _ in range(4):
            tt_op(dve, v16("t1"), v16("c"), v16("c"), ALU.mult, [pv["c"]], [pv["t1"]])
            tt_op(dve, v16("t2"), v16("s"), v16("s"), ALU.mult, [pv["s"]], [pv["t2"]])
            tt_op(dve, v16("s"), v16("s"), v16("c"), ALU.mult, [pv["s"], pv["c"]], [pv["s"]])
            ts_op(dve, v16("s"), v16("s"), 2.0, None, ALU.mult, None, [pv["s"]], [pv["s"]])
            tt_op(dve, v16("c"), v16("t1"), v16("t2"), ALU.subtract, [pv["t1"], pv["t2"]], [pv["c"]])
        tt_op(dve, v16("abr"), v16("er"), v16("c"), ALU.mult, [pv["er"], pv["c"]], [pv["abr"]])
        tt_op(dve, v16("abi"), v16("er"), v16("s"), ALU.mult, [pv["er"], pv["s"]], [pv["abi"]])
        H = p["HSC"]
        evac(H.ap[:, 0, 0, :], v16("abr"), [pv["abr"]], [H], eng=dve)
        evac(H.ap[:, 0, 1, :], v16("abi"), [pv["abi"]], [H], eng=dve)
        ts_op(dve, H.ap[:, 0, 2, :], v16("abi"), -1.0, None, ALU.mult, None, [pv["abi"]], [H])
        evac(p["RHO"].ap, v16("er"), [pv["er"]], [p["RHO"]], eng=dve)
        Uv = uph.ap.rearrange("p (k c q) -> p k c q", k=NSTEP, c=3)
        evac(Uv[:, 0, 0, :], v16("c"), [pv["c"]], [uph], eng=dve)
        evac(Uv[:, 0, 1, :], v16("s"), [pv["s"]], [uph], eng=dve)
        for k in range(1, NSTEP):
            tt_op(dve, v16("t1"), Uv[:, k - 1, 0, :], Uv[:, k - 1, 0, :], ALU.mult, [uph], [pv["t1"]])
            tt_op(dve, v16("t2"), Uv[:, k - 1, 1, :], Uv[:, k - 1, 1, :], ALU.mult, [uph], [pv["t2"]])
            tt_op(dve, Uv[:, k, 0, :], v16("t1"), v16("t2"), ALU.subtract, [pv["t1"], pv["t2"]], [uph])
            tt_op(dve, v16("t1"), Uv[:, k - 1, 0, :], Uv[:, k - 1, 1, :], ALU.mult, [uph], [pv["t1"]])
            ts_op(dve, Uv[:, k, 1, :], v16("t1"), 2.0, None, ALU.mult, None, [pv["t1"]], [uph])
        for k in range(NSTEP):
            ts_op(dve, Uv[:, k, 2, :], Uv[:, k, 1, :], -1.0, None, ALU.mult, None, [uph], [uph])
        evac(p["UPH"].ap, Uv[:, 0, :, :], [uph], [p["UPH"]], eng=dve)
        Cr = prep_x[0].ap.rearrange("p (q r) -> p q r", q=16); Sr = prep_x[1].ap.rearrange("p (q r) -> p q r", q=16)
        Bc = pBb[0].ap.rearrange("p (q r) -> p q r", q=16); Bs = pBb[1].ap.rearrange("p (q r) -> p q r", q=16)
        T1 = prep_cst[0].ap.rearrange("p (q r) -> p q r", q=16); T2 = prep_cst[1].ap.rearrange("p (q r) -> p q r", q=16)
        tset = [prep_x[0], prep_x[1], pBb[0], pBb[1], prep_cst[0], prep_cst[1], uph]

        def dbl(Ctab, Stab, k0, nsteps):
            S.op(dve, lambda: V.memset(Ctab[:, :, 0:1], 1.0), reads=[], writes=tset)
            S.op(dve, lambda: V.memset(Stab[:, :, 0:1], 0.0), reads=[], writes=tset)
            for i in range(nsteps):
                d = 1 << i
                ckb = Uv[:, k0 + i, 0, :].unsqueeze(2).to_broadcast([128, 16, d])
                skb = Uv[:, k0 + i, 1, :].unsqueeze(2).to_broadcast([128, 16, d])
                tt_op(dve, T1[:, :, 0:d], Ctab[:, :, 0:d], ckb, ALU.mult, tset, tset)
                tt_op(dve, T2[:, :, 0:d], Stab[:, :, 0:d], skb, ALU.mult, tset, tset)
                tt_op(dve, Ctab[:, :, d:2 * d], T1[:, :, 0:d], T2[:, :, 0:d], ALU.subtract, tset, tset)
                tt_op(dve, T1[:, :, 0:d], Stab[:, :, 0:d], ckb, ALU.mult, tset, tset)
                tt_op(dve, T2[:, :, 0:d], Ctab[:, :, 0:d], skb, ALU.mult, tset, tset)
                tt_op(dve, Stab[:, :, d:2 * d], T1[:, :, 0:d], T2[:, :, 0:d], ALU.add, tset, tset)
        dbl(Cr, Sr, 0, 5)
        dbl(Bc[:, :, 0:16], Bs[:, :, 0:16], 5, 4)
        for q in range(16):
            def bm(tab):
                return tab[:, q, 0:16].unsqueeze(2).to_broadcast([128, 16, 32])

            def br(tab):
                return tab[:, q, :].unsqueeze(1).to_broadcast([128, 16, 32])
            c1 = scr.get(); c2 = scr.get(); s1_ = scr.get(); s2_ = scr.get()

            def v3(t):
                return t.ap.rearrange("p (m r) -> p m r", m=16)
            tt_op(dve, v3(c1), bm(Bc), br(Cr), ALU.mult, tset, [c1])
            tt_op(dve, v3(c2), bm(Bs), br(Sr), ALU.mult, tset, [c2])
            tt_op(dve, c1.ap, c1.ap, c2.ap, ALU.subtract, [c1, c2], [c1])
            tt_op(pool, v3(s1_), bm(Bs), br(Cr), ALU.mult, tset, [s1_])
            tt_op(pool, v3(s2_), bm(Bc), br(Sr), ALU.mult, tset, [s2_])
            tt_op(pool, s1_.ap, s1_.ap, s2_.ap, ALU.add, [s1_, s2_], [s1_])
            S.dma(tab_d[l, q, :, 0:512], c1.ap, reads=[c1], st=None, track=False)
            S.dma(tab_d[l, q, :, 512:1024], s1_.ap, reads=[s1_], st=None, track=False)
        ts_op(dve, v16("pp"), v16("abr"), -1.0, None, ALU.add, None, [pv["abr"]], [pv["pp"]])
        tt_op(dve, v16("t1"), v16("are"), v16("are"), ALU.mult, [pv["are"]], [pv["t1"]])
        tt_op(dve, v16("t2"), v16("aim"), v16("aim"), ALU.mult, [pv["aim"]], [pv["t2"]])
        tt_op(dve, v16("den"), v16("t1"), v16("t2"), ALU.add, [pv["t1"], pv["t2"]], [pv["den"]])
        S.op(dve, lambda: V.reciprocal(out=v16("den"), in_=v16("den")), reads=[pv["den"]], writes=[pv["den"]])
        tt_op(dve, v16("t1"), v16("pp"), v16("are"), ALU.mult, [pv["pp"], pv["are"]], [pv["t1"]])
        tt_op(dve, v16("t2"), v16("abi"), v16("aim"), ALU.mult, [pv["abi"], pv["aim"]], [pv["t2"]])
        tt_op(dve, v16("t1"), v16("t1"), v16("t2"), ALU.add, [pv["t1"], pv["t2"]], [pv["t1"]])
        tt_op(dve, v16("cfr"), v16("t1"), v16("den"), ALU.mult, [pv["t1"], pv["den"]], [pv["cfr"]])
        tt_op(dve, v16("t1"), v16("abi"), v16("are"), ALU.mult, [pv["abi"], pv["are"]], [pv["t1"]])
        tt_op(dve, v16("t2"), v16("pp"), v16("aim"), ALU.mult, [pv["pp"], pv["aim"]], [pv["t2"]])
        tt_op(dve, v16("t1"), v16("t1"), v16("t2"), ALU.subtract, [pv["t1"], pv["t2"]], [pv["t1"]])
        tt_op(dve, v16("cfi"), v16("t1"), v16("den"), ALU.mult, [pv["t1"], pv["den"]], [pv["cfi"]])
        ts_op(dve, v16("ncfi"), v16("cfi"), -1.0, None, ALU.mult, None, [pv["cfi"]], [pv["ncfi"]])
        for q in range(16):
            bq = slice(16 * q, 16 * q + 16)
            ts_op(dve, pBb[0].ap[:, bq], pB[0].ap[:, bq], v16("cfr")[:, q:q + 1], None, ALU.mult, None,
                  [pB[0], pv["cfr"]], [pBb[0]])
            stt(pBb[0].ap[:, bq], pB[1].ap[:, bq], v16("ncfi")[:, q:q + 1], pBb[0].ap[:, bq], ALU.mult, ALU.add,
                [pB[1], pv["ncfi"], pBb[0]], [pBb[0]])
            ts_op(dve, pBb[1].ap[:, bq], pB[1].ap[:, bq], v16("cfr")[:, q:q + 1], None, ALU.mult, None,
                  [pB[1], pv["cfr"]], [pBb[1]])
            stt(pBb[1].ap[:, bq], pB[0].ap[:, bq], v16("cfi")[:, q:q + 1], pBb[1].ap[:, bq], ALU.mult, ALU.add,
                [pB[0], pv["cfi"], pBb[1]], [pBb[1]])
        for ri in range(2):
            X = prep_x[ri]
            memset(X, X.ap)
            Xv = X.ap.rearrange("p (q g h) -> p q g h", q=16, g=2)
            Bv = pBb[ri].ap.rearrange("p (q h) -> p q h", q=16)
            evac(Xv[0:64, :, 0, :], Bv[0:64, :, :], [pBb[ri]], [X], eng=dve)
            evac(Xv[64:128, :, 1, :], Bv[64:128, :, :], [pBb[ri]], [X], eng=dve)
            pst = P.get()
            for c in range(4):
                transpose(pst.ap[:, c * 128:(c + 1) * 128], X.ap[:, c * 128:(c + 1) * 128], 128, [X], [pst])
            dstT = p["BbRe"] if ri == 0 else p["BbIm"]
            evac(dstT.ap.rearrange("p c m -> p (c m)"), pst.ap, [pst], [dstT])
            P.put(pst)
        for ri, src in enumerate([ssm_c_re, ssm_c_im]):
            Cs = prep_cst[ri]
            memset(Cs, Cs.ap)
            Cv = Cs.ap.rearrange("p (c m) -> p c m", c=4)
            for c in range(4):
                for gi in range(8):
                    S.dma(Cv[16 * gi:16 * gi + 16, c, 64 * (gi % 2):64 * (gi % 2) + 64], src[l, 8 * c + gi],
                          writes=[Cs], st=Cs, track=False, group=True)
            pst = P.get()
            for c in range(4):
                transpose(pst.ap[:, c * 128:(c + 1) * 128], Cs.ap[:, c * 128:(c + 1) * 128], 128, [Cs], [pst])
            if ri == 0:
                evac(p["CTre"].ap.rearrange("p c m -> p (c m)"), pst.ap, [pst], [p["CTre"]], eng=dve)
            else:
                ts_op(dve, p["CTim"].ap.rearrange("p c m -> p (c m)"), pst.ap, -1.0, None, ALU.mult, None, [pst], [p["CTim"]])
            P.put(pst)
    featvec(gfin, gfin.ap, g_final)
    nonc.__exit__(None, None, None)
    for i in range(2):
        memset(Cpad_re[i], Cpad_re[i].ap)
        memset(Cpad_im[i], Cpad_im[i].ap)

    def load_cpad(l, c):
        i = c % 2
        for j in range(4):
            evac(Cpad_re[i].ap[:, j, 32 * j:32 * j + 32], par[l]["CTre"].ap[:, c, 32 * j:32 * j + 32],
                 [par[l]["CTre"]], [Cpad_re[i]], eng=pool)
            evac(Cpad_im[i].ap[:, j, 32 * j:32 * j + 32], par[l]["CTim"].ap[:, c, 32 * j:32 * j + 32],
                 [par[l]["CTim"]], [Cpad_im[i]], eng=pool)

    def S5_ENG():
        return dve if S5_USE_DVE else pool

    tabstate = {"n": 0, "drained": False}

    def tab_load(l, q):
        if not tabstate["drained"]:
            S.drain_pool(S.sp)
            tabstate["drained"] = True
        slot = tabring[tabstate["n"] % 3]
        tabstate["n"] += 1
        S.dma(slot.ap, tab_d[l, q], writes=[slot], st=slot, track=False)
        return slot

    def rmsnorm(gcol_tile, gcol, dst, NT):
        pss = P.get()
        for f in range(8):
            sq = scr_r.get()
            actf(sq.ap[:, :NT], x[f].ap[:, :NT], AF.Square, [x[f]], [sq])
            S.op(pe, lambda: PE.matmul(pss.ap[:, :NT], lhsT=ones_d, rhs=sq.ap[:, :NT], start=(f == 0), stop=(f == 7)),
                 reads=[cstr, sq], writes=[pss])
        sd = scr.get()
        actf(sd.ap[:, :NT], pss.ap[:, :NT], AF.Sqrt, [pss], [sd], bias=EPS)
        P.put(pss)
        S.op(dve, lambda: V.reciprocal(out=sd.ap[:, :NT], in_=sd.ap[:, :NT]), reads=[sd], writes=[sd])
        for f in range(8):
            stt(dst[f].ap[:, :NT], x[f].ap[:, :NT], gcol(f), sd.ap[:, :NT], ALU.mult, ALU.mult,
                [x[f], gcol_tile, sd], [dst[f]])

    def layer(l, kind, ti, NT):
        p = par[l]
        samp = (kind == "s")
        first = (kind == "p" and ti == 0)
        last = (kind == "p" and ti == 3)
        R = 1 if samp else 4
        MT = NS if samp else 128
        gains = p["gains"]
        cf = p["cf"]
        rmsnorm(gains, lambda f: gains.ap[:, 0, f:f + 1], h, NT)
        tg = "%s%dl%d_" % (kind, ti, l)
        dbg(tg + "h0", h[0].ap[:, :NT], [h[0]])

        def a3(c):
            return aext[c].ap[:, 0:496].rearrange("p (b k) -> p b k", k=31)

        def build_dg(c):
            dgt_ = dg[c % 2]
            S.op(pool, lambda: G.tensor_tensor(out=dgt_.ap, in0=ident.unsqueeze(1).to_broadcast([128, 31, 128]),
                                               in1=p["caw"].ap[:, c, :].unsqueeze(2).to_broadcast([128, 31, 128]), op=ALU.mult),
                 reads=[cstt, p["caw"]], writes=[dgt_])
        build_dg(0)
        build_dg(1)

        if samp:
            rows = st_ca[l].rearrange("b k c -> (b k) c")
            for r4 in range(4):
                S.dma(hsAB[r4].ap[:120, :], rows[120 * r4:120 * r4 + 120, :], writes=[hsAB[r4]], st=hsAB[r4])
            for c in range(4):
                pst = P.get()
                for r4 in range(4):
                    transpose(pst.ap[:, r4 * 120:(r4 + 1) * 120], hsAB[r4].ap[:120, c * 128:(c + 1) * 128], 120, [hsAB[r4]], [pst])
                evac(a3(c)[:, :, 0:30], pst.ap[:, 0:480].rearrange("p (b k) -> p b k", k=30), [pst], [aext[c]])
                P.put(pst)
            S.dma(o_ca_s[l, :, 0:29, :], st_ca[l, :, 1:30, :], st=None)
        else:
            for c in range(4):
                if first:
                    memset(aext[c], aext[c].ap[:, 0:30])
                else:
                    evac(aext[c].ap[:, 0:30], p["carA"].ap[:, c, :], [p["carA"]], [aext[c]], eng=pool)
        for c in range(4):
            slot, (vl, vg) = w_next(act if samp else dve)
            psl = P.get(); psg = P.get()
            mm(psl.ap[:, :NT], [(vl[:, kt, :], h[kt].ap[:, :NT]) for kt in range(8)], [slot] + h, [psl])
            mm(psg.ap[:, :NT], [(vg[:, kt, :], h[kt].ap[:, :NT]) for kt in range(8)], [slot] + h, [psg])
            sg = scr.get()
            actf(sg.ap[:, :NT], psg.ap[:, :NT], AF.Sigmoid, [psg], [sg])
            dsta = a3(c)[:, :, 30] if samp else aext[c].ap[:, 30:30 + NT]
            tt_op(dve, dsta, psl.ap[:, :NT], sg.ap[:, :NT], ALU.mult, [psl, sg], [aext[c]])
            if samp:
                tt_op(dve, alast.ap[:, c, 0:NS], psl.ap[:, :NS], sg.ap[:, :NS], ALU.mult, [psl, sg], [alast])
            elif last:
                tt_op(dve, alast.ap[:, c, 0:30], psl.ap[:, NT - 30:NT], sg.ap[:, NT - 30:NT], ALU.mult, [psl, sg], [alast])
            P.put(psl, psg)
        if samp or last:
            pst = P.get()
            nr = NS if samp else 30
            for c in range(4):
                transpose(pst.ap[:nr, c * 128:(c + 1) * 128], alast.ap[:, c, 0:nr], 128, [alast], [pst])
            so = scr.get()
            evac(so.ap[:nr, :], pst.ap[:nr, :], [pst], [so])
            P.put(pst)
            if samp:
                S.dma(o_ca_s[l, :, 29, :], so.ap[:NS, :], reads=[so], st=so)
            else:
                S.dma(o_ca_p[l], so.ap[:30, :], reads=[so], st=so)
        if not samp and not last:
            for c in range(4):
                evac(p["carA"].ap[:, c, :], aext[c].ap[:, NT:NT + 30], [aext[c]], [p["carA"]], eng=pool)
        accs = hsAB
        for c in range(4):
            acc = accs[c]
            dgt = dg[c % 2]
            if c >= 2:
                build_dg(c)

            def tap(k):
                return a3(c)[:, :, k] if samp else aext[c].ap[:, k:k + NT]
            psc = P.get()
            mm(psc.ap[:, :NT], [(dgt.ap[:, k, :], tap(k)) for k in range(31)], [dgt, aext[c]], [psc])
            actf(acc.ap[:, :NT], psc.ap[:, :NT], AF.Identity, [psc, p["avec"]], [acc], bias=p["avec"].ap[:, 0, c:c + 1])
            P.put(psc)
        for c2 in range(2):
            slot, (vu,) = w_next(act if samp else dve)
            for cc in range(2):
                c = 2 * c2 + cc
                psu = P.get()
                mm(psu.ap[:, :NT], [(vu[:, kt, cc * 128:(cc + 1) * 128], h[kt].ap[:, :NT]) for kt in range(8)], [slot] + h, [psu])
                evac(ub[c].ap[:, :NT], psu.ap[:, :NT], [psu], [ub[c]])
                P.put(psu)
        psv = [P.get() for r in range(R)]
        for kh in range(2):
            slot, (vv,) = w_next(act if samp else dve)
            for r in range(R):
                def fn():
                    ins = None
                    for kk in range(4):
                        kt = 4 * kh + kk
                        ins = PE.matmul(psv[r].ap[:MT, :], lhsT=h[kt].ap[:, r * 128:r * 128 + MT], rhs=vv[:, kk, :],
                                        start=(kh == 0 and kk == 0), stop=(kh == 1 and kk == 3))
                    return ins
                S.op(pe, fn, reads=[slot] + h, writes=[psv[r]])
        lnb = p["lnb"]
        for r in range(R):
            st1 = small.get()
            vs = scr.get()
            actf(vs.ap[:MT, :], psv[r].ap[:MT, :], AF.Identity, [psv[r]], [vs, st1], accum=st1.ap[:MT, 0:1])
            junk = scr.get()
            actf(junk.ap[:MT, :], psv[r].ap[:MT, :], AF.Square, [psv[r]], [junk, st1], accum=st1.ap[:MT, 1:2])
            P.put(psv[r])
            ts_op(dve, st1.ap[:MT, 2:3], st1.ap[:MT, 0:1], 1.0 / DA, None, ALU.mult, None, [st1], [st1])
            tt_op(dve, st1.ap[:MT, 3:4], st1.ap[:MT, 2:3], st1.ap[:MT, 2:3], ALU.mult, [st1], [st1])
            stt(st1.ap[:MT, 4:5], st1.ap[:MT, 1:2], 1.0 / DA, st1.ap[:MT, 3:4], ALU.mult, ALU.subtract, [st1], [st1])
            actf(st1.ap[:MT, 5:6], st1.ap[:MT, 4:5], AF.Sqrt, [st1], [st1], bias=EPS)
            S.op(dve, lambda: V.reciprocal(out=st1.ap[:MT, 6:7], in_=st1.ap[:MT, 5:6]), reads=[st1], writes=[st1])
            stt(st1.ap[:MT, 7:8], st1.ap[:MT, 2:3], -1.0, st1.ap[:MT, 6:7], ALU.mult, ALU.mult, [st1], [st1])
            ts_op(dve, vs.ap[:MT, :], vs.ap[:MT, :], st1.ap[:MT, 6:7], st1.ap[:MT, 7:8], ALU.mult, ALU.add, [vs, st1], [vs])
            tt_op(pool, vs.ap[:MT, :], vs.ap[:MT, :], lnb.ap[:MT, 0, :], ALU.mult, [vs, lnb], [vs])
            if samp or (last and r == 3):
                tt_op(pool, vs.ap[:MT, :], vs.ap[:MT, :], lnb.ap[:MT, 1, :], ALU.add, [vs, lnb], [vs])
                evac(vtok[r].ap[:MT, :], vs.ap[:MT, :], [vs], [vtok[r]], eng=act)
                S.dma(o_cv_s[l] if samp else o_cv_p[l], vs.ap[:MT, :], reads=[vs], st=vs)
            else:
                tt_op(pool, vtok[r].ap[:MT, :], vs.ap[:MT, :], lnb.ap[:MT, 1, :], ALU.add, [vs, lnb], [vtok[r]])
        dbg(tg + "vtok0", vtok[0].ap[:MT, :], [vtok[0]])
        for g in range(4):
            psm_ = P.get()
            for r in range(R):
                if samp:
                    S.op(pe, lambda: PE.matmul(psm_.ap[:, :NS], lhsT=vtok[0].ap[:NS, g * 128:(g + 1) * 128],
                                               rhs=p["Wsm"].ap[:NS, g, :], start=True, stop=True),
                         reads=[vtok[0], p["Wsm"]], writes=[psm_])
                else:
                    def fn():
                        PE.matmul(psm_.ap[:, r * 128:(r + 1) * 128], lhsT=vtok[r].ap[:, g * 128:(g + 1) * 128],
                                  rhs=p["WsT"].ap[:, g, :], start=True, stop=False)
                        return PE.matmul(psm_.ap[:, r * 128:(r + 1) * 128], lhsT=ones_row,
                                         rhs=p["bs1"].ap[0:1, g, :], start=False, stop=True)
                    S.op(pe, fn, reads=[vtok[r], p["WsT"], p["bs1"], cstr], writes=[psm_])
            if samp:
                tmp = scr.get()
                ts_op(dve, tmp.ap[:, :NS], psm_.ap[:, :NS], p["bs0"].ap[:, g:g + 1], None, ALU.add, None, [psm_, p["bs0"]], [tmp])
                tt_op(dve, ub[g].ap[:, :NS], ub[g].ap[:, :NS], tmp.ap[:, :NS], ALU.mult, [ub[g], tmp], [ub[g]])
            else:
                tt_op(dve, ub[g].ap[:, :NT], ub[g].ap[:, :NT], psm_.ap[:, :NT], ALU.mult, [ub[g], psm_], [ub[g]])
            P.put(psm_)
        dbg(tg + "ub0", ub[0].ap[:, :NT], [ub[0]])

        for c2 in range(2):
            slot, (vz,) = w_next(act if samp else dve)
            for cc in range(2):
                c = 2 * c2 + cc
                psz = P.get()
                mm(psz.ap[:, :NT], [(vz[:, kt, cc * 128:(cc + 1) * 128], h[kt].ap[:, :NT]) for kt in range(8)], [slot] + h, [psz])
                evac(zc[c].ap[:, :NT], psz.ap[:, :NT], [psz], [zc[c]])
                P.put(psz)
        dbg(tg + "zc0", zc[0].ap[:, :NT], [zc[0]])
        ps1 = P.get(); ps2 = P.get()
        for c in range(4):
            a_r = scr_r.get()
            evac(a_r.ap[:, :NT], accs[c].ap[:, :NT], [accs[c]], [a_r], eng=act)
            S.op(pe, lambda: PE.matmul(ps1.ap[:, :NT], lhsT=ones_c, rhs=a_r.ap[:, :NT], start=(c == 0), stop=(c == 3)),
                 reads=[cstr, a_r], writes=[ps1])
            sq = scr_r.get()
            actf(sq.ap[:, :NT], accs[c].ap[:, :NT], AF.Square, [accs[c]], [sq])
            S.op(pe, lambda: PE.matmul(ps2.ap[:, :NT], lhsT=ones_c, rhs=sq.ap[:, :NT], start=(c == 0), stop=(c == 3)),
                 reads=[cstr, sq], writes=[ps2])
        mean = scr.get()
        evac(mean.ap[:, :NT], ps1.ap[:, :NT], [ps1], [mean], eng=act)
        var = scr.get()
        tt_op(dve, var.ap[:, :NT], mean.ap[:, :NT], mean.ap[:, :NT], ALU.mult, [mean], [var])
        tt_op(dve, var.ap[:, :NT], ps2.ap[:, :NT], var.ap[:, :NT], ALU.subtract, [ps2, var], [var])
        P.put(ps1, ps2)
        actf(var.ap[:, :NT], var.ap[:, :NT], AF.Sqrt, [var], [var], bias=EPS)
        S.op(dve, lambda: V.reciprocal(out=var.ap[:, :NT], in_=var.ap[:, :NT]), reads=[var], writes=[var])
        for c in range(4):
            tt_op(pool, accs[c].ap[:, :NT], accs[c].ap[:, :NT], mean.ap[:, :NT], ALU.subtract, [accs[c], mean], [accs[c]])
            tt_op(dve, accs[c].ap[:, :NT], accs[c].ap[:, :NT], var.ap[:, :NT], ALU.mult, [accs[c], var], [accs[c]])
            actf(acs[c].ap[:, :NT], accs[c].ap[:, :NT], AF.Silu, [accs[c], p["avec"]], [acs[c]],
                 scale=p["avec"].ap[:, 1, c:c + 1], bias=p["avec"].ap[:, 2, c:c + 1])
        dbg(tg + "acs0", acs[0].ap[:, :NT], [acs[0]])

        def merge1_pe(f):
            s1_, (vgA, vgB) = w_next(dve)
            pgA = P.get(); pgB = P.get()
            mm(pgA.ap[:, :NT], [(vgA[:, kt, :], h[kt].ap[:, :NT]) for kt in range(8)], [s1_] + h, [pgA])
            mm(pgB.ap[:, :NT], [(vgB[:, kt, :], h[kt].ap[:, :NT]) for kt in range(8)], [s1_] + h, [pgB])
            s2_, (vao, vbo) = w_next(dve)
            pao = P.get(); pbo = P.get()
            mm(pao.ap[:, :NT], [(vao[:, kt, :], acs[kt].ap[:, :NT]) for kt in range(4)], [s2_] + acs, [pao])
            mm(pbo.ap[:, :NT], [(vbo[:, kt, :], ub[kt].ap[:, :NT]) for kt in range(4)], [s2_] + ub, [pbo])
            sA = scr.get(); sB = scr.get()
            actf(sA.ap[:, :NT], pgA.ap[:, :NT], AF.Sigmoid, [pgA], [sA])
            actf(sB.ap[:, :NT], pgB.ap[:, :NT], AF.Sigmoid, [pgB], [sB])
            P.put(pgA, pgB)
            return (sA, sB, pao, pbo)

        def merge1_rest(f, st_):
            sA, sB, pao, pbo = st_
            tt_op(dve, sA.ap[:, :NT], pao.ap[:, :NT], sA.ap[:, :NT], ALU.mult, [pao, sA], [sA])
            tt_op(dve, sB.ap[:, :NT], pbo.ap[:, :NT], sB.ap[:, :NT], ALU.mult, [pbo, sB], [sB])
            P.put(pao, pbo)
            tt_op(pool, m[f].ap[:, :NT], sA.ap[:, :NT], sB.ap[:, :NT], ALU.add, [sA, sB], [m[f]])

        H = p["HSC"]
        carS = p["carS"]
        s0T = xstage[0]
        if samp:
            for ri, srcst in enumerate([st_re, st_im]):
                for i4 in range(4):
                    S.dma(hsAB[i4].ap[:NS, :], srcst[l][:, 512 * i4:512 * (i4 + 1)], writes=[hsAB[i4]], st=hsAB[i4])
                pst = P.get()
                for q in range(16):
                    transpose(pst.ap[:, 16 * q:16 * q + 16], hsAB[q // 4].ap[:NS, 128 * (q % 4):128 * (q % 4 + 1)], NS,
                              [hsAB[q // 4]], [pst])
                evac(s0T.ap[:, 256 * ri:256 * ri + 256], pst.ap[:, 0:256], [pst], [s0T])
                P.put(pst)
        psy = None
        srow = {}
        pend_y = []
        m1next = {}

        def flush_y(c_, psy_):
            ysb = scr.get()
            stt(ysb.ap[:, :NT], zc[c_].ap[:, :NT], p["avec"].ap[:, 3, c_:c_ + 1], psy_.ap[:, :NT], ALU.mult, ALU.add,
                [zc[c_], p["avec"], psy_], [ysb])
            P.put(psy_)
            actf(zc[c_].ap[:, :NT], ysb.ap[:, :NT], AF.Gelu_apprx_tanh, [ysb], [zc[c_]])
            if c_ == 0:
                dbg(tg + "gy0", zc[0].ap[:, :NT], [zc[0]])
        tabs = {}
        bu = {}
        if not samp:
            tabs[0] = tab_load(l, 0)
            tabs[1] = tab_load(l, 1)
        for q in range(16):
            c = q // 4; j = q % 4
            if j == 0:
                load_cpad(l, c)
            def emit_bu(qq):
                cc_ = qq // 4; jj_ = qq % 4
                rs_ = slice(32 * jj_, 32 * jj_ + 32)
                a_ = P.get(); b_ = P.get()
                S.op(pe, lambda: PE.matmul(a_.ap[:, :NT], lhsT=p["BbRe"].ap[rs_, cc_, :], rhs=zc[cc_].ap[rs_, :NT], start=True,
                                           stop=True, tile_position=(32 * jj_, 0)), reads=[p["BbRe"], zc[cc_]], writes=[a_])
                S.op(pe, lambda: PE.matmul(b_.ap[:, :NT], lhsT=p["BbIm"].ap[rs_, cc_, :], rhs=zc[cc_].ap[rs_, :NT], start=True,
                                           stop=True, tile_position=(32 * jj_, 0)), reads=[p["BbIm"], zc[cc_]], writes=[b_])
                return a_, b_
            def pre(qq):
                a_, b_ = bu.pop(qq)
                tq = tabs[qq]
                Cq = tq.ap[:, 0:NT]; Sq = tq.ap[:, 512:512 + NT]
                v0, v1, v2, v3 = (hsAB if qq % 2 == 0 else hsCD)
                tt_op(dve, v0.ap[:, :NT], a_.ap[:, :NT], Cq, ALU.mult, [a_, tq], [v0])
                tt_op(dve, v1.ap[:, :NT], b_.ap[:, :NT], Sq, ALU.mult, [b_, tq], [v1])
                tt_op(dve, v2.ap[:, :NT], b_.ap[:, :NT], Cq, ALU.mult, [b_, tq], [v2])
                tt_op(dve, v3.ap[:, :NT], a_.ap[:, :NT], Sq, ALU.mult, [a_, tq], [v3])
                P.put(a_, b_)
            if q == 0:
                bu[0] = emit_bu(0)
                if not samp:
                    pre(0)
            if samp:
                psr, psi = bu.pop(q)
            c1 = H.ap[:, 0, 0, q:q + 1]; s1 = H.ap[:, 0, 1, q:q + 1]; ns1 = H.ap[:, 0, 2, q:q + 1]
            m1st = None
            if (not samp) and q % 2 == 0:
                m1st = m1next.pop(q) if q in m1next else merge1_pe(q // 2)
            if samp:
                cur = hsAB[0:2]
                evac(cur[0].ap[:, :NT], psr.ap[:, :NT], [psr], [cur[0]], eng=act)
                evac(cur[1].ap[:, :NT], psi.ap[:, :NT], [psi], [cur[1]], eng=act)
                P.put(psr, psi)
                sre = s0T.ap[:, 0:256].rearrange("p (q b) -> p q b", q=16)[:, q, :]
                sim = s0T.ap[:, 256:512].rearrange("p (q b) -> p q b", q=16)[:, q, :]
                stt(cur[0].ap[:, :NT], sre, c1, cur[0].ap[:, :NT], ALU.mult, ALU.add, [s0T, H, cur[0]], [cur[0]])
                stt(cur[0].ap[:, :NT], sim, ns1, cur[0].ap[:, :NT], ALU.mult, ALU.add, [s0T, H, cur[0]], [cur[0]])
                stt(cur[1].ap[:, :NT], sim, c1, cur[1].ap[:, :NT], ALU.mult, ALU.add, [s0T, H, cur[1]], [cur[1]])
                stt(cur[1].ap[:, :NT], sre, s1, cur[1].ap[:, :NT], ALU.mult, ALU.add, [s0T, H, cur[1]], [cur[1]])
                fin = hsF
                for ri in range(2):
                    evac(hsF[ri].ap[:, :NT], cur[ri].ap[:, :NT], [cur[ri]], [hsF[ri]], eng=act)
                    if j == 0:
                        srow[ri] = P.get()
                    transpose(srow[ri].ap[:NS, 128 * j:128 * (j + 1)], cur[ri].ap[:, :NS], 128, [cur[ri]], [srow[ri]])
                    if j == 3:
                        so = scr.get()
                        evac(so.ap[:NS, :], srow[ri].ap[:NS, :], [srow[ri]], [so])
                        P.put(srow[ri])
                        dsto = (o_re_s if ri == 0 else o_im_s)[l, :, q - 3:q + 1, :]
                        S.dma(dsto, so.ap[:NS, :].rearrange("b (q m) -> b q m", q=4), reads=[so], st=so)
            else:
                tb = tabs[q]
                if q + 2 < 16:
                    tabs[q + 2] = tab_load(l, q + 2)
                Ct = tb.ap[:, 0:NT]; Sn = tb.ap[:, 512:512 + NT]
                w0, w1, w2, w3 = (hsAB if q % 2 == 0 else hsCD)
                hf = hsF2[q % 2]
                tt_op(dve, w0.ap[:, :NT], w0.ap[:, :NT], w1.ap[:, :NT], ALU.add, [w0, w1], [w0])
                tt_op(dve, w2.ap[:, :NT], w2.ap[:, :NT], w3.ap[:, :NT], ALU.subtract, [w2, w3], [w2])
                rho = p["RHO"].ap[:, q:q + 1].to_broadcast([128, NT])
                if first:
                    ini_re = 0.0; ini_im = 0.0
                    ird = []
                else:
                    st0 = small.get()
                    U = p["UPH"]
                    sre = carS.ap[:, 0, q:q + 1]; sim = carS.ap[:, 1, q:q + 1]
                    uc = U.ap[:, 0, q:q + 1]; us = U.ap[:, 1, q:q + 1]; uns = U.ap[:, 2, q:q + 1]
                    ts_op(dve, st0.ap[:, 0:1], sre, uc, None, ALU.mult, None, [carS, U], [st0])
                    stt(st0.ap[:, 0:1], sim, uns, st0.ap[:, 0:1], ALU.mult, ALU.add, [carS, U, st0], [st0])
                    ts_op(dve, st0.ap[:, 1:2], sim, uc, None, ALU.mult, None, [carS, U], [st0])
                    stt(st0.ap[:, 1:2], sre, us, st0.ap[:, 1:2], ALU.mult, ALU.add, [carS, U, st0], [st0])
                    ini_re = st0.ap[:, 0:1]; ini_im = st0.ap[:, 1:2]
                    ird = [st0]
                S.op(dve, lambda: V.tensor_tensor_scan(out=w1.ap[:, :NT], data0=rho, data1=w0.ap[:, :NT], initial=ini_re,
                                                       op0=ALU.mult, op1=ALU.add), reads=[w0, p["RHO"]] + ird, writes=[w1])
                S.op(dve, lambda: V.tensor_tensor_scan(out=w3.ap[:, :NT], data0=rho, data1=w2.ap[:, :NT], initial=ini_im,
                                                       op0=ALU.mult, op1=ALU.add), reads=[w2, p["RHO"]] + ird, writes=[w3])
                if q + 1 < 16:
                    bu[q + 1] = emit_bu(q + 1)
                    pre(q + 1)
                while pend_y:
                    flush_y(*pend_y.pop(0))
                pa = scr.get(); pb = scr.get(); pd = scr.get()
                tt_op(pool, pd.ap[:, :NT], w3.ap[:, :NT], Sn, ALU.mult, [w3, tb], [pd])
                tt_op(pool, pa.ap[:, :NT], w3.ap[:, :NT], Ct, ALU.mult, [w3, tb], [pa])
                tt_op(pool, pb.ap[:, :NT], w1.ap[:, :NT], Sn, ALU.mult, [w1, tb], [pb])
                tt_op(dve, w0.ap[:, :NT], w1.ap[:, :NT], Ct, ALU.mult, [w1, tb], [w0])
                tt_op(dve, w0.ap[:, :NT], w0.ap[:, :NT], pd.ap[:, :NT], ALU.subtract, [w0, pd], [w0])
                tt_op(pool, pa.ap[:, :NT], pa.ap[:, :NT], pb.ap[:, :NT], ALU.add, [pa, pb], [pa])
                evac(hf[0].ap[:, :NT], w0.ap[:, :NT], [w0], [hf[0]], eng=act)
                evac(hf[1].ap[:, :NT], pa.ap[:, :NT], [pa], [hf[1]], eng=act)
                fin = hf
                evac(carS.ap[:, 0, q:q + 1], w0.ap[:, NT - 1:NT], [w0], [carS], eng=pool)
                evac(carS.ap[:, 1, q:q + 1], pa.ap[:, NT - 1:NT], [pa], [carS], eng=pool)
            if samp and q + 1 < 16:
                bu[q + 1] = emit_bu(q + 1)
            if m1st is not None:
                merge1_rest(q // 2, m1st)
            if (not samp) and q % 2 == 1 and q + 1 < 16:
                m1next[q + 1] = merge1_pe((q + 1) // 2)
            if j == 0:
                psy = P.get()
            cpr = Cpad_re[c % 2]; cpi = Cpad_im[c % 2]
            if q == 0:
                dbg(tg + "sre0", fin[0].ap[:, :NT], [fin[0]])
                dbg(tg + "sim0", fin[1].ap[:, :NT], [fin[1]])

            def fny():
                PE.matmul(psy.ap[:, :NT], lhsT=cpr.ap[:, j, :], rhs=fin[0].ap[:, :NT], start=(j == 0), stop=False)
                return PE.matmul(psy.ap[:, :NT], lhsT=cpi.ap[:, j, :], rhs=fin[1].ap[:, :NT], start=False, stop=(j == 3))
            S.op(pe, fny, reads=[cpr, cpi, fin[0], fin[1]], writes=[psy])
            if j == 3:
                if samp:
                    flush_y(c, psy)
                else:
                    pend_y.append((c, psy))
        while pend_y:
            flush_y(*pend_y.pop(0))
        if last:
            for ri in range(2):
                pst = P.get()
                transpose(pst.ap[:16, 0:128], carS.ap[:, ri, :], 128, [carS], [pst])
                so = scr.get()
                evac(so.ap[:16, 0:128], pst.ap[:16, 0:128], [pst], [so])
                P.put(pst)
                S.dma((o_re_p if ri == 0 else o_im_p)[l], so.ap[:16, 0:128], reads=[so], st=so)

        if samp:
            for f in range(8):
                st_ = merge1_pe(f)
                merge1_rest(f, st_)
        for f in range(8):
            s3_, (vgC, vcl, vcg) = w_next(dve)
            pgC = P.get(); pcl = P.get(); pcg = P.get()
            mm(pgC.ap[:, :NT], [(vgC[:, kt, :], h[kt].ap[:, :NT]) for kt in range(8)], [s3_] + h, [pgC])
            mm(pcl.ap[:, :NT], [(vcl[:, kt, :], zc[kt].ap[:, :NT]) for kt in range(4)], [s3_] + zc, [pcl])
            mm(pcg.ap[:, :NT], [(vcg[:, kt, :], zc[kt].ap[:, :NT]) for kt in range(4)], [s3_] + zc, [pcg])
            sC = scr.get(); sG = scr.get()
            actf(sC.ap[:, :NT], pgC.ap[:, :NT], AF.Sigmoid, [pgC], [sC])
            actf(sG.ap[:, :NT], pcg.ap[:, :NT], AF.Sigmoid, [pcg], [sG])
            tt_op(dve, sG.ap[:, :NT], pcl.ap[:, :NT], sG.ap[:, :NT], ALU.mult, [pcl, sG], [sG])
            P.put(pgC, pcl, pcg)
            tt_op(pool, sG.ap[:, :NT], sG.ap[:, :NT], sC.ap[:, :NT], ALU.mult, [sG, sC], [sG])
            tt_op(dve, m[f].ap[:, :NT], m[f].ap[:, :NT], sG.ap[:, :NT], ALU.add, [m[f], sG], [m[f]])
        dbg(tg + "m0", m[0].ap[:, :NT], [m[0]])
        for c4 in range(4):
            slot, (vo,) = w_next(dve)
            for cc in range(2):
                f = 2 * c4 + cc
                pso = P.get()
                mm(pso.ap[:, :NT], [(vo[:, kt, cc * 128:(cc + 1) * 128], m[kt].ap[:, :NT]) for kt in range(8)], [slot] + m, [pso])
                tt_op(dve, x[f].ap[:, :NT], x[f].ap[:, :NT], pso.ap[:, :NT], ALU.add, [x[f], pso], [x[f]])
                P.put(pso)

        S.phase_barrier(ffn_tiles)
        dbg(tg + "xmix0", x[0].ap[:, :NT], [x[0]])
        rmsnorm(gains, lambda f: gains.ap[:, 1, f:f + 1], h, NT)
        carF = p["carF"]
        hist = pesb[0:3]
        hist_ap = arenaF[:, pesb_off:pesb_off + 44 * 32].rearrange("p (j b k) -> p j b k", j=44, k=2)
        if samp:
            rows = st_cf[l].rearrange("b k c -> (b k) c")
            S.dma(o_cf_s[l, :, 0, :], st_cf[l, :, 1, :], st=None)
            for j4 in range(11):
                rw = scr.get()
                S.dma(rw.ap[:32, :], rows[:, 512 * j4:512 * (j4 + 1)], writes=[rw], st=rw)
                pst = P.get()
                for jj in range(4):
                    transpose(pst.ap[:, 32 * jj:32 * jj + 32], rw.ap[:32, 128 * jj:128 * (jj + 1)], 32, [rw], [pst])
                evac(arenaF[:, pesb_off + 128 * j4: pesb_off + 128 * (j4 + 1)], pst.ap[:, 0:128], [pst], hist)
                P.put(pst)
        for j in range(NJ):
            slot, (vg_, vv_) = w_next(dve if j % 2 else act)
            outs = []
            for half, (vw, idx, ebuf) in enumerate([(vg_, j, eg[j % 2]), (vv_, NJ + j, ev[j % 2])]):
                psu = P.get()
                mm(psu.ap[:, :NT], [(vw[:, kt, :], h[kt].ap[:, :NT]) for kt in range(8)], [slot] + h, [psu])
                t0 = scr.get()
                actf(t0.ap[:, :NT], psu.ap[:, :NT], AF.Identity, [psu, cf], [t0],
                     scale=cf.ap[:, idx, 2:3], bias=cf.ap[:, idx, 3:4])
                if samp:
                    hv = hist_ap[:, idx, :, :]
                    stt(t0.ap[:, :NT], hv[:, :, 1], cf.ap[:, idx, 1:2], t0.ap[:, :NT], ALU.mult, ALU.add, hist + [cf, t0], [t0])
                    stt(t0.ap[:, :NT], hv[:, :, 0], cf.ap[:, idx, 0:1], t0.ap[:, :NT], ALU.mult, ALU.add, hist + [cf, t0], [t0])
                    rawc = ebuf
                    evac(rawc.ap[:, :NT], psu.ap[:, :NT], [psu], [rawc], eng=act)
                    P.put(psu)
                    upst = P.get()
                    transpose(upst.ap[:NS, 0:128], rawc.ap[:, :NS], 128, [rawc], [upst])
                    so2 = scr.get()
                    evac(so2.ap[:NS, 0:128], upst.ap[:NS, 0:128], [upst], [so2])
                    P.put(upst)
                    S.dma(o_cf_s[l, :, 1, 128 * idx:128 * (idx + 1)], so2.ap[:NS, 0:128], reads=[so2], st=so2)
                else:
                    evac(ebuf.ap[:, 2:2 + NT], psu.ap[:, :NT], [psu], [ebuf], eng=act)
                    P.put(psu)
                    if first:
                        memset(ebuf, ebuf.ap[:, 0:2])
                    else:
                        evac(ebuf.ap[:, 0:2], carF.ap[:, idx, :], [carF], [ebuf], eng=pool)
                    stt(t0.ap[:, :NT], ebuf.ap[:, 1:1 + NT], cf.ap[:, idx, 1:2], t0.ap[:, :NT], ALU.mult, ALU.add, [ebuf, cf, t0], [t0])
                    stt(t0.ap[:, :NT], ebuf.ap[:, 0:NT], cf.ap[:, idx, 0:1], t0.ap[:, :NT], ALU.mult, ALU.add, [ebuf, cf, t0], [t0])
                    evac(carF.ap[:, idx, :], ebuf.ap[:, NT:NT + 2], [ebuf], [carF], eng=pool)
                outs.append(t0)
            gl_ = scr.get()
            actf(gl_.ap[:, :NT], outs[0].ap[:, :NT], AF.Gelu_apprx_tanh, [outs[0]], [gl_])
            tt_op(dve, actt[j].ap[:, :NT], gl_.ap[:, :NT], outs[1].ap[:, :NT], ALU.mult, [gl_, outs[1]], [actt[j]])
        if last:
            pst = P.get()
            transpose(pst.ap[:88, 0:128], carF.ap.rearrange("p j k -> p (j k)"), 128, [carF], [pst])
            so = scr.get()
            evac(so.ap[:88, 0:128], pst.ap[:88, 0:128], [pst], [so])
            P.put(pst)
            S.dma(o_cf_p[l].rearrange("k (j c) -> j k c", c=128), so.ap[:88, 0:128], reads=[so], st=so)
        for fh in range(2):
            pacc = [P.get() for _ in range(4)]
            for j0 in range(0, NJ, 4):
                nj = min(4, NJ - j0)
                slot, (vd,) = w_next(dve)

                def fn():
                    ins = None
                    for jj in range(nj):
                        for fi in range(4):
                            ins = PE.matmul(pacc[fi].ap[:, :NT], lhsT=vd[:, jj, fi * 128:(fi + 1) * 128],
                                            rhs=actt[j0 + jj].ap[:, :NT], start=(j0 + jj == 0), stop=(j0 + jj == NJ - 1))
                    return ins
                S.op(pe, fn, reads=[slot] + actt[j0:j0 + nj], writes=pacc)
            for fi in range(4):
                f = 4 * fh + fi
                tt_op(dve, x[f].ap[:, :NT], x[f].ap[:, :NT], pacc[fi].ap[:, :NT], ALU.add, [x[f], pacc[fi]], [x[f]])
                P.put(pacc[fi])

        dbg(tg + "act0", actt[0].ap[:, :NT], [actt[0]])
        dbg(tg + "xffn0", x[0].ap[:, :NT], [x[0]])
        rmsnorm(gains, lambda f: gains.ap[:, 2, f:f + 1], h, NT)
        if samp:
            pstg = scr.get()
            S.dma(pstg.ap[:NS, 0:256], psm[l], writes=[pstg], st=pstg)
            pst = P.get()
            for kt in range(2):
                transpose(pst.ap[:, kt * 16:kt * 16 + 16], pstg.ap[:NS, kt * 128:(kt + 1) * 128], NS, [pstg], [pst])
            for kt in range(2):
                evac(pT[kt].ap[:, :NS], pst.ap[:, kt * 16:kt * 16 + 16], [pst], [pT[kt]])
            P.put(pst)
        else:
            pstg = [scr.get(), scr.get()]
            for r in range(4):
                tk = TT * ti + 128 * r
                S.dma(pstg[r // 2].ap[:, (r % 2) * 256:(r % 2) * 256 + 256], pp[l, tk:tk + 128, :], writes=[pstg[r // 2]], st=pstg[r // 2])
            for kt in range(2):
                pst = P.get()
                for r in range(4):
                    transpose(pst.ap[:, r * 128:(r + 1) * 128],
                              pstg[r // 2].ap[:, (r % 2) * 256 + kt * 128:(r % 2) * 256 + (kt + 1) * 128], 128, [pstg[r // 2]], [pst])
                evac(pT[kt].ap[:, :NT], pst.ap[:, :NT], [pst], [pT[kt]])
                P.put(pst)
        slot, (vpe,) = w_next(dve)
        for f in range(8):
            pse = P.get()
            mm(pse.ap[:, :NT], [(vpe[:, kt, f * 128:(f + 1) * 128], pT[kt].ap[:, :NT]) for kt in range(2)], [slot] + pT, [pse])
            evac(pesb[f].ap[:, :NT], pse.ap[:, :NT], [pse], [pesb[f]])
            P.put(pse)
        for c4 in range(4):
            slot, (vpg,) = w_next(dve)
            for cc in range(2):
                f = 2 * c4 + cc
                psg = P.get()
                mm(psg.ap[:, :NT], [(vpg[:, kt, cc * 128:(cc + 1) * 128], h[kt].ap[:, :NT]) for kt in range(8)], [slot] + h, [psg])
                sg = scr.get()
                actf(sg.ap[:, :NT], psg.ap[:, :NT], AF.Sigmoid, [psg], [sg])
                P.put(psg)
                tt_op(pool, sg.ap[:, :NT], sg.ap[:, :NT], pesb[f].ap[:, :NT], ALU.mult, [sg, pesb[f]], [sg])
                tt_op(dve, x[f].ap[:, :NT], x[f].ap[:, :NT], sg.ap[:, :NT], ALU.add, [x[f], sg], [x[f]])
        dbg(tg + "xple0", x[0].ap[:, :NT], [x[0]])

    for (kind, ti) in tiles_order:
        samp = (kind == "s")
        NT = NS if samp else TT
        R = 1 if samp else 4
        MT = NS if samp else 128
        S.phase_barrier(mixer_tiles)
        for r in range(R):
            xst = xstage[r % 2]
            if samp:
                S.dma(xst.ap[:NS, :], xs, writes=[xst], st=xst)
            else:
                S.dma(xst.ap[:, :], xp[TT * ti + 128 * r: TT * ti + 128 * (r + 1), :], writes=[xst], st=xst)
            for fg in range(2):
                pst = P.get()
                for k in range(4):
                    f = 4 * fg + k
                    transpose(pst.ap[:, k * 128:k * 128 + MT], xst.ap[:MT, f * 128:(f + 1) * 128], MT, [xst], [pst])
                evac(xbig[:, 4 * fg:4 * fg + 4, r * 128:r * 128 + MT],
                     pst.ap.rearrange("p (k t) -> p k t", k=4)[:, :, 0:MT], [pst], x[4 * fg:4 * fg + 4])
                P.put(pst)
        dbg("%s%d_x0" % (kind, ti), x[0].ap[:, :NT], [x[0]])
        for l in range(NL):
            if l > 0:
                S.phase_barrier(mixer_tiles)
            layer(l, kind, ti, NT)
        S.phase_barrier(fin_tiles)
        rmsnorm(gfin, lambda f: gfin.ap[:, f:f + 1], yT, NT)
        for r in range(R):
            ys = ystage[r % 2]
            for fg in range(2):
                pst = P.get()
                for k in range(4):
                    f = 4 * fg + k
                    transpose(pst.ap[:MT, k * 128:(k + 1) * 128], yT[f].ap[:, r * 128:r * 128 + MT], 128, [yT[f]], [pst])
                evac(ys.ap[:MT, fg * 512:(fg + 1) * 512], pst.ap[:MT, :], [pst], [ys])
                P.put(pst)
            if samp:
                S.dma(y_s, ys.ap[:NS, :], reads=[ys], st=ys)
            else:
                S.dma(y_p[TT * ti + 128 * r: TT * ti + 128 * (r + 1), :], ys.ap[:, :], reads=[ys], st=ys)
    assert wstate["next_use"] == len(gchunks), (wstate, len(gchunks))
    S.finish()
    print("SBUF bytes remaining:", nc.sbuf_bytes_remaining, "sems:", S.nsem,
          "ops:", {E.name: E.count for E in S.compute})


_NC_CACHE = {}


def pack_weights(g):
    nch = len(layer_chunks(0))
    out = np.zeros((L, nch, 128, SLOT), np.float32)
    for l in range(L):
        for ci, chunk in enumerate(layer_chunks(l)):
            o = 0
            for (wn, l_, r0, nkt, c0, nc_) in chunk:
                blk = g[wn][l, r0:r0 + 128 * nkt, c0:c0 + nc_].reshape(nkt, 128, nc_)
                out[l, ci, :, o:o + nkt * nc_] = blk.transpose(1, 0, 2).reshape(128, nkt * nc_)
                o += nkt * nc_
    return out


def kernel(**inp):
    f32 = np.float32
    g = {k: np.ascontiguousarray(np.asarray(v), dtype=f32) for k, v in inp.items()}
    if "nc" not in _NC_CACHE:
        _NC_CACHE["nc"] = build_nc()
    nc = _NC_CACHE["nc"]
    cst = np.zeros((128, 256), f32)
    cst[:, 0:128] = np.eye(128, dtype=f32)
    cst[:, 128:256] = np.tril(np.ones((128, 128), f32))
    cst_r = np.zeros((128, 384), f32)
    cst_r[:, 0:128] = 1.0 / 1024.0
    cst_r[:, 128:256] = 1.0 / 512.0
    cst_r[:, 256:384] = 1.0
    shared = {
        "cst": cst, "cst_r": cst_r, "b_s_r": g["b_s"],
    }
    shared["wpack"] = pack_weights(g)
    for k in ["g_mix", "g_ffn", "g_ple",
              "g_final", "conv_a_w", "conv_a_b", "ln_a_g", "ln_a_b", "ln_b_g", "ln_b_b", "w_s", "b_s", "ssm_a_re", "ssm_a_im",
              "ssm_log_dt", "ssm_b_re", "ssm_b_im", "ssm_c_re", "ssm_c_im", "ssm_d", "conv_f_w", "conv_f_b"]:
        shared[k] = g[k]
    shared["conv_f_b"] = np.ascontiguousarray(g["conv_f_b"].reshape(L, 1, 2 * DFF))
    in_maps = []
    for c in range(8):
        sl = slice(NS * c, NS * (c + 1))
        d = dict(shared)
        d["xp"] = g["x_prompt"][c]
        d["xs"] = np.ascontiguousarray(g["x_sample"][sl, 0, :])
        d["pp"] = np.ascontiguousarray(g["p_prompt"][:, c])
        d["psm"] = np.ascontiguousarray(g["p_sample"][:, sl, 0, :])
        d["st_ca"] = np.ascontiguousarray(g["state_conv_a"][:, sl])
        d["st_re"] = np.ascontiguousarray(g["state_ssm_re"][:, sl].reshape(L, NS, 2048))
        d["st_im"] = np.ascontiguousarray(g["state_ssm_im"][:, sl].reshape(L, NS, 2048))
        d["st_cf"] = np.ascontiguousarray(g["state_conv_ffn"][:, sl])
        in_maps.append(d)
    res = run_bass_kernel_spmd(nc, in_maps, core_ids=list(range(8)))
    rs = res.results

    def cat(name, axis, shape=None):
        a = np.concatenate([np.asarray(r[name], dtype=f32) for r in rs], axis=axis)
        return a

    y_prompt = np.stack([np.asarray(r["y_p"], f32) for r in rs], 0)
    y_sample = cat("y_s", 0).reshape(128, 1, D)
    conv_a_p = np.stack([np.asarray(r["o_ca_p"], f32) for r in rs], 1)
    conv_a_s = cat("o_ca_s", 1)
    chunk_v_p = np.stack([np.asarray(r["o_cv_p"], f32) for r in rs], 1)
    chunk_v_s = cat("o_cv_s", 1).reshape(L, 128, 1, DA)
    re_p = np.stack([np.asarray(r["o_re_p"], f32).reshape(L, 32, 64) for r in rs], 1)
    im_p = np.stack([np.asarray(r["o_im_p"], f32).reshape(L, 32, 64) for r in rs], 1)
    re_s = cat("o_re_s", 1).reshape(L, 128, 32, 64)
    im_s = cat("o_im_s", 1).reshape(L, 128, 32, 64)
    cf_p = np.stack([np.asarray(r["o_cf_p"], f32) for r in rs], 1)
    cf_s = cat("o_cf_s", 1)
    return (y_prompt, y_sample, conv_a_p, conv_a_s, chunk_v_p, chunk_v_s, re_p, im_p, re_s, im_s, cf_p, cf_s)
```
